# Optimizing a Trainium2 kernel written in Bass

```python
import jax, jax.numpy as jnp
from jax import lax
import numpy as np

D_MODEL = 1024
BATCH = 2
SEQ = 8192
DEPTH = 2

GRID_W = 64
CTX_LEN = 256
HEAD_DIM = 64
CONV_W = 256
CONV_GROUPS = 4
CONV_K = 3
RET_HEADS = 6
RET_W = RET_HEADS * HEAD_DIM
NA_HEADS = 6
NA_W = NA_HEADS * HEAD_DIM
MIX_W = CONV_W + RET_W + NA_W
RET_CHUNK = 128
WIN_H = 8
WIN_W = 16
ROPE_BASE = 10000.0
LN_EPS = 1e-5
DEEPNORM_ALPHA = (2 * DEPTH) ** 0.25
DEEPNORM_BETA = (8 * DEPTH) ** -0.25

PROJ_SPLITS = (RET_W, RET_W, NA_W, NA_W,
               RET_W, RET_W, NA_W, NA_W,
               CONV_W, CONV_W, CONV_W, CONV_W)
PROJ_W = sum(PROJ_SPLITS)
KV_W = 2 * RET_W + 2 * NA_W

kernel_name = 'hybrid_conv_retention_natten_dit'

F32 = jnp.float32


def split_cols(p, sizes):
    idx = np.cumsum(sizes)[:-1].tolist()
    return jnp.split(p, idx, axis=-1)


def to_heads(t, n_heads):
    b, l, _ = t.shape
    return t.reshape(b, l, n_heads, HEAD_DIM).transpose(0, 2, 1, 3)


def from_heads(t):
    b, h, l, d = t.shape
    return t.transpose(0, 2, 1, 3).reshape(b, l, h * d)


def layer_norm(x, g, b):
    xf = x.astype(F32)
    mu = jnp.mean(xf, axis=-1, keepdims=True)
    var = jnp.mean(jnp.square(xf - mu), axis=-1, keepdims=True)
    return ((xf - mu) * lax.rsqrt(var + LN_EPS) * g + b).astype(x.dtype)


def head_norm(o):
    of = o.astype(F32)
    mu = jnp.mean(of, axis=-1, keepdims=True)
    var = jnp.mean(jnp.square(of - mu), axis=-1, keepdims=True)
    return (of - mu) * lax.rsqrt(var + LN_EPS)


def axial_rope_angles(n):
    t = jnp.arange(n)
    row = (t // GRID_W).astype(F32)
    col = (t % GRID_W).astype(F32)
    nf = HEAD_DIM // 4
    inv = ROPE_BASE ** (-jnp.arange(nf, dtype=F32) / nf)
    return row[:, None] * inv, col[:, None] * inv


def rotate(xp, ang):
    a, b = jnp.split(xp, 2, axis=-1)
    cos, sin = jnp.cos(ang), jnp.sin(ang)
    return jnp.concatenate([a * cos - b * sin, a * sin + b * cos], axis=-1)


def apply_axial_rope(x, ang_r, ang_c):
    xr, xc = jnp.split(x, 2, axis=-1)
    return jnp.concatenate([rotate(xr, ang_r), rotate(xc, ang_c)], axis=-1).astype(x.dtype)


def short_conv_branch(h, bg, cg, z, w, b):
    u = cg * h
    up = jnp.pad(u, ((0, 0), (1, 1), (0, 0)))
    conv = up[:, :-2] * w[0] + up[:, 1:-1] * w[1] + up[:, 2:] * w[2] + b
    return bg * conv * jax.nn.silu(z)


def ret_final_state(k, v, lg):
    l = k.shape[2]
    w = jnp.exp((l - 1 - jnp.arange(l, dtype=F32))[None, :] * lg[:, None])
    return jnp.einsum('bhld,hl,bhle->bhde', k, w, v)


def chunk_retention(q, k, v, lg, s0, inclusive):
    b, h, l, dk = q.shape
    dv = v.shape[-1]
    c = min(RET_CHUNK, l)
    nc = l // c
    qc = q.reshape(b, h, nc, c, dk)
    kc = k.reshape(b, h, nc, c, dk)
    vc = v.reshape(b, h, nc, c, dv)
    i = jnp.arange(c, dtype=F32)
    diff = i[:, None] - i[None, :]
    mask = diff >= 0 if inclusive else diff > 0
    dmat = jnp.where(mask[None], jnp.exp(jnp.where(mask, diff, 0.0)[None] * lg[:, None, None]), 0.0)
    scores = jnp.einsum('bhnid,bhnjd->bhnij', qc, kc) * dmat[None, :, None]
    o_inner = jnp.einsum('bhnij,bhnje->bhnie', scores, vc)
    wk = jnp.exp((c - 1 - i)[None, :] * lg[:, None])
    u = jnp.einsum('bhnjd,hj,bhnje->bhnde', kc, wk, vc)
    g_chunk = jnp.exp(c * lg)[:, None, None]

    def step(s, u_n):
        return g_chunk * s + u_n, s

    _, s_prev = lax.scan(step, s0, jnp.moveaxis(u, 2, 0))
    s_prev = jnp.moveaxis(s_prev, 0, 2)
    wq = jnp.exp((i + 1)[None, :] * lg[:, None])
    o_cross = jnp.einsum('bhnid,hi,bhnde->bhnie', qc, wq, s_prev)
    return (o_inner + o_cross).reshape(b, h, l, dv)


def bidir_retention(q, k, v, lg_f, lg_b, s_f, s_b):
    fwd = chunk_retention(q, k, v, lg_f, s_f, True)
    bwd = chunk_retention(jnp.flip(q, 2), jnp.flip(k, 2), jnp.flip(v, 2), lg_b, s_b, False)
    return fwd + jnp.flip(bwd, 2)


def retention_branch(q, k, v, g, kc, vc, qc, gc, decay_logit, need_ctx):
    n = q.shape[1]
    kscale = HEAD_DIM ** -0.5
    ang_r, ang_c = axial_rope_angles(n)
    qh = apply_axial_rope(to_heads(q, RET_HEADS), ang_r, ang_c)
    kh = apply_axial_rope(to_heads(k, RET_HEADS), ang_r, ang_c) * kscale
    vh = to_heads(v, RET_HEADS)
    kch = to_heads(kc, RET_HEADS) * kscale
    vch = to_heads(vc, RET_HEADS)
    lg = jax.nn.log_sigmoid(decay_logit.astype(F32))
    lg_f, lg_b = lg[0], lg[1]
    s_f = ret_final_state(kch, vch, lg_f)
    s_b = ret_final_state(jnp.flip(kch, 2), jnp.flip(vch, 2), lg_b)
    o = bidir_retention(qh, kh, vh, lg_f, lg_b, s_f, s_b)
    y = from_heads(head_norm(o)) * jax.nn.silu(g)
    if not need_ctx:
        return y, None
    qch = to_heads(qc, RET_HEADS)
    zero = jnp.zeros_like(s_f)
    oc = bidir_retention(qch, kch, vch, lg_f, lg_b, zero, zero)
    yc = from_heads(head_norm(oc)) * jax.nn.silu(gc)
    return y, yc


def neighbourhood_branch(q, k, v, g, kc, vc, qc, gc, rpb, need_ctx):
    b, n, _ = q.shape
    rows = n // GRID_W
    win_h = min(WIN_H, rows)
    n_loc = win_h * WIN_W
    scale = HEAD_DIM ** -0.5

    def grid(t):
        return to_heads(t, NA_HEADS).reshape(b, NA_HEADS, rows, GRID_W, HEAD_DIM)

    qg, kg, vg = grid(q), grid(k), grid(v)
    kch, vch = to_heads(kc, NA_HEADS), to_heads(vc, NA_HEADS)
    r = np.arange(rows)
    row_start = np.clip(r - win_h // 2, 0, rows - win_h)
    d_row = row_start[:, None] + np.arange(win_h)[None, :] - r[:, None]
    cpos = np.arange(GRID_W)
    col_idx = np.clip(cpos - WIN_W // 2, 0, GRID_W - WIN_W)[:, None] + np.arange(WIN_W)[None, :]
    d_col = col_idx - cpos[:, None]

    def row_block(args):
        q_r, rs, dr = args
        k_band = lax.dynamic_slice_in_dim(kg, rs, win_h, axis=2)
        v_band = lax.dynamic_slice_in_dim(vg, rs, win_h, axis=2)
        k_win = k_band[:, :, :, col_idx]
        v_win = v_band[:, :, :, col_idx]
        s_loc = jnp.einsum('bhcd,bhrckd->bhcrk', q_r, k_win).astype(F32) * scale
        bias = rpb[:, dr[:, None, None] + (WIN_H - 1), d_col[None] + (WIN_W - 1)]
        s_loc = s_loc + bias.transpose(0, 2, 1, 3)[None].astype(F32)
        s_ctx = jnp.einsum('bhcd,bhjd->bhcj', q_r, kch).astype(F32) * scale
        p = jax.nn.softmax(jnp.concatenate([s_loc.reshape(b, NA_HEADS, GRID_W, n_loc), s_ctx], axis=-1), axis=-1)
        p_loc = p[..., :n_loc].reshape(b, NA_HEADS, GRID_W, win_h, WIN_W).astype(v.dtype)
        p_ctx = p[..., n_loc:].astype(v.dtype)
        return (jnp.einsum('bhcrk,bhrckd->bhcd', p_loc, v_win)
                + jnp.einsum('bhcj,bhjd->bhcd', p_ctx, vch))

    o = lax.map(row_block, (jnp.moveaxis(qg, 2, 0),
                            jnp.asarray(row_start, jnp.int32),
                            jnp.asarray(d_row, jnp.int32)))
    y = o.transpose(1, 0, 3, 2, 4).reshape(b, n, NA_W) * jax.nn.silu(g)
    if not need_ctx:
        return y, None
    qch = to_heads(qc, NA_HEADS)
    s = jnp.einsum('bhid,bhjd->bhij', qch, kch).astype(F32) * scale
    oc = jnp.einsum('bhij,bhjd->bhid', jax.nn.softmax(s, axis=-1).astype(vc.dtype), vch)
    yc = from_heads(oc) * jax.nn.silu(gc)
    return y, yc


def hybrid_layer(x, xc, c_act, cc_act, w_mod, b_mod, w_in, conv_w, conv_b,
                 ret_decay, na_rpb, w_out, ln_g, ln_b, need_ctx):
    shift, scale, gate = jnp.split(c_act @ w_mod + b_mod, 3, axis=-1)
    h = x * (1 + scale[:, None]) + shift[:, None]
    rk, rv, nk, nv, rq, rg, nq, ng, ch, cb, ccg, cz = split_cols(h @ w_in, PROJ_SPLITS)
    n_mod = 3 if need_ctx else 2
    mod_c = jnp.split(cc_act @ w_mod[:, :n_mod * D_MODEL] + b_mod[:n_mod * D_MODEL], n_mod)
    hc = xc * (1 + mod_c[1]) + mod_c[0]
    if need_ctx:
        pc = split_cols(hc @ w_in, PROJ_SPLITS)
    else:
        pc = split_cols(hc @ w_in[:, :KV_W], PROJ_SPLITS[:4]) + [None] * 8
    rkc, rvc, nkc, nvc, rqc, rgc, nqc, ngc, chc, cbc, ccc, czc = pc

    y_conv = short_conv_branch(ch, cb, ccg, cz, conv_w, conv_b)
    y_ret, yc_ret = retention_branch(rq, rk, rv, rg, rkc, rvc, rqc, rgc, ret_decay, need_ctx)
    y_na, yc_na = neighbourhood_branch(nq, nk, nv, ng, nkc, nvc, nqc, ngc, na_rpb, need_ctx)
    y = jnp.concatenate([y_conv, y_ret, y_na], axis=-1) @ w_out
    x_new = layer_norm(DEEPNORM_ALPHA * x + gate[:, None] * y, ln_g, ln_b)
    if not need_ctx:
        return x_new, None
    yc_conv = short_conv_branch(chc, cbc, ccc, czc, conv_w, conv_b)
    yc = jnp.concatenate([yc_conv, yc_ret, yc_na], axis=-1) @ w_out
    xc_new = layer_norm(DEEPNORM_ALPHA * xc + mod_c[2] * yc, ln_g, ln_b)
    return x_new, xc_new


def setup_inputs(seed: int = 0) -> dict:
    key = jax.random.key(seed)
    ks = jax.random.split(key, 14)
    nrm = jax.random.normal
    x = nrm(ks[0], (BATCH, SEQ, D_MODEL), F32)
    c = nrm(ks[1], (BATCH, D_MODEL), F32)
    ctx = nrm(ks[2], (BATCH, CTX_LEN, D_MODEL), F32)
    c_ctx = nrm(ks[3], (D_MODEL,), F32)
    w_mod = nrm(ks[4], (DEPTH, D_MODEL, 3 * D_MODEL), F32) * (0.5 * D_MODEL ** -0.5)
    b_mod = 0.01 * nrm(ks[5], (DEPTH, 3 * D_MODEL), F32)
    w_in = nrm(ks[6], (DEPTH, D_MODEL, PROJ_W), F32) * (D_MODEL ** -0.5)
    conv_w = nrm(ks[7], (DEPTH, CONV_K, CONV_W), F32) * (CONV_K ** -0.5)
    conv_b = 0.01 * nrm(ks[8], (DEPTH, CONV_W), F32)
    gamma = 1.0 - 2.0 ** (-5.0 - np.arange(RET_HEADS))
    decay_init = np.log(gamma / (1.0 - gamma)).astype(np.float32)
    ret_decay = jnp.asarray(decay_init)[None, None, :] + 0.1 * nrm(ks[9], (DEPTH, 2, RET_HEADS), F32)
    na_rpb = 0.05 * nrm(ks[10], (DEPTH, NA_HEADS, 2 * WIN_H - 1, 2 * WIN_W - 1), F32)
    w_out = nrm(ks[11], (DEPTH, MIX_W, D_MODEL), F32) * (DEEPNORM_BETA * MIX_W ** -0.5)
    ln_g = 1.0 + 0.01 * nrm(ks[12], (DEPTH, D_MODEL), F32)
    ln_b = 0.01 * nrm(ks[13], (DEPTH, D_MODEL), F32)
    return {'x': x, 'c': c, 'ctx': ctx, 'c_ctx': c_ctx, 'w_mod': w_mod, 'b_mod': b_mod,
            'w_in': w_in, 'conv_w': conv_w, 'conv_b': conv_b, 'ret_decay': ret_decay,
            'na_rpb': na_rpb, 'w_out': w_out, 'ln_g': ln_g, 'ln_b': ln_b}


def reference(x, c, ctx, c_ctx, w_mod, b_mod, w_in, conv_w, conv_b, ret_decay, na_rpb,
              w_out, ln_g, ln_b):
    c_act = jax.nn.silu(c)
    cc_act = jax.nn.silu(c_ctx)
    xc = ctx
    for l in range(DEPTH):
        x, xc = hybrid_layer(x, xc, c_act, cc_act, w_mod[l], b_mod[l], w_in[l], conv_w[l], conv_b[l],
                             ret_decay[l], na_rpb[l], w_out[l], ln_g[l], ln_b[l],
                             need_ctx=(l < DEPTH - 1))
    return x
```

```python
import numpy as np
import ml_dtypes
from contextlib import ExitStack
import concourse.bass as bass
import concourse.mybir as mybir
from concourse.bass_utils import run_bass_kernel_spmd

F32 = mybir.dt.float32
BF16 = mybir.dt.bfloat16
I32 = mybir.dt.int32
ALU = mybir.AluOpType
AF = mybir.ActivationFunctionType
AX = mybir.AxisListType

DEBUG = False
NT = 16
NCX = 2
D = 1024
ALPHA = float((2 * 2) ** 0.25)
LN_EPS = 1e-5
NEG = -30000.0
PKB = 3104 + 768
OFF_ST = 3104
OFF_NKT_TOP, OFF_NKT_BOT, OFF_NVA_TOP, OFF_NVA_BOT, OFF_U_FIRST, OFF_U_LAST = 0, 768, 1536, 2316, 3096, 3098
C_RK, C_RV, C_NK, C_NV, C_RQ, C_RG, C_NQ, C_NG, C_CH, C_CB, C_CCG, C_CZ = (
    0, 384, 768, 1152, 1536, 1920, 2304, 2688, 3072, 3328, 3584, 3840)


class Reg:
    __slots__ = ("name", "w", "rs", "excl")

    def __init__(self, name="", excl=False):
        self.name = name
        self.w = None
        self.rs = {}
        self.excl = excl


class Sched:
    ENG = ("pe", "act", "dve", "pool", "sp")
    LIMIT = 2000
    NDMA = 6

    def __init__(self, nc, stack):
        self.nc = nc
        self.stack = stack
        self.prog = {e: [] for e in self.ENG}
        self.state = {}
        self.waited = {e: {} for e in self.ENG}
        self.nsem = 0
        self.nops = 0
        self.dcount = {}

    def _tick(self, key, inc):
        st = self.state.get(key)
        if st is None or st[1] + inc > self.LIMIT:
            s = self.stack.enter_context(self.nc.semaphore(f"s{self.nsem}_{key}"))
            self.nsem += 1
            st = [s, 0]
            self.state[key] = st
            self.allsems.append(st)
        st[1] += inc
        return st[0], st[1]

    allsems = None

    def op(self, eng, fn, reads=(), writes=(), dma=None, inc=None):
        if self.allsems is None:
            self.allsems = []
        deps = {}

        def add(tok):
            if tok is None:
                return
            e, s, v = tok
            if e == "pe" and eng == "pe" and dma is None:
                return
            k = id(s)
            if k not in deps or deps[k][1] < v:
                deps[k] = (s, v)

        for r in reads:
            add(r.w)
            if r.excl:
                for t in r.rs.values():
                    if t[0] != eng:
                        add(t)
        for w in writes:
            add(w.w)
            for t in w.rs.values():
                add(t)
        wd = self.waited[eng]
        waits = []
        for k, (s, v) in deps.items():
            if wd.get(k, 0) < v:
                wd[k] = v
                waits.append((s, v))
        isdma = dma is not None
        if inc is None:
            inc = 16 if isdma else 1
        if isdma:
            i = self.dcount.get(dma, 0)
            self.dcount[dma] = i + 1
            slot = i % self.NDMA
            sem, val = self._tick(f"dma_{dma}_{slot}", inc)
            if val > inc and wd.get(id(sem), 0) < val - inc:
                wd[id(sem)] = val - inc
                waits.append((sem, val - inc))
        else:
            sem, val = self._tick(eng, inc)
        tok = (None if isdma else eng, sem, val)

        def emit(h, waits=waits, fn=fn, sem=sem, inc=inc):
            for s, v in waits:
                h.wait_ge(s, v)
            fn(h).then_inc(sem, inc)

        self.prog[eng].append(emit)
        self.nops += 1
        for r in reads:
            k = id(sem)
            old = r.rs.get(k)
            if old is None or old[2] < val:
                r.rs[k] = tok
        for w in writes:
            w.w = tok
            w.rs = {}
        return tok

    def finish(self):
        fin = [(st[0], st[1]) for st in self.allsems]

        def emit(h, fin=fin):
            for s, v in fin:
                h.wait_ge(s, v)

        self.prog["sp"].append(emit)

    def emit_all(self):
        nc = self.nc
        prog = self.prog
        with nc.Block() as block:
            @block.tensor
            def _(h):
                for f in prog["pe"]:
                    f(h)

            @block.scalar
            def _(h):
                for f in prog["act"]:
                    f(h)

            @block.vector
            def _(h):
                for f in prog["dve"]:
                    f(h)

            @block.gpsimd
            def _(h):
                for f in prog["pool"]:
                    f(h)

            @block.sync
            def _(h):
                for f in prog["sp"]:
                    f(h)


def build_program():
    nc = bass.Bass("TRN2", target_bir_lowering=False)

    def din(name, shape, dt=F32):
        return nc.dram_tensor(name, list(shape), dt, kind="ExternalInput").ap()

    x_in = din("x", [NT * 128, D])
    ctx_in = din("ctx", [NCX * 128, D])
    cvec = din("cvec", [128, 8, 2])
    w_mod = din("w_mod", [2, D, 3 * D])
    bmodT = din("bmodT", [128, 2, 24])
    w_in = din("w_in", [2, D, 4096])
    w_out = din("w_out", [2, D, D])
    convp = din("convp", [128, 2, 2, 4])
    dec = din("dec", [128, 36])
    nabias = din("nabias", [2, NT, 128, 6 * 6 * 128], BF16)
    lnp = din("lnp", [2, 2, 128, D])
    rope = din("rope", [NT, 128, 128])
    cst = din("cst", [128, 5 * 128 + 2])
    rankc = din("rankc", [128, 20])
    edge = din("edge", [128, 2])
    selin = din("sel", [128, 8])
    idx = din("idx", [128, 2], I32)
    out = nc.dram_tensor("out", [NT * 128, D], F32, kind="ExternalOutput").ap()
    okind = "ExternalOutput" if DEBUG else "Internal"
    x1 = nc.dram_tensor("x1", [NT * 128, D], F32, kind=okind).ap()
    xc1 = nc.dram_tensor("xc1", [NCX * 128, D], F32, kind=okind).ap()
    packF = [nc.dram_tensor(f"packF{l}", [128, 384], F32).ap() for l in range(2)]
    gathF = [nc.dram_tensor(f"gathF{l}", [512, 384], F32).ap() for l in range(2)]
    packB = [nc.dram_tensor(f"packB{l}", [128, PKB], BF16).ap() for l in range(2)]
    gathB = [nc.dram_tensor(f"gathB{l}", [512, PKB], BF16).ap() for l in range(2)]
    RG = [[0, 1, 2, 3], [4, 5, 6, 7]]

    with ExitStack() as st:
        S = Sched(nc, st)

        def sb(name, shape, dt=F32):
            return st.enter_context(nc.sbuf_tensor(name, list(shape), dt))

        def R(n=""):
            return Reg(n)

        def Rs(n, k):
            return [Reg(f"{n}{i}") for i in range(k)]

        import os
        KSTOP = float(os.environ.get("KSTOP", "999"))

        class _Stop(Exception):
            pass

        def ck(n):
            if n >= KSTOP:
                raise _Stop()

        def MM(o, lhsT, rhs, start, stop, rd, wr, tp=None):
            if tp is None:
                S.op("pe", lambda h: h.matmul(o, lhsT, rhs, start=start, stop=stop), rd, wr)
            else:
                S.op("pe", lambda h: h.matmul(o, lhsT, rhs, start=start, stop=stop, tile_position=tp), rd, wr)

        def TR(o, i, ident, rd, wr):
            S.op("pe", lambda h: h.transpose(o, i, ident), rd, wr)

        def ACT(o, i, func, rd, wr, bias=None, scale=None, accum=None):
            kw = {}
            if bias is not None:
                kw["bias"] = bias
            if scale is not None:
                kw["scale"] = scale
            if accum is not None:
                kw["accum_out"] = accum
            S.op("act", lambda h: h.activation(out=o, in_=i, func=func, **kw), rd, wr)

        def TT(e, o, a, b, op, rd, wr):
            S.op(e, lambda h: h.tensor_tensor(out=o, in0=a, in1=b, op=op), rd, wr)

        def TS(e, o, a, s1, s2, op0, op1, rd, wr):
            if s2 is None:
                S.op(e, lambda h: h.tensor_scalar(out=o, in0=a, scalar1=s1, scalar2=None, op0=op0), rd, wr)
            else:
                S.op(e, lambda h: h.tensor_scalar(out=o, in0=a, scalar1=s1, scalar2=s2, op0=op0, op1=op1), rd, wr)

        def STT(o, a, s, b, op0, op1, rd, wr):
            S.op("dve", lambda h: h.scalar_tensor_tensor(out=o, in0=a, scalar=s, in1=b, op0=op0, op1=op1), rd, wr)

        def CP(e, o, i, rd, wr):
            if e == "act":
                S.op("act", lambda h: h.copy(out=o, in_=i), rd, wr)
            else:
                S.op(e, lambda h: h.tensor_copy(out=o, in_=i), rd, wr)

        def MSET(e, ap, v, wr):
            S.op(e, lambda h: h.memset(ap, v), [], wr)

        def DMA(e, o, i, rd, wr, stream):
            S.op(e, lambda h: h.dma_start(out=o, in_=i), rd, wr, dma=stream)

        PB = [st.enter_context(nc.psum_tensor(f"pb{i}", [128, 512], F32)) for i in range(7)]
        PB.append(st.enter_context(nc.psum_tensor("pb7", [128, 1024], BF16)))
        PR = [Reg(f"pb{i}", excl=True) for i in range(8)]

        cs = sb("cs", [128, 5 * 128 + 2])
        r_cs = R("cs")
        DMA("sp", cs[:], cst[:, :], [], [r_cs], "ld")
        identf = cs[:, 0:128]
        DFt = cs[:, 128:256]
        DBt = cs[:, 256:384]
        ip1 = cs[:, 384:512]
        rev = cs[:, 512:640]
        jrev = cs[:, 640:641]
        jpos = cs[:, 641:642]
        identb = sb("identb", [128, 128], BF16)
        r_idb = R("idb")
        CP("dve", identb[:], identf, [r_cs], [r_idb])
        onesf = sb("onesf", [128, 128])
        r_ones = R("ones")
        MSET("pool", onesf[:], 1.0, [r_ones])
        epsc = sb("epsc", [128, 1])
        r_eps = R("eps")
        MSET("pool", epsc[:], LN_EPS, [r_eps])
        rk_t = sb("rk_t", [128, 20])
        eg_t = sb("eg_t", [128, 2])
        ix_t = sb("ix_t", [128, 2], I32)
        cp_t = sb("cp_t", [128, 2, 2, 4])
        r_misc = R("misc")
        DMA("sp", rk_t[:], rankc[:, :], [], [r_misc], "ld")
        DMA("sp", eg_t[:], edge[:, :], [], [r_misc], "ld")
        sel_t = sb("sel_t", [128, 8])
        DMA("sp", sel_t[:], selin[:, :], [], [r_misc], "ld")
        DMA("sp", ix_t[:], idx[:, :], [], [r_misc], "ld")
        DMA("sp", cp_t[:], convp[:, :, :, :], [], [r_misc], "ld")
        dc = sb("dc", [128, 36])
        lg = sb("lg", [128, 36])
        r_lg = R("lg")
        DMA("sp", dc[:], dec[:, :], [], [r_lg], "ld")
        ACT(lg[:], dc[:], AF.Exp, [r_lg], [r_lg], scale=-1.0)
        TS("dve", lg[:], lg[:], 1.0, None, ALU.add, None, [r_lg], [r_lg])
        ACT(lg[:], lg[:], AF.Ln, [r_lg], [r_lg])
        TS("dve", lg[:], lg[:], -1.0, None, ALU.mult, None, [r_lg], [r_lg])

        def lg_bc(l, d, h):
            c = l * 12 + d * 6 + h
            return lg[:, c:c + 1]

        def lg_pp(l, d, pr):
            c = 24 + l * 6 + d * 3 + pr
            return lg[:, c:c + 1]

        T128 = sb("T128", [128, 2, 16])
        r_t128 = R("t128")
        TS("dve", T128[:, 0, :], ip1[:, 0:16], -1.0, 128.0, ALU.add, ALU.mult, [r_cs], [r_t128])
        TS("dve", T128[:, 1, :], ip1[:, 0:16], -16.0, -128.0, ALU.add, ALU.mult, [r_cs, r_t128], [r_t128])

        ca = sb("ca", [128, 8, 2])
        r_ca = R("ca")
        DMA("sp", ca[:], cvec[:, :, :], [], [r_ca], "ld")
        ACT(ca[:], ca[:], AF.Silu, [r_ca], [r_ca])
        cab = sb("cab", [128, 8, 2], BF16)
        CP("dve", cab[:], ca[:], [r_ca], [r_ca])

        Wb = sb("Wb", [128, 8, 2560], BF16)
        WRK = [[Reg(f"W{g}_{kp}") for kp in range(4)] for g in range(8)]

        def WOK(k):
            return [WRK[g][k // 2] for g in range(4, 8)]

        def mkseq(name, nt, nslot, halo):
            q = dict(name=name, nt=nt, nslot=nslot, halo=halo)
            q["kT"] = sb(name + "kT", [128, 3, nt * 128], BF16)
            q["vret"] = sb(name + "vret", [128, nt, 384], BF16)
            q["nkT"] = sb(name + "nkT", [128, 3, nslot * 128], BF16)
            q["nva"] = sb(name + "nva", [128, nslot, 6, 65], BF16)
            q["sfst"] = sb(name + "sfst", [128, nt, 3, 64], BF16)
            q["tbst"] = sb(name + "tbst", [128, nt, 3, 64], BF16)
            q["uT"] = sb(name + "uT", [128, 2, nt * 128 + 2], BF16)
            q["gcT"] = sb(name + "gcT", [128, 2, nt * 128], BF16)
            q["runf"] = sb(name + "runf", [128, 3, 64])
            q["runb"] = sb(name + "runb", [128, 3, 64])
            for k in ("kT", "vret", "sfst", "tbst", "gcT"):
                q["r_" + k] = Rs(name + k, nt)
            q["r_nk"] = Rs(name + "nk", nslot)
            q["r_nv"] = Rs(name + "nv", nslot)
            q["r_u"] = Rs(name + "u", nt + 2)
            q["r_runf"] = R(name + "runf")
            q["r_runb"] = R(name + "runb")
            return q

        MQ = mkseq("m", NT, NT + 4, 2)
        CQ = mkseq("c", NCX, NCX, 0)
        MSET("pool", MQ["nva"][:], 1.0, MQ["r_nv"])
        MSET("pool", CQ["nva"][:], 1.0, CQ["r_nv"])
        MSET("pool", CQ["uT"][:], 0.0, CQ["r_u"])
        MSET("pool", MQ["uT"][:], 0.0, MQ["r_u"])

        biasb = sb("biasb", [128, 3, 6, 2, 128], BF16)
        r_bias = R("bias")
        gbc = sb("gbc", [128, D])
        bbc = sb("bbc", [128, D])
        r_lnp = R("lnp")
        gate_bc = sb("gate_bc", [128, D])
        r_gate = R("gate")
        modTs = [sb(f"modT{i}", [128, 24, 2]) for i in range(2)]
        r_mods = Rs("mod", 2)
        cur = {"l": 0}
        Dcomb = sb("Dcomb", [128, 6, 128])
        r_dc = R("dcomb")
        WQ = sb("WQ", [128, 2, 3, 128])
        r_wq = R("wq")
        WK = sb("WK", [128, 2, 6])
        r_wk = R("wk")
        G128 = sb("G128", [128, 2, 3])
        r_g128 = R("g128")
        gpow = sb("gpow", [128, 3, 3, 16])
        r_gpow = R("gpow")
        coef = sb("coef", [128, 2, 3, 5])
        r_coef = R("coef")
        Sin = sb("Sin", [128, 2, 3, 64], BF16)
        r_sin = R("sin")

        xt = [sb(f"xt{i}", [128, D]) for i in range(2)]
        r_xt = Rs("xt", 2)
        hT = [sb(f"hT{i}", [128, 8, 128], BF16) for i in range(2)]
        r_hT = Rs("hT", 2)
        ropet = [sb(f"ropet{i}", [128, 128]) for i in range(2)]
        r_rope = Rs("rope", 2)
        tA = sb("tA", [128, 384])
        tB = sb("tB", [128, 384])
        r_tA, r_tB = R("tA"), R("tB")
        krot = sb("krot", [128, 384], BF16)
        r_krot = R("krot")
        cvt = sb("cvt", [128, 2, 128])
        r_cvt = R("cvt")
        cvu = sb("cvu", [128, 2, 128])
        r_cvu = R("cvu")
        qT = sb("qT", [128, 3, 128], BF16)
        r_qT = R("qT")
        qsc = sb("qsc", [128, 4, 3, 128], BF16)
        r_qsc = R("qsc")
        kw = qsc[:, 0:2, :, :].rearrange("p d a b -> p d (a b)")
        r_kw = r_qsc
        nqTm = sb("nqTm", [128, 3, 2, 128], BF16)
        r_nqT = R("nqT")
        MSET("pool", nqTm[:], 0.0, [r_nqT])
        srg = sb("srg", [128, 384], BF16)
        sng = sb("sng", [128, 384], BF16)
        r_srg, r_sng = R("srg"), R("sng")
        AT = sb("AT", [128, 6, 128], BF16)
        r_AT = R("AT")
        st6 = sb("st6", [128, 8, 6])
        r_st6 = R("st6")
        r_st6a = R("st6a")
        r_st6c = R("st6c")
        r_st6b = R("st6b")
        yret = sb("yret", [128, 384], BF16)
        yna = sb("yna", [128, 384], BF16)
        r_yret, r_yna = R("yret"), R("yna")
        Eb = [sb(f"Eb{i}", [128, 8, 128], BF16) for i in range(2)]
        r_Eb = Rs("Eb", 2)
        yT = sb("yT", [128, 8, 128], BF16)
        r_yTc, r_yTr, r_yTn = R("yTc"), R("yTr"), R("yTn")
        zb = sb("zb", [128, D])
        r_zb = R("zb")
        st1 = sb("st1", [128, 8])
        r_st1 = R("st1")
        r_st1a = R("st1a")
        tmp64 = tA[:, 0:192].rearrange("p (a b) -> p a b", b=64)
        r_tmp64 = r_tA
        sacc = tB[:].rearrange("p (d a b) -> p d a b", d=2, a=3)
        r_sacc = r_tB
        tmp2 = tB[:, 0:192].rearrange("p (a b) -> p a b", b=64)
        r_tmp2 = r_tB
        totb = sb("totb", [128, 3, 64])
        r_totb = R("totb")
        ub = sb("ub", [128, 8], BF16)
        ubg = sb("ubg", [128, 4], BF16)
        r_ub, r_ubg = R("ub"), R("ubg")

        x1_r = Rs("x1d", NT)
        xc1_r = Rs("xc1d", NCX)

        def mod_setup(l):
            modT, r_mod = modTs[l], r_mods[l]
            wsrc = w_mod[l].rearrange("(k p) c -> p k c", p=128)
            stg = [(xt[0], r_xt[0]), (xt[1], r_xt[1]), (zb, r_zb), (gate_bc, r_gate)]
            bst = [(hT[0], r_hT[0]), (hT[1], r_hT[1]), (Eb[0], r_Eb[0]), (Eb[1], r_Eb[1])]
            ceng = ("pool", "dve", "act", "pool")
            for fc in range(24):
                bt, rb = stg[fc % 4]
                bb, rbb = bst[fc % 4]
                b = bt[:].rearrange("p (k c) -> p k c", c=128)
                DMA("sp", b, wsrc[:, :, fc * 128:(fc + 1) * 128], [], [rb], "wm")
                CP(ceng[fc % 4], bb[:], b, [rb], [rbb])
                for k in range(8):
                    MM(PB[6][:, fc * 2:fc * 2 + 2], bb[:, k, :], cab[:, k, :], k == 0, k == 7, [rbb, r_ca], [PR[6]])
            bm = sb(f"bm{l}", [128, 24])
            r_bm = R("bm")
            DMA("sp", bm[:], bmodT[:, l, :], [], [r_bm], "ld")
            TT("dve", modT[:], PB[6][:, 0:48].rearrange("p (a b) -> p a b", b=2),
               bm[:].unsqueeze(2).broadcast_to([128, 24, 2]), ALU.add, [PR[6], r_bm], [r_mod])
            TS("dve", modT[:, 8:16, :], modT[:, 8:16, :], 1.0, None, ALU.add, None, [r_mod], [r_mod])

        def layer_setup(l):
            cur["l"] = l
            DMA("sp", gbc[:], lnp[l, 0, :, :], [], [r_lnp], "ld")
            DMA("sp", bbc[:], lnp[l, 1, :, :], [], [r_lnp], "ld")
            for h in range(6):
                ACT(tA[:, 0:128], DFt, AF.Exp, [r_cs, r_lg], [r_tA], scale=lg_bc(l, 0, h))
                ACT(tB[:, 0:128], DBt, AF.Exp, [r_cs, r_lg], [r_tB], scale=lg_bc(l, 1, h))
                TT("dve", Dcomb[:, (h % 2) * 3 + h // 2, :], tA[:, 0:128], tB[:, 0:128], ALU.add, [r_tA, r_tB], [r_dc])
            TS("dve", Dcomb[:], Dcomb[:], 0.125, None, ALU.mult, None, [r_dc], [r_dc])
            for pr in range(3):
                ACT(WQ[:, 0, pr, :], ip1, AF.Exp, [r_cs, r_lg], [r_wq], scale=lg_pp(l, 0, pr))
                ACT(WQ[:, 1, pr, :], rev, AF.Exp, [r_cs, r_lg], [r_wq], scale=lg_pp(l, 1, pr))
                ACT(gpow[:, 0, pr, :], T128[:, 0, :], AF.Exp, [r_t128, r_lg], [r_gpow], scale=lg_pp(l, 0, pr))
                ACT(gpow[:, 1, pr, :], T128[:, 1, :], AF.Exp, [r_t128, r_lg], [r_gpow], scale=lg_pp(l, 1, pr))
                ACT(gpow[:, 2, pr, :], T128[:, 0, :], AF.Exp, [r_t128, r_lg], [r_gpow], scale=lg_pp(l, 1, pr))
                for d in range(2):
                    ACT(coef[:, d, pr, :], rk_t[:, d * 5:d * 5 + 5], AF.Exp, [r_misc, r_lg], [r_coef],
                        scale=lg_pp(l, d, pr))
                    TT("dve", coef[:, d, pr, :], coef[:, d, pr, :], rk_t[:, 10 + d * 5:15 + d * 5], ALU.mult,
                       [r_coef, r_misc], [r_coef])
            for d in range(2):
                ACT(G128[:, d, :], lg[:, 24 + l * 6 + d * 3:24 + l * 6 + d * 3 + 3], AF.Exp, [r_lg], [r_g128],
                    scale=128.0)
                TS("dve", WK[:, d, :], lg[:, l * 12 + d * 6:l * 12 + d * 6 + 6], jrev if d == 0 else jpos, None,
                   ALU.mult, None, [r_lg, r_cs], [r_wk])
            ACT(WK[:], WK[:], AF.Exp, [r_wk], [r_wk])
            TS("dve", WK[:], WK[:], 0.125, None, ALU.mult, None, [r_wk], [r_wk])

        def build_gate(m):
            for k in range(8):
                TS("dve", tB[:, 0:128], identf, modTs[cur["l"]][:, 16 + k, m:m + 1], None, ALU.mult, None,
                   [r_cs, r_mods[cur["l"]]], [r_tB])
                MM(PB[5][:, (k % 4) * 128:(k % 4 + 1) * 128], onesf[:], tB[:, 0:128], True, True, [r_ones, r_tB], [PR[5]])
                if k % 4 == 3:
                    CP("act", gate_bc[:, (k // 4) * 512:(k // 4 + 1) * 512], PB[5][:, :], [PR[5]], [r_gate])

        W1_GROUPS = ((C_RK, 384, 0, 0), (C_RV, 384, 384, 1), (C_NV, 384, 768, 2), (C_NK, 384, 1152, 3),
                     (C_CH, 256, 1536, 4), (C_CCG, 256, 1792, 5), (C_CB, 256, 2048, 6), (C_CZ, 256, 2304, 7))

        def load_w1(l, part):
            src = w_in[l].rearrange("(k p) c -> p k c", p=128)
            for (c0, n, o, g) in (W1_GROUPS[0:4] if part == 0 else W1_GROUPS[4:8]):
                DMA("pool", Wb[:, :, o:o + n], src[:, :, c0:c0 + n], [], WRK[g], "w")

        def load_w2(l):
            src = w_in[l].rearrange("(k p) c -> p k c", p=128)
            if l == 0:
                for (c0, n, o, g) in ((C_RQ, 384, 0, 0), (C_NQ, 384, 1152, 3), (C_RG, 384, 384, 1), (C_NG, 384, 768, 2)):
                    DMA("pool", Wb[:, :, o:o + n], src[:, :, c0:c0 + n], [], WRK[g], "w")
                srco0 = w_out[l].rearrange("(k p) c -> p k c", p=128)
                DMA("pool", Wb[:, :, 1536:2560], srco0[:, :, :], [], [r for g in range(4, 8) for r in WRK[g]], "w")
                return
            stg = [(xt[0], r_xt[0]), (xt[1], r_xt[1]), (zb, r_zb), (gate_bc, r_gate)]
            ceng = ("pool", "act", "dve", "pool")
            i = 0
            for (c0, n, o, g) in ((C_RQ, 384, 0, 0), (C_NQ, 384, 1152, 3), (C_RG, 384, 384, 1), (C_NG, 384, 768, 2)):
                for kp in range(4):
                    bt, rb = stg[i % 4]
                    v = bt[:, 0:768].rearrange("p (k c) -> p k c", c=384)
                    DMA("sp", v, src[:, 2 * kp:2 * kp + 2, c0:c0 + n], [], [rb], "w")
                    CP(ceng[i % 4], Wb[:, 2 * kp:2 * kp + 2, o:o + n], v, [rb], [WRK[g][kp]])
                    i += 1
            srco = w_out[l].rearrange("(k p) c -> p k c", p=128)
            for k in range(8):
                bt, rb = stg[i % 4]
                DMA("sp", bt[:], srco[:, k, :], [], [rb], "w")
                CP(ceng[i % 4], Wb[:, k, 1536:2560], bt[:], [rb], WOK(k))
                i += 1

        def LX(src, src_r, t, m, l=None, rope_on=False, bias_on=False):
            s = t % 2
            rd = [src_r[t]] if src_r is not None else []
            DMA("sp", xt[s][:], src[t * 128:(t + 1) * 128, :], rd, [r_xt[s]], "x")
            if rope_on:
                DMA("sp", ropet[s][:], rope[t, :, :], [], [r_rope[s]], "x")
            if bias_on:
                DMA("sp", biasb[:].rearrange("p a b h i -> p (a b h i)"), nabias[l, t, :, :], [], [r_bias], "x")
            for k in range(8):
                TR(PB[k // 4][:, (k % 4) * 128:(k % 4 + 1) * 128], xt[s][:, k * 128:(k + 1) * 128], identf,
                   [r_xt[s], r_cs], [PR[k // 4]])
            for k in range(8):
                ACT(hT[s][:, k, :], PB[k // 4][:, (k % 4) * 128:(k % 4 + 1) * 128], AF.Identity,
                    [PR[k // 4], r_mods[cur["l"]]], [r_hT[s]], bias=modTs[cur["l"]][:, k, m:m + 1],
                    scale=modTs[cur["l"]][:, 8 + k, m:m + 1])

        def do_rope(src_ps, src_r, s_rope, dst, dst_r):
            v = src_ps.rearrange("p (h d) -> p h d", d=64)
            cosb = ropet[s_rope][:, 0:64].unsqueeze(1).broadcast_to([128, 6, 64])
            TT("dve", tA[:].rearrange("p (h d) -> p h d", d=64), v, cosb, ALU.mult, [src_r, r_rope[s_rope]], [r_tA])
            v5 = src_ps.rearrange("p (h r a f) -> p h r a f", r=2, a=2, f=16)
            tB5 = tB[:].rearrange("p (h r a f) -> p h r a f", r=2, a=2, f=16)
            sn = ropet[s_rope][:, 64:128].rearrange("p (r a f) -> p r a f", r=2, a=2)
            for a in range(2):
                TT("dve", tB5[:, :, :, a, :], v5[:, :, :, 1 - a, :],
                   sn[:, :, a, :].unsqueeze(1).broadcast_to([128, 6, 2, 16]), ALU.mult,
                   [src_r, r_rope[s_rope]], [r_tB])
            TT("dve", dst, tA[:], tB[:], ALU.add, [r_tA, r_tB], [dst_r])

        def P1(q, t, src, src_r, m, use_rope, hoist=None):
            nt = q["nt"]
            slot = t + q["halo"]
            s = t % 2
            hs, rhs_ = hT[s], r_hT[s]
            for (bank, wo, g) in ((2, 0, 0), (3, 384, 1), (4, 768, 2)):
                for k in range(8):
                    MM(PB[bank][:, 0:384], hs[:, k, :], Wb[:, k, wo:wo + 384], k == 0, k == 7, [rhs_, WRK[g][k // 2]],
                       [PR[bank]])
            for c in range(3):
                for k in range(8):
                    MM(PB[5][:, c * 128:(c + 1) * 128], Wb[:, k, 1152 + c * 128:1152 + (c + 1) * 128], hs[:, k, :],
                       k == 0, k == 7, [rhs_, WRK[3][k // 2]], [PR[5]])
            def conv_half(bank, groups):
                for gi, (wo, g) in enumerate(groups):
                    for c in range(2):
                        blk = gi * 2 + c
                        for k in range(8):
                            MM(PB[bank][:, blk * 128:(blk + 1) * 128], Wb[:, k, wo + c * 128:wo + (c + 1) * 128],
                               hs[:, k, :], k == 0, k == 7, [rhs_, WRK[g][k // 2]], [PR[bank]])

            conv_half(6, ((1536, 4), (1792, 5)))
            v3 = PB[3][:, 0:384].rearrange("p (h e) -> p h e", e=64)
            S.op("dve", lambda h_: h_.reduce_sum(out=st6[:, 7, :], in_=v3, axis=AX.X), [PR[3]], [r_st6c])
            TS("dve", st6[:, 7, :], st6[:, 7, :], 1.0 / 64.0, None, ALU.mult, None, [r_st6c], [r_st6c])
            TT("dve", q["vret"][:, t, :].rearrange("p (h e) -> p h e", e=64), v3,
               st6[:, 7, :].unsqueeze(2).broadcast_to([128, 6, 64]), ALU.subtract, [PR[3], r_st6c], [q["r_vret"][t]])
            CP("act", q["nva"][:, slot, :, 0:64], PB[4][:, 0:384].rearrange("p (h e) -> p h e", e=64), [PR[4]],
               [q["r_nv"][slot]])
            conv_half(4, ((2048, 6), (2304, 7)))
            if hoist is not None:
                hoist()
            if use_rope:
                do_rope(PB[2][:, 0:384], PR[2], s, krot[:], r_krot)
            else:
                CP("dve", krot[:], PB[2][:, 0:384], [PR[2]], [r_krot])
            for d in range(2):
                TT("pool", kw[:, d, :].rearrange("p (h e) -> p h e", e=64), krot[:].rearrange("p (h e) -> p h e", e=64),
                   WK[:, d, :].unsqueeze(2).broadcast_to([128, 6, 64]), ALU.mult, [r_krot, r_wk], [r_kw])
            for c in range(3):
                TR(PB[7][:, c * 128:(c + 1) * 128], krot[:, c * 128:(c + 1) * 128], identb[:], [r_krot, r_idb], [PR[7]])
            CP("dve", q["kT"][:, :, t * 128:(t + 1) * 128], PB[7][:, 0:384].rearrange("p (c i) -> p c i", i=128),
               [PR[7]], [q["r_kT"][t]])
            CP("dve", q["nkT"][:, :, slot * 128:(slot + 1) * 128], PB[5][:, 0:384].rearrange("p (c i) -> p c i", i=128),
               [PR[5]], [q["r_nk"][slot]])
            CP("act", cvt[:], PB[6][:, 256:512].rearrange("p (c i) -> p c i", i=128), [PR[6]], [r_cvt])
            TT("dve", q["uT"][:, :, 1 + t * 128:1 + (t + 1) * 128], PB[6][:, 0:256].rearrange("p (c i) -> p c i", i=128),
               cvt[:], ALU.mult, [PR[6], r_cvt], [q["r_u"][t + 1]])
            ACT(cvu[:], PB[4][:, 256:512].rearrange("p (c i) -> p c i", i=128), AF.Silu, [PR[4]], [r_cvu])
            TT("dve", q["gcT"][:, :, t * 128:(t + 1) * 128], PB[4][:, 0:256].rearrange("p (c i) -> p c i", i=128),
               cvu[:], ALU.mult, [PR[4], r_cvu], [q["r_gcT"][t]])
            for d in range(2):
                for pr in range(3):
                    MM(PB[2 + d][:, pr * 128:(pr + 1) * 128], kw[:, d, pr * 128:(pr + 1) * 128],
                       q["vret"][:, t, pr * 128:(pr + 1) * 128], True, True, [r_kw, q["r_vret"][t]], [PR[2 + d]])

            def diag(bank, hp):
                rows = slice(hp * 64, hp * 64 + 64)
                return PB[bank][rows, 0:384].rearrange("p (a b) -> p a b", b=128)[:, :, hp * 64:(hp + 1) * 64]

            CP("pool", q["sfst"][:, t, :, :], q["runf"][:], [q["r_runf"]], [q["r_sfst"][t]])
            TT("pool", tmp64, q["runf"][:], G128[:, 0, :].unsqueeze(2).broadcast_to([128, 3, 64]), ALU.mult,
               [q["r_runf"], r_g128], [r_tmp64])
            for hp in range(2):
                rows = slice(hp * 64, hp * 64 + 64)
                TT("dve", q["runf"][rows, :, :], diag(2, hp), tmp64[rows, :, :], ALU.add, [PR[2], r_tmp64], [q["r_runf"]])
                CP("act", q["tbst"][rows, t, :, :], diag(3, hp), [PR[3]], [q["r_tbst"][t]])
            if use_rope:
                for hp in range(2):
                    rows = slice(hp * 64, hp * 64 + 64)
                    TT("dve", tmp2[rows, :, :], diag(3, hp), gpow[rows, 2, :, t:t + 1].broadcast_to([64, 3, 64]), ALU.mult,
                       [PR[3], r_gpow], [r_tmp2])
                TT("pool", totb[:], totb[:], tmp2, ALU.add, [r_totb, r_tmp2], [r_totb])

        def P1_finish(q):
            nt = q["nt"]
            for t in range(nt - 1, -1, -1):
                CP("dve", tmp64, q["tbst"][:, t, :, :], [q["r_tbst"][t]], [r_tmp64])
                CP("dve", q["tbst"][:, t, :, :], q["runb"][:], [q["r_runb"]], [q["r_tbst"][t]])
                for pr in range(3):
                    STT(q["runb"][:, pr, :], q["runb"][:, pr, :], G128[:, 1, pr:pr + 1], tmp64[:, pr, :], ALU.mult,
                        ALU.add, [q["r_runb"], r_g128, r_tmp64], [q["r_runb"]])

        def P2(q, t, dst, dst_r, l, is_main, hoist=None, hooks=None, late_ret=False, prev_lb=None):
            nt = q["nt"]
            s = t % 2
            hs, rhs_ = hT[s], r_hT[s]
            def tok_proj(bank, wo, g):
                for k in range(8):
                    MM(PB[bank][:, 0:384], hs[:, k, :], Wb[:, k, wo:wo + 384], k == 0, k == 7, [rhs_, WRK[g][k // 2]],
                       [PR[bank]])

            tok_proj(2, 0, 0)
            for c in range(3):
                for k in range(8):
                    MM(PB[5][:, c * 128:(c + 1) * 128], Wb[:, k, 1152 + c * 128:1152 + (c + 1) * 128], hs[:, k, :],
                       k == 0, k == 7, [rhs_, WRK[3][k // 2]], [PR[5]])
            tok_proj(3, 384, 1)
            tok_proj(4, 768, 2)
            if hooks is not None and "a2" in hooks:
                hooks["a2"]()
            for hp in range(2):
                rows = slice(hp * 64, hp * 64 + 64)
                ACT(nqTm[rows, :, hp, :], PB[5][rows, 0:384].rearrange("p (c i) -> p c i", i=128), AF.Identity, [PR[5]],
                    [r_nqT], scale=0.125)
            if is_main:
                do_rope(PB[2][:, 0:384], PR[2], s, krot[:], r_krot)
            else:
                CP("dve", krot[:], PB[2][:, 0:384], [PR[2]], [r_krot])
            ACT(srg[:], PB[3][:, 0:384], AF.Silu, [PR[3]], [r_srg])
            ACT(sng[:], PB[4][:, 0:384], AF.Silu, [PR[4]], [r_sng])

            if is_main:
                dlo = -3 if t == NT - 1 else -2
                dhi = 3 if t == 0 else 2
                blocks = [(q, t + dt + 2, dt - dlo) for dt in range(dlo, dhi + 1)] + [(CQ, c, None) for c in range(NCX)]
            else:
                blocks = [(CQ, c, None) for c in range(NCX)]
            nb = len(blocks)

            def unit_blocks(half):
                return list(enumerate(blocks))[0:4] if half == 0 else list(enumerate(blocks))[4:nb]

            def N_S(pr, half):
                banks = (5, 6) if half == 0 else (0, 1)
                for j, (bi, (kq, slot, bidx)) in enumerate(unit_blocks(half)):
                    bank = banks[j // 2]
                    col = (j % 2) * 256
                    last = bidx is None
                    MM(PB[bank][:, col:col + 256], kq["nkT"][:, pr, slot * 128:(slot + 1) * 128], nqTm[:, pr, :, :],
                       True, last, [kq["r_nk"][slot], r_nqT], [PR[bank]])
                    if not last:
                        MM(PB[bank][:, col:col + 256], identb[:], biasb[:, pr, bidx, :, :], False, True, [r_idb, r_bias],
                           [PR[bank]])

            def N_E(pr, half):
                banks = (5, 6) if half == 0 else (0, 1)
                ub_ = unit_blocks(half)
                e4 = Eb[half][:].rearrange("p (b h) i -> p b h i", h=2)
                for jb in range(2):
                    nblk = min(2, len(ub_) - jb * 2)
                    if nblk <= 0:
                        continue
                    ACT(e4[:, jb * 2:jb * 2 + nblk, :, :],
                        PB[banks[jb]][:, 0:nblk * 256].rearrange("p (b h i) -> p b h i", h=2, i=128), AF.Exp,
                        [PR[banks[jb]]], [r_Eb[half]])

            def N_PV(pr):
                for hp in range(2):
                    h = 2 * pr + hp
                    for bi, (kq, slot, bidx) in enumerate(blocks):
                        half, j = (0, bi) if bi < 4 else (1, bi - 4)
                        e4 = Eb[half][:].rearrange("p (b h) i -> p b h i", h=2)
                        MM(PB[3][:, h * 65:(h + 1) * 65], e4[:, j, hp, :], kq["nva"][:, slot, h, :], bi == 0, bi == nb - 1,
                           [r_Eb[half], kq["r_nv"][slot]], [PR[3]])

            def N_F():
                ona = PB[3][:, 0:390].rearrange("p (h e) -> p h e", e=65)
                S.op("dve", lambda h_: h_.reciprocal(out=st6[:, 6, :], in_=ona[:, :, 64]), [PR[3]], [r_st6b])
                tA3 = tA[:].rearrange("p (h e) -> p h e", e=64)
                TT("dve", tA3, ona[:, :, 0:64], st6[:, 6, :].unsqueeze(2).broadcast_to([128, 6, 64]), ALU.mult,
                   [PR[3], r_st6b], [r_tA])
                TT("pool", yna[:], tA[:], sng[:], ALU.mult, [r_tA, r_sng], [r_yna])

            def R1():
                for c in range(3):
                    TR(PB[7][:, c * 128:(c + 1) * 128], krot[:, c * 128:(c + 1) * 128], identb[:], [r_krot, r_idb],
                       [PR[7]])
                CP("dve", qT[:], PB[7][:, 0:384].rearrange("p (c i) -> p c i", i=128), [PR[7]], [r_qT])
                for d in range(2):
                    TT("pool", qsc[:, d, :, :], qT[:], WQ[:, d, :, :], ALU.mult, [r_qT, r_wq], [r_qsc])
                if is_main:
                    for d in range(2):
                        TT("pool", qsc[:, 2 + d, :, :], qsc[:, d, :, :], gpow[:, d, :, t:t + 1].broadcast_to([128, 3, 128]),
                           ALU.mult, [r_qsc, r_gpow], [r_qsc])

            def R2():
                for h in range(6):
                    pr, hp = h // 2, h % 2
                    rows = slice(hp * 64, hp * 64 + 64)
                    MM(PB[hp][:, pr * 128:(pr + 1) * 128], q["kT"][rows, pr, t * 128:(t + 1) * 128], qT[rows, pr, :],
                       True, True, [q["r_kT"][t], r_qT], [PR[hp]], tp=(hp * 64, 0))
                for hp in range(2):
                    TT("dve", AT[:, hp * 3:hp * 3 + 3, :], PB[hp][:, 0:384].rearrange("p (h i) -> p h i", i=128),
                       Dcomb[:, hp * 3:hp * 3 + 3, :], ALU.mult, [PR[hp], r_dc], [r_AT])

            def R3():
                for h in range(6):
                    pr, hp = h // 2, h % 2
                    rows = slice(hp * 64, hp * 64 + 64)
                    o = PB[2][:, h * 64:(h + 1) * 64]
                    tp = (hp * 64, 0)
                    MM(o, AT[:, hp * 3 + pr, :], q["vret"][:, t, h * 64:(h + 1) * 64], True, False,
                       [r_AT, q["r_vret"][t]], [PR[2]])
                    MM(o, qsc[rows, 0, pr, :], q["sfst"][rows, t, pr, :], False, False, [r_qsc, q["r_sfst"][t]], [PR[2]],
                       tp=tp)
                    MM(o, qsc[rows, 1, pr, :], q["tbst"][rows, t, pr, :], False, not is_main, [r_qsc, q["r_tbst"][t]],
                       [PR[2]], tp=tp)
                    if is_main:
                        MM(o, qsc[rows, 2, pr, :], Sin[rows, 0, pr, :], False, False, [r_qsc, r_sin], [PR[2]], tp=tp)
                        MM(o, qsc[rows, 3, pr, :], Sin[rows, 1, pr, :], False, True, [r_qsc, r_sin], [PR[2]], tp=tp)

            def R4():
                o3 = PB[2][:, 0:384].rearrange("p (h e) -> p h e", e=64)
                ACT(tB[:], PB[2][:, 0:384], AF.Square, [PR[2]], [r_tB])
                S.op("dve", lambda h_: h_.reduce_sum(out=st6[:, 1, :], in_=tB[:].rearrange("p (h e) -> p h e", e=64),
                                                     axis=AX.X), [r_tB], [r_st6])
                ACT(st6[:, 4, :], st6[:, 1, :], AF.Sqrt, [r_st6, r_eps], [r_st6], bias=epsc[:], scale=1.0 / 64.0)
                S.op("dve", lambda h_: h_.reciprocal(out=st6[:, 5, :], in_=st6[:, 4, :]), [r_st6], [r_st6])
                tB3 = tB[:].rearrange("p (h e) -> p h e", e=64)
                TT("dve", tB3, o3, st6[:, 5, :].unsqueeze(2).broadcast_to([128, 6, 64]), ALU.mult, [PR[2], r_st6], [r_tB])
                TT("pool", yret[:], tB[:], srg[:], ALU.mult, [r_tB, r_srg], [r_yret])

            def C():
                uT_, gc_ = q["uT"], q["gcT"]
                ru = [q["r_u"][t], q["r_u"][t + 1], q["r_u"][t + 2]]
                b0 = t * 128

                def wb(j):
                    return cp_t[:, l, :, j:j + 1].broadcast_to([128, 2, 128])

                TT("pool", cvt[:], uT_[:, :, b0:b0 + 128], wb(0), ALU.mult, ru + [r_misc], [r_cvt])
                TT("pool", cvu[:], uT_[:, :, b0 + 1:b0 + 129], wb(1), ALU.mult, ru + [r_misc], [r_cvu])
                TT("pool", cvt[:], cvt[:], cvu[:], ALU.add, [r_cvt, r_cvu], [r_cvt])
                TT("pool", cvu[:], uT_[:, :, b0 + 2:b0 + 130], wb(2), ALU.mult, ru + [r_misc], [r_cvu])
                TT("pool", cvt[:], cvt[:], cvu[:], ALU.add, [r_cvt, r_cvu], [r_cvt])
                TT("pool", cvt[:], cvt[:], wb(3), ALU.add, [r_cvt, r_misc], [r_cvt])
                TT("pool", yT[:, 0:2, :], cvt[:], gc_[:, :, t * 128:(t + 1) * 128], ALU.mult, [r_cvt, q["r_gcT"][t]], [r_yTc])

            def Y_ret():
                for c in range(3):
                    TR(PB[7][:, c * 128:(c + 1) * 128], yret[:, c * 128:(c + 1) * 128], identb[:], [r_yret, r_idb],
                       [PR[7]])
                CP("act", yT[:, 2:5, :], PB[7][:, 0:384].rearrange("p (c i) -> p c i", i=128), [PR[7]], [r_yTr])

            def Y_na():
                for c in range(3):
                    TR(PB[7][:, (3 + c) * 128:(4 + c) * 128], yna[:, c * 128:(c + 1) * 128], identb[:], [r_yna, r_idb],
                       [PR[7]])
                CP("act", yT[:, 5:8, :], PB[7][:, 384:768].rearrange("p (c i) -> p c i", i=128), [PR[7]], [r_yTn])

            def O(ks, first, last):
                for j in range(2):
                    for k in ks:
                        rk_ = r_yTc if k < 2 else (r_yTr if k < 5 else r_yTn)
                        MM(PB[5 + j][:, :], yT[:, k, :], Wb[:, k, 1536 + j * 512:1536 + (j + 1) * 512],
                           first and k == ks[0], last and k == ks[-1], [rk_] + WOK(k), [PR[5 + j]])

            def L_a():
                for j in range(2):
                    TT("dve", zb[:, j * 512:(j + 1) * 512], PB[5 + j][:, :], gate_bc[:, j * 512:(j + 1) * 512], ALU.mult,
                       [PR[5 + j], r_gate], [r_zb])
                STT(zb[:], xt[s][:], ALPHA, zb[:], ALU.mult, ALU.add, [r_xt[s], r_zb], [r_zb])
                S.op("dve", lambda h_: h_.reduce_sum(out=st1[:, 0:1], in_=zb[:], axis=AX.X), [r_zb], [r_st1a])

            def L_b():
                MSET("pool", st1[:, 1:3], 0.0, [r_st1])
                ACT(PB[4][:, :], zb[:, 0:512], AF.Square, [r_zb], [PR[4], r_st1], accum=st1[:, 1:2])
                ACT(PB[4][:, :], zb[:, 512:1024], AF.Square, [r_zb], [PR[4], r_st1], accum=st1[:, 2:3])
                TS("pool", st1[:, 0:1], st1[:, 0:1], 1.0 / D, None, ALU.mult, None, [r_st1a], [r_st1a])
                TT("pool", st1[:, 1:2], st1[:, 1:2], st1[:, 2:3], ALU.add, [r_st1], [r_st1])
                TT("pool", st1[:, 2:3], st1[:, 0:1], st1[:, 0:1], ALU.mult, [r_st1a, r_st1], [r_st1])
                TS("pool", st1[:, 1:2], st1[:, 1:2], 1.0 / D, None, ALU.mult, None, [r_st1], [r_st1])
                TT("pool", st1[:, 3:4], st1[:, 1:2], st1[:, 2:3], ALU.subtract, [r_st1], [r_st1])
                ACT(st1[:, 4:5], st1[:, 3:4], AF.Sqrt, [r_st1, r_eps], [r_st1], bias=epsc[:], scale=1.0)
                S.op("dve", lambda h_: h_.reciprocal(out=st1[:, 5:6], in_=st1[:, 4:5]), [r_st1], [r_st1])
                STT(st1[:, 6:7], st1[:, 0:1], -1.0, st1[:, 5:6], ALU.mult, ALU.mult, [r_st1, r_st1a], [r_st1])
                ACT(zb[:], zb[:], AF.Identity, [r_zb, r_st1], [r_zb], bias=st1[:, 6:7], scale=st1[:, 5:6])
                TT("pool", zb[:], zb[:], gbc[:], ALU.mult, [r_zb, r_lnp], [r_zb])
                TT("pool", zb[:], zb[:], bbc[:], ALU.add, [r_zb, r_lnp], [r_zb])
                DMA("sp", dst[t * 128:(t + 1) * 128, :], zb[:], [r_zb], [dst_r[t]] if dst_r is not None else [], "st")

            if prev_lb is not None:
                prev_lb()

            has_b = nb > 4
            N_S(0, 0)
            R1()
            N_E(0, 0)
            R2()
            if has_b:
                N_S(0, 1)
                N_E(0, 1)
            N_PV(0)
            N_S(1, 0)
            C()
            if not late_ret:
                R3()
            N_E(1, 0)
            if has_b:
                N_S(1, 1)
                N_E(1, 1)
            N_PV(1)
            if not late_ret:
                R4()
            N_S(2, 0)
            N_E(2, 0)
            if has_b:
                N_S(2, 1)
            if not late_ret:
                Y_ret()
            if has_b:
                N_E(2, 1)
            if not late_ret:
                O([0, 1, 2, 3, 4], True, False)
            if hoist is not None:
                hoist()
            N_PV(2)
            if hooks is not None and "post_nbr" in hooks:
                hooks["post_nbr"]()
            if late_ret:
                if hooks is not None and "pre_ret" in hooks:
                    hooks["pre_ret"]()
                R3()
                R4()
                Y_ret()
                O([0, 1, 2, 3, 4], True, False)
            N_F()
            Y_na()
            O([5, 6, 7], False, True)
            if hooks is not None and "o" in hooks:
                hooks["o"]()
            L_a()
            return L_b

        def exchange_gathers(q, gb, r_gb):
            nk, nv, uT_ = q["nkT"], q["nva"], q["uT"]

            def gather(o, col0, n, which, wr, shape3=None):
                cands = (0, 1, 2) if which == 0 else (1, 2, 3)
                for j, k in enumerate(cands):
                    e = Eb[j % 2]
                    re = r_Eb[j % 2]
                    stg = e[:].rearrange("p a b -> p (a b)")[:, 0:n]
                    DMA("sp", stg, gb[k * 128:(k + 1) * 128, col0:col0 + n], [r_gb], [re], "ex")
                    sv = stg if shape3 is None else stg.rearrange("p (a b) -> p a b", b=shape3)
                    sc = sel_t[:, which * 4 + k:which * 4 + k + 1]
                    if j == 0:
                        TS("dve", o, sv, sc, None, ALU.mult, None, [re, r_misc], wr)
                    else:
                        STT(o, sv, sc, o, ALU.mult, ALU.add, [re, r_misc] + wr, wr)

            items = [
                lambda: gather(nk[:, :, 0:256], OFF_NKT_BOT, 768, 0, [q["r_nk"][0], q["r_nk"][1]], 256),
                lambda: gather(nk[:, :, (NT + 2) * 128:(NT + 4) * 128], OFF_NKT_TOP, 768, 1,
                               [q["r_nk"][NT + 2], q["r_nk"][NT + 3]], 256),
                lambda: gather(nv[:, 0:2, :, :].rearrange("p a h e -> p (a h e)"), OFF_NVA_BOT, 780, 0,
                               [q["r_nv"][0], q["r_nv"][1]]),
                lambda: gather(nv[:, NT + 2:NT + 4, :, :].rearrange("p a h e -> p (a h e)"), OFF_NVA_TOP, 780, 1,
                               [q["r_nv"][NT + 2], q["r_nv"][NT + 3]]),
                lambda: gather(uT_[:, :, 0], OFF_U_LAST, 2, 0, [q["r_u"][0]]),
                lambda: gather(uT_[:, :, NT * 128 + 1], OFF_U_FIRST, 2, 1, [q["r_u"][NT + 1]]),
            ]
            return items

        def exchange(l):
            q = MQ
            pb, gb = packB[l], gathB[l]
            r_pb, r_gb = R("pb"), R("gb")
            pf = pb[:, OFF_ST:OFF_ST + 768].bitcast(F32)
            DMA("sp", pf[:, 0:192].rearrange("p (a b) -> p a b", b=64), q["runf"][:], [q["r_runf"]], [r_pb], "ex")
            DMA("sp", pf[:, 192:384].rearrange("p (a b) -> p a b", b=64), totb[:], [r_totb], [r_pb], "ex")
            nk, nv, uT_ = q["nkT"], q["nva"], q["uT"]
            DMA("sp", pb[:, OFF_NKT_TOP:OFF_NKT_TOP + 768].rearrange("p (a b) -> p a b", b=256), nk[:, :, 2 * 128:4 * 128],
                [q["r_nk"][2], q["r_nk"][3]], [r_pb], "ex")
            DMA("sp", pb[:, OFF_NKT_BOT:OFF_NKT_BOT + 768].rearrange("p (a b) -> p a b", b=256),
                nk[:, :, (NT) * 128:(NT + 2) * 128], [q["r_nk"][NT], q["r_nk"][NT + 1]], [r_pb], "ex")
            DMA("sp", pb[:, OFF_NVA_TOP:OFF_NVA_TOP + 780], nv[:, 2:4, :, :].rearrange("p a h e -> p (a h e)"),
                [q["r_nv"][2], q["r_nv"][3]], [r_pb], "ex")
            DMA("sp", pb[:, OFF_NVA_BOT:OFF_NVA_BOT + 780], nv[:, NT:NT + 2, :, :].rearrange("p a h e -> p (a h e)"),
                [q["r_nv"][NT], q["r_nv"][NT + 1]], [r_pb], "ex")
            MSET("dve", ub[:, 4:8], 0.0, [r_ub])
            CP("dve", ub[:, 0:2], uT_[:, :, 1], [q["r_u"][1]], [r_ub])
            CP("dve", ub[:, 2:4], uT_[:, :, NT * 128], [q["r_u"][NT]], [r_ub])
            DMA("sp", pb[:, OFF_U_FIRST:OFF_U_FIRST + 8], ub[:], [r_ub], [r_pb], "ex")
            S.op("pool", lambda h: h.collective_compute("AllGather", ALU.bypass, replica_groups=RG, ins=[pb[:, :]],
                                                        outs=[gb[:, :]]), [r_pb], [r_gb], dma="cc", inc=1)
            return dict(gb=gb, r_gb=r_gb)

        def exchange_recv_halo(l, X):
            return exchange_gathers(MQ, X["gb"], X["r_gb"])

        def exchange_recv(l, X):
            gb, r_gb = X["gb"], X["r_gb"]
            stgs = [(tA[:], r_tA),
                    (Eb[0][:].rearrange("p a b -> p (a b)")[:, 0:768].bitcast(F32), r_Eb[0]),
                    (Eb[1][:].rearrange("p a b -> p (a b)")[:, 0:768].bitcast(F32), r_Eb[1])]
            for k in range(4):
                stg, rs = stgs[k % 3]
                DMA("sp", stg, gb[k * 128:(k + 1) * 128, OFF_ST:OFF_ST + 768].bitcast(F32), [r_gb], [rs], "ex")
                v4 = stg.rearrange("p (d a b) -> p d a b", d=2, a=3)
                cb_ = coef[:, :, :, k:k + 1].broadcast_to([128, 2, 3, 64])
                if k == 0:
                    TT("dve", sacc, v4, cb_, ALU.mult, [rs, r_coef], [r_sacc])
                else:
                    TT("dve", v4, v4, cb_, ALU.mult, [rs, r_coef], [rs])
                    TT("dve", sacc, sacc, v4, ALU.add, [rs, r_sacc], [r_sacc])
            t4 = tA[:].rearrange("p (d a b) -> p d a b", d=2, a=3)
            for d in range(2):
                s0 = CQ["runf"] if d == 0 else CQ["runb"]
                r_s0 = CQ["r_runf"] if d == 0 else CQ["r_runb"]
                TT("dve", t4[:, d, :, :], s0[:], coef[:, d, :, 4:5].broadcast_to([128, 3, 64]), ALU.mult,
                   [r_s0, r_coef, r_tA], [r_tA])
            TT("dve", sacc, sacc, t4, ALU.add, [r_tA, r_sacc], [r_sacc])
            CP("dve", Sin[:], sacc, [r_sacc], [r_sin])

        def reset_run(q):
            if q is MQ:
                MSET("pool", totb[:], 0.0, [r_totb])
            MSET("pool", q["runf"][:], 0.0, [q["r_runf"]])
            MSET("pool", q["runb"][:], 0.0, [q["r_runb"]])

        try:
            for l in range(2):
                if l == 0:
                    load_w1(0, 0)
                    load_w1(0, 1)
                ck(10 * l + 0)
                if l == 0:
                    mod_setup(0)
                    mod_setup(1)
                layer_setup(l)
                ck(10 * l + 1)
                csrc, csrc_r = (ctx_in, None) if l == 0 else (xc1, xc1_r)
                msrc, msrc_r = (x_in, None) if l == 0 else (x1, x1_r)
                mdst, mdst_r = (x1, x1_r) if l == 0 else (out, None)
                reset_run(CQ)
                reset_run(MQ)
                LX(msrc, msrc_r, 0, 0, l, rope_on=True)
                for t in range(NT):
                    hz = (lambda t=t: LX(msrc, msrc_r, t + 1, 0, l, rope_on=True)) if t + 1 < NT else \
                        (lambda: LX(csrc, csrc_r, 0, 1))
                    P1(MQ, t, msrc, msrc_r, 0, True, hoist=hz)
                ck(10 * l + 2)
                X = exchange(l)
                for t in range(NCX):
                    hz = (lambda t=t: LX(csrc, csrc_r, t + 1, 1)) if t + 1 < NCX else None
                    P1(CQ, t, csrc, csrc_r, 1, False, hoist=hz)
                ck(10 * l + 3)
                P1_finish(CQ)
                P1_finish(MQ)
                load_w2(l)
                ck(10 * l + 4)
                ck(10 * l + 5)
                if l == 0:
                    build_gate(1)
                    ck(5.1)
                    LX(csrc, csrc_r, 0, 1)
                    lb = None
                    for t in range(NCX):
                        hz = (lambda t=t: LX(csrc, csrc_r, t + 1, 1)) if t + 1 < NCX else None
                        lb = P2(CQ, t, xc1, xc1_r, l, False, hoist=hz, prev_lb=lb)
                    lb()
                ck(10 * l + 6)
                build_gate(0)
                order = [2, 3, 4, 5, 6, 7, 8, 9, 10, 11, 12, 13, 0, 1, 14, 15]
                LX(msrc, msrc_r, order[0], 0, l, rope_on=True, bias_on=True)
                halo_items = exchange_recv_halo(l, X)
                for i, t in enumerate(order):
                    hz = (lambda tn=order[i + 1]: LX(msrc, msrc_r, tn, 0, l, rope_on=True, bias_on=True)) \
                        if i + 1 < NT else None
                    hk = None
                    if i == 0:
                        hk = {"pre_ret": (lambda: exchange_recv(l, X))}
                    if 1 <= i <= len(halo_items):
                        hk = {"post_nbr": halo_items[i - 1]}
                    if l == 0 and i == NT - 1:
                        hk = {"a2": (lambda: load_w1(1, 0)), "o": (lambda: load_w1(1, 1))}
                    lbm = P2(MQ, t, mdst, mdst_r, l, True, hoist=hz, hooks=hk, late_ret=(i == 0),
                             prev_lb=(lbm if i > 0 else None))
                    ck(10 * l + 7)
                lbm()
                ck(10 * l + 8)
        except _Stop:
            pass
        S.finish()
        S.emit_all()
        print("ops", S.nops, "sems", S.nsem)
    return nc


def _host_tables():
    P = np.arange(128)
    I = np.arange(128)
    cst = np.zeros((128, 5 * 128 + 2), np.float32)
    cst[:, 0:128] = np.eye(128)
    diff = I[None, :] - P[:, None]
    BIG = 1.0e6
    cst[:, 128:256] = np.where(diff >= 0, diff, BIG)
    cst[:, 256:384] = np.where(diff < 0, -diff, BIG)
    cst[:, 384:512] = (I + 1)[None, :]
    cst[:, 512:640] = (128 - I)[None, :]
    cst[:, 640] = 127 - P
    cst[:, 641] = P
    return cst


def _rope_tables(rank):
    nf = 16
    inv = (10000.0 ** (-np.arange(nf, dtype=np.float64) / nf))
    out = np.zeros((NT, 128, 128), np.float32)
    for t in range(NT):
        p = np.arange(128)
        row = (32 * rank + 2 * t + p // 64).astype(np.float64)
        col = (p % 64).astype(np.float64)
        ar = (row[:, None].astype(np.float32) * inv.astype(np.float32)[None, :]).astype(np.float64)
        ac = (col[:, None].astype(np.float32) * inv.astype(np.float32)[None, :]).astype(np.float64)
        cr, sr, cc, sc = np.cos(ar), np.sin(ar), np.cos(ac), np.sin(ac)
        out[t, :, 0:64] = np.concatenate([cr, cr, cc, cc], 1)
        out[t, :, 64:128] = np.concatenate([-sr, sr, -sc, sc], 1)
    return out


def _bias_tables(rpb, rank):
    out = np.full((NT, 128, 6, 6, 128), NEG, np.float32)
    j = np.arange(128)
    i = np.arange(128)
    rkl, ck = j // 64, j % 64
    rql, cq = i // 64, i % 64
    cstart = np.clip(cq - 8, 0, 48)
    colok = (ck[:, None] >= cstart[None, :]) & (ck[:, None] < cstart[None, :] + 16)
    dcol = ck[:, None] - cq[None, :] + 15
    for t in range(NT):
        dlo = -3 if t == NT - 1 else -2
        dhi = 3 if t == 0 else 2
        rq = 32 * rank + 2 * t + rql
        rstart = np.clip(rq - 4, 0, 120)
        for dt in range(dlo, dhi + 1):
            b = dt - dlo
            rk = 32 * rank + 2 * (t + dt) + rkl
            rowok = (rk[:, None] >= rstart[None, :]) & (rk[:, None] < rstart[None, :] + 8) & \
                    (rk[:, None] >= 0) & (rk[:, None] < 128)
            ok = rowok & colok
            drow = np.clip(rk[:, None] - rq[None, :] + 7, 0, 14)
            dc = np.clip(dcol, 0, 30)
            vals = rpb[:, drow, dc]
            blk = np.where(ok[None], vals, NEG)
            out[t, :, :, b, :] = blk.transpose(1, 0, 2)
    out = out.reshape(NT, 128, 3, 2, 6, 128).transpose(0, 1, 2, 4, 3, 5)
    return np.ascontiguousarray(out).reshape(NT, 128, 6 * 6 * 128).astype(ml_dtypes.bfloat16)


_NC_CACHE = {}


def kernel(x, c, ctx, c_ctx, w_mod, b_mod, w_in, conv_w, conv_b, ret_decay, na_rpb, w_out, ln_g, ln_b):
    f32 = np.float32
    x = np.asarray(x, f32)
    c = np.asarray(c, f32)
    ctx = np.asarray(ctx, f32)
    c_ctx = np.asarray(c_ctx, f32)
    w_mod = np.ascontiguousarray(np.asarray(w_mod, f32))
    b_mod = np.asarray(b_mod, f32)
    w_in = np.ascontiguousarray(np.asarray(w_in, f32))
    w_out = np.ascontiguousarray(np.asarray(w_out, f32))
    conv_w = np.asarray(conv_w, f32)
    conv_b = np.asarray(conv_b, f32)
    ret_decay = np.asarray(ret_decay, f32)
    na_rpb = np.asarray(na_rpb, f32)
    ln_g = np.asarray(ln_g, f32)
    ln_b = np.asarray(ln_b, f32)

    if "nc" not in _NC_CACHE:
        _NC_CACHE["nc"] = build_program()
    nc = _NC_CACHE["nc"]

    cst = _host_tables()
    bmodT = np.ascontiguousarray(b_mod.reshape(2, 24, 128).transpose(2, 0, 1))
    convp = np.zeros((128, 2, 2, 4), f32)
    for l in range(2):
        for ch in range(2):
            convp[:, l, ch, 0:3] = conv_w[l, :, ch * 128:(ch + 1) * 128].T
            convp[:, l, ch, 3] = conv_b[l, ch * 128:(ch + 1) * 128]
    dec = np.zeros((128, 36), f32)
    dec[:, 0:24] = ret_decay.reshape(1, 24)
    for l in range(2):
        for d in range(2):
            for pr in range(3):
                dec[0:64, 24 + l * 6 + d * 3 + pr] = ret_decay[l, d, 2 * pr]
                dec[64:128, 24 + l * 6 + d * 3 + pr] = ret_decay[l, d, 2 * pr + 1]
    lnp = np.zeros((2, 2, 128, D), f32)
    lnp[:, 0] = ln_g[:, None, :]
    lnp[:, 1] = ln_b[:, None, :]

    in_maps = []
    for core in range(8):
        b, r = core // 4, core % 4
        cvec = np.zeros((128, 8, 2), f32)
        cvec[:, :, 0] = c[b].reshape(8, 128).T
        cvec[:, :, 1] = c_ctx.reshape(8, 128).T
        rankc = np.zeros((128, 20), f32)
        for k in range(4):
            if k < r:
                rankc[:, 0 + k] = 2048.0 * (r - k - 1)
                rankc[:, 10 + k] = 1.0
            if k > r:
                rankc[:, 5 + k] = 2048.0 * (k - r - 1)
                rankc[:, 15 + k] = 1.0
        rankc[:, 4] = 2048.0 * r
        rankc[:, 14] = 1.0
        rankc[:, 9] = 2048.0 * (3 - r)
        rankc[:, 19] = 1.0
        edge = np.zeros((128, 2), f32)
        edge[:, 0] = 1.0 if r > 0 else 0.0
        edge[:, 1] = 1.0 if r < 3 else 0.0
        sel = np.zeros((128, 8), f32)
        if r > 0:
            sel[:, r - 1] = 1.0
        if r < 3:
            sel[:, 4 + r + 1] = 1.0
        idx = np.zeros((128, 2), np.int32)
        idx[:, 0] = (r - 1 if r > 0 else r) * 128 + np.arange(128)
        idx[:, 1] = (r + 1 if r < 3 else r) * 128 + np.arange(128)
        nab = np.stack([_bias_tables(na_rpb[l], r) for l in range(2)], 0)
        in_maps.append({
            "x": np.ascontiguousarray(x[b, r * 2048:(r + 1) * 2048, :]),
            "ctx": np.ascontiguousarray(ctx[b]),
            "cvec": cvec, "w_mod": w_mod, "bmodT": bmodT, "w_in": w_in, "w_out": w_out, "convp": convp,
            "dec": dec, "nabias": nab, "lnp": lnp, "rope": _rope_tables(r), "cst": cst, "rankc": rankc,
            "edge": edge, "idx": idx, "sel": sel,
        })
    res = run_bass_kernel_spmd(nc, in_maps, core_ids=list(range(8)))
    out = np.zeros((2, 8192, D), f32)
    for core in range(8):
        b, r = core // 4, core % 4
        out[b, r * 2048:(r + 1) * 2048, :] = np.asarray(res.results[core]["out"], f32)
    if DEBUG:
        kernel.debug = res.results
    return out
```

```python
import numpy as np
import ml_dtypes
from contextlib import ExitStack
import concourse.bass as bass
import concourse.mybir as mybir
from concourse.bass_utils import run_bass_kernel_spmd

F32 = mybir.dt.float32
BF16 = mybir.dt.bfloat16
I32 = mybir.dt.int32
ALU = mybir.AluOpType
AF = mybir.ActivationFunctionType
AX = mybir.AxisListType

DEBUG = False
NT = 16
NCX = 2
D = 1024
ALPHA = float((2 * 2) ** 0.25)
LN_EPS = 1e-5
NEG = -30000.0
PKB = 3104 + 768
OFF_ST = 3104
OFF_NKT_TOP, OFF_NKT_BOT, OFF_NVA_TOP, OFF_NVA_BOT, OFF_U_FIRST, OFF_U_LAST = 0, 768, 1536, 2316, 3096, 3098
C_RK, C_RV, C_NK, C_NV, C_RQ, C_RG, C_NQ, C_NG, C_CH, C_CB, C_CCG, C_CZ = (
    0, 384, 768, 1152, 1536, 1920, 2304, 2688, 3072, 3328, 3584, 3840)


class Reg:
    __slots__ = ("name", "w", "rs", "excl")

    def __init__(self, name="", excl=False):
        self.name = name
        self.w = None
        self.rs = {}
        self.excl = excl


class Sched:
    ENG = ("pe", "act", "dve", "pool", "sp")
    LIMIT = 2000
    NDMA = 6

    def __init__(self, nc, stack):
        self.nc = nc
        self.stack = stack
        self.prog = {e: [] for e in self.ENG}
        self.state = {}
        self.waited = {e: {} for e in self.ENG}
        self.nsem = 0
        self.nops = 0
        self.dcount = {}

    def _tick(self, key, inc):
        st = self.state.get(key)
        if st is None or st[1] + inc > self.LIMIT:
            s = self.stack.enter_context(self.nc.semaphore(f"s{self.nsem}_{key}"))
            self.nsem += 1
            st = [s, 0]
            self.state[key] = st
            self.allsems.append(st)
        st[1] += inc
        return st[0], st[1]

    allsems = None

    def op(self, eng, fn, reads=(), writes=(), dma=None, inc=None):
        if self.allsems is None:
            self.allsems = []
        deps = {}

        def add(tok):
            if tok is None:
                return
            e, s, v = tok
            if e == "pe" and eng == "pe" and dma is None:
                return
            k = id(s)
            if k not in deps or deps[k][1] < v:
                deps[k] = (s, v)

        for r in reads:
            add(r.w)
            if r.excl:
                for t in r.rs.values():
                    if t[0] != eng:
                        add(t)
        for w in writes:
            add(w.w)
            for t in w.rs.values():
                add(t)
        wd = self.waited[eng]
        waits = []
        for k, (s, v) in deps.items():
            if wd.get(k, 0) < v:
                wd[k] = v
                waits.append((s, v))
        isdma = dma is not None
        if inc is None:
            inc = 16 if isdma else 1
        if isdma:
            i = self.dcount.get(dma, 0)
            self.dcount[dma] = i + 1
            slot = i % self.NDMA
            sem, val = self._tick(f"dma_{dma}_{slot}", inc)
            if val > inc and wd.get(id(sem), 0) < val - inc:
                wd[id(sem)] = val - inc
                waits.append((sem, val - inc))
        else:
            sem, val = self._tick(eng, inc)
        tok = (None if isdma else eng, sem, val)

        def emit(h, waits=waits, fn=fn, sem=sem, inc=inc):
            for s, v in waits:
                h.wait_ge(s, v)
            fn(h).then_inc(sem, inc)

        self.prog[eng].append(emit)
        self.nops += 1
        for r in reads:
            k = id(sem)
            old = r.rs.get(k)
            if old is None or old[2] < val:
                r.rs[k] = tok
        for w in writes:
            w.w = tok
            w.rs = {}
        return tok

    def finish(self):
        fin = [(st[0], st[1]) for st in self.allsems]

        def emit(h, fin=fin):
            for s, v in fin:
                h.wait_ge(s, v)

        self.prog["sp"].append(emit)

    def emit_all(self):
        nc = self.nc
        prog = self.prog
        with nc.Block() as block:
            @block.tensor
            def _(h):
                for f in prog["pe"]:
                    f(h)

            @block.scalar
            def _(h):
                for f in prog["act"]:
                    f(h)

            @block.vector
            def _(h):
                for f in prog["dve"]:
                    f(h)

            @block.gpsimd
            def _(h):
                for f in prog["pool"]:
                    f(h)

            @block.sync
            def _(h):
                for f in prog["sp"]:
                    f(h)


def build_program():
    nc = bass.Bass("TRN2", target_bir_lowering=False)

    def din(name, shape, dt=F32):
        return nc.dram_tensor(name, list(shape), dt, kind="ExternalInput").ap()

    x_in = din("x", [NT * 128, D])
    ctx_in = din("ctx", [NCX * 128, D])
    cvec = din("cvec", [128, 8, 2])
    w_mod = din("w_mod", [2, D, 3 * D])
    bmodT = din("bmodT", [128, 2, 24])
    w_in = din("w_in", [2, D, 4096])
    w_out = din("w_out", [2, D, D])
    convp = din("convp", [128, 2, 2, 4])
    dec = din("dec", [128, 36])
    nabias = din("nabias", [2, NT, 128, 6 * 6 * 128], BF16)
    lnp = din("lnp", [2, 2, 128, D])
    rope = din("rope", [NT, 128, 128])
    cst = din("cst", [128, 5 * 128 + 2])
    rankc = din("rankc", [128, 20])
    edge = din("edge", [128, 2])
    selin = din("sel", [128, 8])
    idx = din("idx", [128, 2], I32)
    out = nc.dram_tensor("out", [NT * 128, D], F32, kind="ExternalOutput").ap()
    okind = "ExternalOutput" if DEBUG else "Internal"
    x1 = nc.dram_tensor("x1", [NT * 128, D], F32, kind=okind).ap()
    xc1 = nc.dram_tensor("xc1", [NCX * 128, D], F32, kind=okind).ap()
    packF = [nc.dram_tensor(f"packF{l}", [128, 384], F32).ap() for l in range(2)]
    gathF = [nc.dram_tensor(f"gathF{l}", [512, 384], F32).ap() for l in range(2)]
    packB = [nc.dram_tensor(f"packB{l}", [128, PKB], BF16).ap() for l in range(2)]
    gathB = [nc.dram_tensor(f"gathB{l}", [512, PKB], BF16).ap() for l in range(2)]
    RG = [[0, 1, 2, 3], [4, 5, 6, 7]]

    with ExitStack() as st:
        S = Sched(nc, st)

        def sb(name, shape, dt=F32):
            return st.enter_context(nc.sbuf_tensor(name, list(shape), dt))

        def R(n=""):
            return Reg(n)

        def Rs(n, k):
            return [Reg(f"{n}{i}") for i in range(k)]

        import os
        KSTOP = float(os.environ.get("KSTOP", "999"))

        class _Stop(Exception):
            pass

        def ck(n):
            if n >= KSTOP:
                raise _Stop()

        def MM(o, lhsT, rhs, start, stop, rd, wr, tp=None):
            if tp is None:
                S.op("pe", lambda h: h.matmul(o, lhsT, rhs, start=start, stop=stop), rd, wr)
            else:
                S.op("pe", lambda h: h.matmul(o, lhsT, rhs, start=start, stop=stop, tile_position=tp), rd, wr)

        def TR(o, i, ident, rd, wr):
            S.op("pe", lambda h: h.transpose(o, i, ident), rd, wr)

        def ACT(o, i, func, rd, wr, bias=None, scale=None, accum=None):
            kw = {}
            if bias is not None:
                kw["bias"] = bias
            if scale is not None:
                kw["scale"] = scale
            if accum is not None:
                kw["accum_out"] = accum
            S.op("act", lambda h: h.activation(out=o, in_=i, func=func, **kw), rd, wr)

        def TT(e, o, a, b, op, rd, wr):
            S.op(e, lambda h: h.tensor_tensor(out=o, in0=a, in1=b, op=op), rd, wr)

        def TS(e, o, a, s1, s2, op0, op1, rd, wr):
            if s2 is None:
                S.op(e, lambda h: h.tensor_scalar(out=o, in0=a, scalar1=s1, scalar2=None, op0=op0), rd, wr)
            else:
                S.op(e, lambda h: h.tensor_scalar(out=o, in0=a, scalar1=s1, scalar2=s2, op0=op0, op1=op1), rd, wr)

        def STT(o, a, s, b, op0, op1, rd, wr):
            S.op("dve", lambda h: h.scalar_tensor_tensor(out=o, in0=a, scalar=s, in1=b, op0=op0, op1=op1), rd, wr)

        def CP(e, o, i, rd, wr):
            if e == "act":
                S.op("act", lambda h: h.copy(out=o, in_=i), rd, wr)
            else:
                S.op(e, lambda h: h.tensor_copy(out=o, in_=i), rd, wr)

        def MSET(e, ap, v, wr):
            S.op(e, lambda h: h.memset(ap, v), [], wr)

        def DMA(e, o, i, rd, wr, stream):
            S.op(e, lambda h: h.dma_start(out=o, in_=i), rd, wr, dma=stream)

        PB = [st.enter_context(nc.psum_tensor(f"pb{i}", [128, 512], F32)) for i in range(7)]
        PB.append(st.enter_context(nc.psum_tensor("pb7", [128, 1024], BF16)))
        PR = [Reg(f"pb{i}", excl=True) for i in range(8)]

        cs = sb("cs", [128, 5 * 128 + 2])
        r_cs = R("cs")
        DMA("sp", cs[:], cst[:, :], [], [r_cs], "ld")
        identf = cs[:, 0:128]
        DFt = cs[:, 128:256]
        DBt = cs[:, 256:384]
        ip1 = cs[:, 384:512]
        rev = cs[:, 512:640]
        jrev = cs[:, 640:641]
        jpos = cs[:, 641:642]
        identb = sb("identb", [128, 128], BF16)
        r_idb = R("idb")
        CP("dve", identb[:], identf, [r_cs], [r_idb])
        onesf = sb("onesf", [128, 128])
        r_ones = R("ones")
        MSET("pool", onesf[:], 1.0, [r_ones])
        epsc = sb("epsc", [128, 1])
        r_eps = R("eps")
        MSET("pool", epsc[:], LN_EPS, [r_eps])
        rk_t = sb("rk_t", [128, 20])
        eg_t = sb("eg_t", [128, 2])
        ix_t = sb("ix_t", [128, 2], I32)
        cp_t = sb("cp_t", [128, 2, 2, 4])
        r_misc = R("misc")
        DMA("sp", rk_t[:], rankc[:, :], [], [r_misc], "ld")
        DMA("sp", eg_t[:], edge[:, :], [], [r_misc], "ld")
        sel_t = sb("sel_t", [128, 8])
        DMA("sp", sel_t[:], selin[:, :], [], [r_misc], "ld")
        DMA("sp", ix_t[:], idx[:, :], [], [r_misc], "ld")
        DMA("sp", cp_t[:], convp[:, :, :, :], [], [r_misc], "ld")
        dc = sb("dc", [128, 36])
        lg = sb("lg", [128, 36])
        r_lg = R("lg")
        DMA("sp", dc[:], dec[:, :], [], [r_lg], "ld")
        ACT(lg[:], dc[:], AF.Exp, [r_lg], [r_lg], scale=-1.0)
        TS("dve", lg[:], lg[:], 1.0, None, ALU.add, None, [r_lg], [r_lg])
        ACT(lg[:], lg[:], AF.Ln, [r_lg], [r_lg])
        TS("dve", lg[:], lg[:], -1.0, None, ALU.mult, None, [r_lg], [r_lg])

        def lg_bc(l, d, h):
            c = l * 12 + d * 6 + h
            return lg[:, c:c + 1]

        def lg_pp(l, d, pr):
            c = 24 + l * 6 + d * 3 + pr
            return lg[:, c:c + 1]

        T128 = sb("T128", [128, 2, 16])
        r_t128 = R("t128")
        TS("dve", T128[:, 0, :], ip1[:, 0:16], -1.0, 128.0, ALU.add, ALU.mult, [r_cs], [r_t128])
        TS("dve", T128[:, 1, :], ip1[:, 0:16], -16.0, -128.0, ALU.add, ALU.mult, [r_cs, r_t128], [r_t128])

        ca = sb("ca", [128, 8, 2])
        r_ca = R("ca")
        DMA("sp", ca[:], cvec[:, :, :], [], [r_ca], "ld")
        ACT(ca[:], ca[:], AF.Silu, [r_ca], [r_ca])
        cab = sb("cab", [128, 8, 2], BF16)
        CP("dve", cab[:], ca[:], [r_ca], [r_ca])

        Wb = sb("Wb", [128, 8, 2560], BF16)
        WRK = [[Reg(f"W{g}_{kp}") for kp in range(4)] for g in range(8)]

        def WOK(k):
            return [WRK[g][k // 2] for g in range(4, 8)]

        def mkseq(name, nt, nslot, halo):
            q = dict(name=name, nt=nt, nslot=nslot, halo=halo)
            q["kT"] = sb(name + "kT", [128, 3, nt * 128], BF16)
            q["vret"] = sb(name + "vret", [128, nt, 384], BF16)
            q["nkT"] = sb(name + "nkT", [128, 3, nslot * 128], BF16)
            q["nva"] = sb(name + "nva", [128, nslot, 6, 65], BF16)
            q["sfst"] = sb(name + "sfst", [128, nt, 3, 64], BF16)
            q["tbst"] = sb(name + "tbst", [128, nt, 3, 64], BF16)
            q["uT"] = sb(name + "uT", [128, 2, nt * 128 + 2], BF16)
            q["gcT"] = sb(name + "gcT", [128, 2, nt * 128], BF16)
            q["runf"] = sb(name + "runf", [128, 3, 64])
            q["runb"] = sb(name + "runb", [128, 3, 64])
            for k in ("kT", "vret", "sfst", "tbst", "gcT"):
                q["r_" + k] = Rs(name + k, nt)
            q["r_nk"] = Rs(name + "nk", nslot)
            q["r_nv"] = Rs(name + "nv", nslot)
            q["r_u"] = Rs(name + "u", nt + 2)
            q["r_runf"] = R(name + "runf")
            q["r_runb"] = R(name + "runb")
            return q

        MQ = mkseq("m", NT, NT + 4, 2)
        CQ = mkseq("c", NCX, NCX, 0)
        MSET("pool", MQ["nva"][:], 1.0, MQ["r_nv"])
        MSET("pool", CQ["nva"][:], 1.0, CQ["r_nv"])
        MSET("pool", CQ["uT"][:], 0.0, CQ["r_u"])
        MSET("pool", MQ["uT"][:], 0.0, MQ["r_u"])

        biasb = sb("biasb", [128, 3, 6, 2, 128], BF16)
        r_bias = R("bias")
        gbc = sb("gbc", [128, D])
        bbc = sb("bbc", [128, D])
        r_lnp = R("lnp")
        gate_bc = sb("gate_bc", [128, D])
        r_gate = R("gate")
        modTs = [sb(f"modT{i}", [128, 24, 2]) for i in range(2)]
        r_mods = Rs("mod", 2)
        cur = {"l": 0}
        Dcomb = sb("Dcomb", [128, 6, 128])
        r_dc = R("dcomb")
        WQ = sb("WQ", [128, 2, 3, 128])
        r_wq = R("wq")
        WK = sb("WK", [128, 2, 6])
        r_wk = R("wk")
        G128 = sb("G128", [128, 2, 3])
        r_g128 = R("g128")
        gpow = sb("gpow", [128, 3, 3, 16])
        r_gpow = R("gpow")
        coef = sb("coef", [128, 2, 3, 5])
        r_coef = R("coef")
        Sin = sb("Sin", [128, 2, 3, 64], BF16)
        r_sin = R("sin")

        xt = [sb(f"xt{i}", [128, D]) for i in range(2)]
        r_xt = Rs("xt", 2)
        hT = [sb(f"hT{i}", [128, 8, 128], BF16) for i in range(2)]
        r_hT = Rs("hT", 2)
        ropet = [sb(f"ropet{i}", [128, 128]) for i in range(2)]
        r_rope = Rs("rope", 2)
        tA = sb("tA", [128, 384])
        tB = sb("tB", [128, 384])
        r_tA, r_tB = R("tA"), R("tB")
        krot = sb("krot", [128, 384], BF16)
        r_krot = R("krot")
        cvt = sb("cvt", [128, 2, 128])
        r_cvt = R("cvt")
        cvu = sb("cvu", [128, 2, 128])
        r_cvu = R("cvu")
        qT = sb("qT", [128, 3, 128], BF16)
        r_qT = R("qT")
        qsc = sb("qsc", [128, 4, 3, 128], BF16)
        r_qsc = R("qsc")
        kw = qsc[:, 0:2, :, :].rearrange("p d a b -> p d (a b)")
        r_kw = r_qsc
        nqTm = sb("nqTm", [128, 3, 2, 128], BF16)
        r_nqT = R("nqT")
        MSET("pool", nqTm[:], 0.0, [r_nqT])
        srg = sb("srg", [128, 384], BF16)
        sng = sb("sng", [128, 384], BF16)
        r_srg, r_sng = R("srg"), R("sng")
        AT = sb("AT", [128, 6, 128], BF16)
        r_AT = R("AT")
        st6 = sb("st6", [128, 8, 6])
        r_st6 = R("st6")
        r_st6a = R("st6a")
        r_st6c = R("st6c")
        r_st6b = R("st6b")
        yret = sb("yret", [128, 384], BF16)
        yna = sb("yna", [128, 384], BF16)
        r_yret, r_yna = R("yret"), R("yna")
        Eb = [sb(f"Eb{i}", [128, 8, 128], BF16) for i in range(2)]
        r_Eb = Rs("Eb", 2)
        yT = sb("yT", [128, 8, 128], BF16)
        r_yTc, r_yTr, r_yTn = R("yTc"), R("yTr"), R("yTn")
        zb = sb("zb", [128, D])
        r_zb = R("zb")
        st1 = sb("st1", [128, 8])
        r_st1 = R("st1")
        r_st1a = R("st1a")
        tmp64 = tA[:, 0:192].rearrange("p (a b) -> p a b", b=64)
        r_tmp64 = r_tA
        sacc = tB[:].rearrange("p (d a b) -> p d a b", d=2, a=3)
        r_sacc = r_tB
        tmp2 = tB[:, 0:192].rearrange("p (a b) -> p a b", b=64)
        r_tmp2 = r_tB
        totb = sb("totb", [128, 3, 64])
        r_totb = R("totb")
        ub = sb("ub", [128, 8], BF16)
        ubg = sb("ubg", [128, 4], BF16)
        r_ub, r_ubg = R("ub"), R("ubg")

        x1_r = Rs("x1d", NT)
        xc1_r = Rs("xc1d", NCX)

        def mod_setup(l):
            modT, r_mod = modTs[l], r_mods[l]
            wsrc = w_mod[l].rearrange("(k p) c -> p k c", p=128)
            stg = [(xt[0], r_xt[0]), (xt[1], r_xt[1]), (zb, r_zb), (gate_bc, r_gate)]
            bst = [(hT[0], r_hT[0]), (hT[1], r_hT[1]), (Eb[0], r_Eb[0]), (Eb[1], r_Eb[1])]
            ceng = ("act", "dve", "act", "dve")
            for fc in range(24):
                bt, rb = stg[fc % 4]
                bb, rbb = bst[fc % 4]
                b = bt[:].rearrange("p (k c) -> p k c", c=128)
                DMA("sp", b, wsrc[:, :, fc * 128:(fc + 1) * 128], [], [rb], "wm")
                CP(ceng[fc % 4], bb[:], b, [rb], [rbb])
                for k in range(8):
                    MM(PB[6][:, fc * 2:fc * 2 + 2], bb[:, k, :], cab[:, k, :], k == 0, k == 7, [rbb, r_ca], [PR[6]])
            bm = sb(f"bm{l}", [128, 24])
            r_bm = R("bm")
            DMA("sp", bm[:], bmodT[:, l, :], [], [r_bm], "ld")
            TT("dve", modT[:], PB[6][:, 0:48].rearrange("p (a b) -> p a b", b=2),
               bm[:].unsqueeze(2).broadcast_to([128, 24, 2]), ALU.add, [PR[6], r_bm], [r_mod])
            TS("dve", modT[:, 8:16, :], modT[:, 8:16, :], 1.0, None, ALU.add, None, [r_mod], [r_mod])

        def layer_setup(l):
            cur["l"] = l
            DMA("sp", gbc[:], lnp[l, 0, :, :], [], [r_lnp], "ld")
            DMA("sp", bbc[:], lnp[l, 1, :, :], [], [r_lnp], "ld")
            for h in range(6):
                ACT(tA[:, 0:128], DFt, AF.Exp, [r_cs, r_lg], [r_tA], scale=lg_bc(l, 0, h))
                ACT(tB[:, 0:128], DBt, AF.Exp, [r_cs, r_lg], [r_tB], scale=lg_bc(l, 1, h))
                TT("dve", Dcomb[:, (h % 2) * 3 + h // 2, :], tA[:, 0:128], tB[:, 0:128], ALU.add, [r_tA, r_tB], [r_dc])
            TS("dve", Dcomb[:], Dcomb[:], 0.125, None, ALU.mult, None, [r_dc], [r_dc])
            for pr in range(3):
                ACT(WQ[:, 0, pr, :], ip1, AF.Exp, [r_cs, r_lg], [r_wq], scale=lg_pp(l, 0, pr))
                ACT(WQ[:, 1, pr, :], rev, AF.Exp, [r_cs, r_lg], [r_wq], scale=lg_pp(l, 1, pr))
                ACT(gpow[:, 0, pr, :], T128[:, 0, :], AF.Exp, [r_t128, r_lg], [r_gpow], scale=lg_pp(l, 0, pr))
                ACT(gpow[:, 1, pr, :], T128[:, 1, :], AF.Exp, [r_t128, r_lg], [r_gpow], scale=lg_pp(l, 1, pr))
                ACT(gpow[:, 2, pr, :], T128[:, 0, :], AF.Exp, [r_t128, r_lg], [r_gpow], scale=lg_pp(l, 1, pr))
                for d in range(2):
                    ACT(coef[:, d, pr, :], rk_t[:, d * 5:d * 5 + 5], AF.Exp, [r_misc, r_lg], [r_coef],
                        scale=lg_pp(l, d, pr))
                    TT("dve", coef[:, d, pr, :], coef[:, d, pr, :], rk_t[:, 10 + d * 5:15 + d * 5], ALU.mult,
                       [r_coef, r_misc], [r_coef])
            for d in range(2):
                ACT(G128[:, d, :], lg[:, 24 + l * 6 + d * 3:24 + l * 6 + d * 3 + 3], AF.Exp, [r_lg], [r_g128],
                    scale=128.0)
                TS("dve", WK[:, d, :], lg[:, l * 12 + d * 6:l * 12 + d * 6 + 6], jrev if d == 0 else jpos, None,
                   ALU.mult, None, [r_lg, r_cs], [r_wk])
            ACT(WK[:], WK[:], AF.Exp, [r_wk], [r_wk])
            TS("dve", WK[:], WK[:], 0.125, None, ALU.mult, None, [r_wk], [r_wk])

        def build_gate(m):
            for k in range(8):
                TS("dve", tB[:, 0:128], identf, modTs[cur["l"]][:, 16 + k, m:m + 1], None, ALU.mult, None,
                   [r_cs, r_mods[cur["l"]]], [r_tB])
                MM(PB[5][:, (k % 4) * 128:(k % 4 + 1) * 128], onesf[:], tB[:, 0:128], True, True, [r_ones, r_tB], [PR[5]])
                if k % 4 == 3:
                    CP("act", gate_bc[:, (k // 4) * 512:(k // 4 + 1) * 512], PB[5][:, :], [PR[5]], [r_gate])

        W1_GROUPS = ((C_RK, 384, 0, 0), (C_RV, 384, 384, 1), (C_NV, 384, 768, 2), (C_NK, 384, 1152, 3),
                     (C_CH, 256, 1536, 4), (C_CCG, 256, 1792, 5), (C_CB, 256, 2048, 6), (C_CZ, 256, 2304, 7))

        def load_w1(l, part):
            src = w_in[l].rearrange("(k p) c -> p k c", p=128)
            for (c0, n, o, g) in (W1_GROUPS[0:4] if part == 0 else W1_GROUPS[4:8]):
                DMA("pool", Wb[:, :, o:o + n], src[:, :, c0:c0 + n], [], WRK[g], "w")

        def load_w2(l):
            src = w_in[l].rearrange("(k p) c -> p k c", p=128)
            if l == 0:
                for (c0, n, o, g) in ((C_RQ, 384, 0, 0), (C_NQ, 384, 1152, 3), (C_RG, 384, 384, 1), (C_NG, 384, 768, 2)):
                    DMA("pool", Wb[:, :, o:o + n], src[:, :, c0:c0 + n], [], WRK[g], "w")
                srco0 = w_out[l].rearrange("(k p) c -> p k c", p=128)
                DMA("pool", Wb[:, :, 1536:2560], srco0[:, :, :], [], [r for g in range(4, 8) for r in WRK[g]], "w")
                return
            stg = [(xt[0], r_xt[0]), (xt[1], r_xt[1]), (zb, r_zb), (gate_bc, r_gate)]
            ceng = ("pool", "act", "dve", "pool")
            i = 0
            for (c0, n, o, g) in ((C_RQ, 384, 0, 0), (C_NQ, 384, 1152, 3), (C_RG, 384, 384, 1), (C_NG, 384, 768, 2)):
                for kp in range(4):
                    bt, rb = stg[i % 4]
                    v = bt[:, 0:768].rearrange("p (k c) -> p k c", c=384)
                    DMA("sp", v, src[:, 2 * kp:2 * kp + 2, c0:c0 + n], [], [rb], "w")
                    CP(ceng[i % 4], Wb[:, 2 * kp:2 * kp + 2, o:o + n], v, [rb], [WRK[g][kp]])
                    i += 1
            srco = w_out[l].rearrange("(k p) c -> p k c", p=128)
            for k in range(8):
                bt, rb = stg[i % 4]
                DMA("sp", bt[:], srco[:, k, :], [], [rb], "w")
                CP(ceng[i % 4], Wb[:, k, 1536:2560], bt[:], [rb], WOK(k))
                i += 1

        def LX(src, src_r, t, m, l=None, rope_on=False, bias_on=False):
            s = t % 2
            rd = [src_r[t]] if src_r is not None else []
            DMA("sp", xt[s][:], src[t * 128:(t + 1) * 128, :], rd, [r_xt[s]], "x")
            if rope_on:
                DMA("sp", ropet[s][:], rope[t, :, :], [], [r_rope[s]], "x")
            if bias_on:
                DMA("sp", biasb[:].rearrange("p a b h i -> p (a b h i)"), nabias[l, t, :, :], [], [r_bias], "x")
            for k in range(8):
                TR(PB[k // 4][:, (k % 4) * 128:(k % 4 + 1) * 128], xt[s][:, k * 128:(k + 1) * 128], identf,
                   [r_xt[s], r_cs], [PR[k // 4]])
            for k in range(8):
                ACT(hT[s][:, k, :], PB[k // 4][:, (k % 4) * 128:(k % 4 + 1) * 128], AF.Identity,
                    [PR[k // 4], r_mods[cur["l"]]], [r_hT[s]], bias=modTs[cur["l"]][:, k, m:m + 1],
                    scale=modTs[cur["l"]][:, 8 + k, m:m + 1])

        def do_rope(src_ps, src_r, s_rope, dst, dst_r):
            v = src_ps.rearrange("p (h d) -> p h d", d=64)
            cosb = ropet[s_rope][:, 0:64].unsqueeze(1).broadcast_to([128, 6, 64])
            TT("dve", tA[:].rearrange("p (h d) -> p h d", d=64), v, cosb, ALU.mult, [src_r, r_rope[s_rope]], [r_tA])
            v5 = src_ps.rearrange("p (h r a f) -> p h r a f", r=2, a=2, f=16)
            tB5 = tB[:].rearrange("p (h r a f) -> p h r a f", r=2, a=2, f=16)
            sn = ropet[s_rope][:, 64:128].rearrange("p (r a f) -> p r a f", r=2, a=2)
            for a in range(2):
                TT("dve", tB5[:, :, :, a, :], v5[:, :, :, 1 - a, :],
                   sn[:, :, a, :].unsqueeze(1).broadcast_to([128, 6, 2, 16]), ALU.mult,
                   [src_r, r_rope[s_rope]], [r_tB])
            TT("dve", dst, tA[:], tB[:], ALU.add, [r_tA, r_tB], [dst_r])

        def P1(q, t, src, src_r, m, use_rope, hoist=None):
            nt = q["nt"]
            slot = t + q["halo"]
            s = t % 2
            hs, rhs_ = hT[s], r_hT[s]
            for (bank, wo, g) in ((2, 0, 0), (3, 384, 1), (4, 768, 2)):
                for k in range(8):
                    MM(PB[bank][:, 0:384], hs[:, k, :], Wb[:, k, wo:wo + 384], k == 0, k == 7, [rhs_, WRK[g][k // 2]],
                       [PR[bank]])
            for c in range(3):
                for k in range(8):
                    MM(PB[5][:, c * 128:(c + 1) * 128], Wb[:, k, 1152 + c * 128:1152 + (c + 1) * 128], hs[:, k, :],
                       k == 0, k == 7, [rhs_, WRK[3][k // 2]], [PR[5]])
            def conv_half(bank, groups):
                for gi, (wo, g) in enumerate(groups):
                    for c in range(2):
                        blk = gi * 2 + c
                        for k in range(8):
                            MM(PB[bank][:, blk * 128:(blk + 1) * 128], Wb[:, k, wo + c * 128:wo + (c + 1) * 128],
                               hs[:, k, :], k == 0, k == 7, [rhs_, WRK[g][k // 2]], [PR[bank]])

            conv_half(6, ((1536, 4), (1792, 5)))
            v3 = PB[3][:, 0:384].rearrange("p (h e) -> p h e", e=64)
            S.op("dve", lambda h_: h_.reduce_sum(out=st6[:, 7, :], in_=v3, axis=AX.X), [PR[3]], [r_st6c])
            TS("dve", st6[:, 7, :], st6[:, 7, :], 1.0 / 64.0, None, ALU.mult, None, [r_st6c], [r_st6c])
            TT("dve", q["vret"][:, t, :].rearrange("p (h e) -> p h e", e=64), v3,
               st6[:, 7, :].unsqueeze(2).broadcast_to([128, 6, 64]), ALU.subtract, [PR[3], r_st6c], [q["r_vret"][t]])
            CP("act", q["nva"][:, slot, :, 0:64], PB[4][:, 0:384].rearrange("p (h e) -> p h e", e=64), [PR[4]],
               [q["r_nv"][slot]])
            conv_half(4, ((2048, 6), (2304, 7)))
            if hoist is not None:
                hoist()
            if use_rope:
                do_rope(PB[2][:, 0:384], PR[2], s, krot[:], r_krot)
            else:
                CP("dve", krot[:], PB[2][:, 0:384], [PR[2]], [r_krot])
            for d in range(2):
                TT("pool", kw[:, d, :].rearrange("p (h e) -> p h e", e=64), krot[:].rearrange("p (h e) -> p h e", e=64),
                   WK[:, d, :].unsqueeze(2).broadcast_to([128, 6, 64]), ALU.mult, [r_krot, r_wk], [r_kw])
            for c in range(3):
                TR(PB[7][:, c * 128:(c + 1) * 128], krot[:, c * 128:(c + 1) * 128], identb[:], [r_krot, r_idb], [PR[7]])
            CP("dve", q["kT"][:, :, t * 128:(t + 1) * 128], PB[7][:, 0:384].rearrange("p (c i) -> p c i", i=128),
               [PR[7]], [q["r_kT"][t]])
            CP("dve", q["nkT"][:, :, slot * 128:(slot + 1) * 128], PB[5][:, 0:384].rearrange("p (c i) -> p c i", i=128),
               [PR[5]], [q["r_nk"][slot]])
            CP("act", cvt[:], PB[6][:, 256:512].rearrange("p (c i) -> p c i", i=128), [PR[6]], [r_cvt])
            TT("dve", q["uT"][:, :, 1 + t * 128:1 + (t + 1) * 128], PB[6][:, 0:256].rearrange("p (c i) -> p c i", i=128),
               cvt[:], ALU.mult, [PR[6], r_cvt], [q["r_u"][t + 1]])
            ACT(cvu[:], PB[4][:, 256:512].rearrange("p (c i) -> p c i", i=128), AF.Silu, [PR[4]], [r_cvu])
            TT("dve", q["gcT"][:, :, t * 128:(t + 1) * 128], PB[4][:, 0:256].rearrange("p (c i) -> p c i", i=128),
               cvu[:], ALU.mult, [PR[4], r_cvu], [q["r_gcT"][t]])
            for d in range(2):
                for pr in range(3):
                    MM(PB[2 + d][:, pr * 128:(pr + 1) * 128], kw[:, d, pr * 128:(pr + 1) * 128],
                       q["vret"][:, t, pr * 128:(pr + 1) * 128], True, True, [r_kw, q["r_vret"][t]], [PR[2 + d]])

            def diag(bank, hp):
                rows = slice(hp * 64, hp * 64 + 64)
                return PB[bank][rows, 0:384].rearrange("p (a b) -> p a b", b=128)[:, :, hp * 64:(hp + 1) * 64]

            CP("pool", q["sfst"][:, t, :, :], q["runf"][:], [q["r_runf"]], [q["r_sfst"][t]])
            TT("pool", tmp64, q["runf"][:], G128[:, 0, :].unsqueeze(2).broadcast_to([128, 3, 64]), ALU.mult,
               [q["r_runf"], r_g128], [r_tmp64])
            for hp in range(2):
                rows = slice(hp * 64, hp * 64 + 64)
                TT("dve", q["runf"][rows, :, :], diag(2, hp), tmp64[rows, :, :], ALU.add, [PR[2], r_tmp64], [q["r_runf"]])
                CP("act", q["tbst"][rows, t, :, :], diag(3, hp), [PR[3]], [q["r_tbst"][t]])
            if use_rope:
                for hp in range(2):
                    rows = slice(hp * 64, hp * 64 + 64)
                    TT("dve", tmp2[rows, :, :], diag(3, hp), gpow[rows, 2, :, t:t + 1].broadcast_to([64, 3, 64]), ALU.mult,
                       [PR[3], r_gpow], [r_tmp2])
                TT("pool", totb[:], totb[:], tmp2, ALU.add, [r_totb, r_tmp2], [r_totb])

        def P1_finish(q):
            nt = q["nt"]
            for t in range(nt - 1, -1, -1):
                CP("dve", tmp64, q["tbst"][:, t, :, :], [q["r_tbst"][t]], [r_tmp64])
                CP("dve", q["tbst"][:, t, :, :], q["runb"][:], [q["r_runb"]], [q["r_tbst"][t]])
                for pr in range(3):
                    STT(q["runb"][:, pr, :], q["runb"][:, pr, :], G128[:, 1, pr:pr + 1], tmp64[:, pr, :], ALU.mult,
                        ALU.add, [q["r_runb"], r_g128, r_tmp64], [q["r_runb"]])

        def P2(q, t, dst, dst_r, l, is_main, hoist=None, hooks=None, late_ret=False):
            nt = q["nt"]
            s = t % 2
            hs, rhs_ = hT[s], r_hT[s]
            def tok_proj(bank, wo, g):
                for k in range(8):
                    MM(PB[bank][:, 0:384], hs[:, k, :], Wb[:, k, wo:wo + 384], k == 0, k == 7, [rhs_, WRK[g][k // 2]],
                       [PR[bank]])

            tok_proj(2, 0, 0)
            for c in range(3):
                for k in range(8):
                    MM(PB[5][:, c * 128:(c + 1) * 128], Wb[:, k, 1152 + c * 128:1152 + (c + 1) * 128], hs[:, k, :],
                       k == 0, k == 7, [rhs_, WRK[3][k // 2]], [PR[5]])
            tok_proj(3, 384, 1)
            tok_proj(4, 768, 2)
            if hooks is not None and "a2" in hooks:
                hooks["a2"]()
            for hp in range(2):
                rows = slice(hp * 64, hp * 64 + 64)
                ACT(nqTm[rows, :, hp, :], PB[5][rows, 0:384].rearrange("p (c i) -> p c i", i=128), AF.Identity, [PR[5]],
                    [r_nqT], scale=0.125)
            if is_main:
                do_rope(PB[2][:, 0:384], PR[2], s, krot[:], r_krot)
            else:
                CP("dve", krot[:], PB[2][:, 0:384], [PR[2]], [r_krot])
            ACT(srg[:], PB[3][:, 0:384], AF.Silu, [PR[3]], [r_srg])
            ACT(sng[:], PB[4][:, 0:384], AF.Silu, [PR[4]], [r_sng])

            if is_main:
                dlo = -3 if t == NT - 1 else -2
                dhi = 3 if t == 0 else 2
                blocks = [(q, t + dt + 2, dt - dlo) for dt in range(dlo, dhi + 1)] + [(CQ, c, None) for c in range(NCX)]
            else:
                blocks = [(CQ, c, None) for c in range(NCX)]
            nb = len(blocks)

            def unit_blocks(half):
                return list(enumerate(blocks))[0:4] if half == 0 else list(enumerate(blocks))[4:nb]

            def N_S(pr, half):
                banks = (5, 6) if half == 0 else (0, 1)
                for j, (bi, (kq, slot, bidx)) in enumerate(unit_blocks(half)):
                    bank = banks[j // 2]
                    col = (j % 2) * 256
                    last = bidx is None
                    MM(PB[bank][:, col:col + 256], kq["nkT"][:, pr, slot * 128:(slot + 1) * 128], nqTm[:, pr, :, :],
                       True, last, [kq["r_nk"][slot], r_nqT], [PR[bank]])
                    if not last:
                        MM(PB[bank][:, col:col + 256], identb[:], biasb[:, pr, bidx, :, :], False, True, [r_idb, r_bias],
                           [PR[bank]])

            def N_E(pr, half):
                banks = (5, 6) if half == 0 else (0, 1)
                ub_ = unit_blocks(half)
                e4 = Eb[half][:].rearrange("p (b h) i -> p b h i", h=2)
                for jb in range(2):
                    nblk = min(2, len(ub_) - jb * 2)
                    if nblk <= 0:
                        continue
                    ACT(e4[:, jb * 2:jb * 2 + nblk, :, :],
                        PB[banks[jb]][:, 0:nblk * 256].rearrange("p (b h i) -> p b h i", h=2, i=128), AF.Exp,
                        [PR[banks[jb]]], [r_Eb[half]])

            def N_PV(pr):
                for hp in range(2):
                    h = 2 * pr + hp
                    for bi, (kq, slot, bidx) in enumerate(blocks):
                        half, j = (0, bi) if bi < 4 else (1, bi - 4)
                        e4 = Eb[half][:].rearrange("p (b h) i -> p b h i", h=2)
                        MM(PB[3][:, h * 65:(h + 1) * 65], e4[:, j, hp, :], kq["nva"][:, slot, h, :], bi == 0, bi == nb - 1,
                           [r_Eb[half], kq["r_nv"][slot]], [PR[3]])

            def N_F():
                ona = PB[3][:, 0:390].rearrange("p (h e) -> p h e", e=65)
                S.op("dve", lambda h_: h_.reciprocal(out=st6[:, 6, :], in_=ona[:, :, 64]), [PR[3]], [r_st6b])
                tA3 = tA[:].rearrange("p (h e) -> p h e", e=64)
                TT("dve", tA3, ona[:, :, 0:64], st6[:, 6, :].unsqueeze(2).broadcast_to([128, 6, 64]), ALU.mult,
                   [PR[3], r_st6b], [r_tA])
                TT("pool", yna[:], tA[:], sng[:], ALU.mult, [r_tA, r_sng], [r_yna])

            def R1():
                for c in range(3):
                    TR(PB[7][:, c * 128:(c + 1) * 128], krot[:, c * 128:(c + 1) * 128], identb[:], [r_krot, r_idb],
                       [PR[7]])
                CP("dve", qT[:], PB[7][:, 0:384].rearrange("p (c i) -> p c i", i=128), [PR[7]], [r_qT])
                for d in range(2):
                    TT("pool", qsc[:, d, :, :], qT[:], WQ[:, d, :, :], ALU.mult, [r_qT, r_wq], [r_qsc])
                if is_main:
                    for d in range(2):
                        TT("pool", qsc[:, 2 + d, :, :], qsc[:, d, :, :], gpow[:, d, :, t:t + 1].broadcast_to([128, 3, 128]),
                           ALU.mult, [r_qsc, r_gpow], [r_qsc])

            def R2():
                for h in range(6):
                    pr, hp = h // 2, h % 2
                    rows = slice(hp * 64, hp * 64 + 64)
                    MM(PB[hp][:, pr * 128:(pr + 1) * 128], q["kT"][rows, pr, t * 128:(t + 1) * 128], qT[rows, pr, :],
                       True, True, [q["r_kT"][t], r_qT], [PR[hp]], tp=(hp * 64, 0))
                for hp in range(2):
                    TT("dve", AT[:, hp * 3:hp * 3 + 3, :], PB[hp][:, 0:384].rearrange("p (h i) -> p h i", i=128),
                       Dcomb[:, hp * 3:hp * 3 + 3, :], ALU.mult, [PR[hp], r_dc], [r_AT])

            def R3():
                for h in range(6):
                    pr, hp = h // 2, h % 2
                    rows = slice(hp * 64, hp * 64 + 64)
                    o = PB[2][:, h * 64:(h + 1) * 64]
                    tp = (hp * 64, 0)
                    MM(o, AT[:, hp * 3 + pr, :], q["vret"][:, t, h * 64:(h + 1) * 64], True, False,
                       [r_AT, q["r_vret"][t]], [PR[2]])
                    MM(o, qsc[rows, 0, pr, :], q["sfst"][rows, t, pr, :], False, False, [r_qsc, q["r_sfst"][t]], [PR[2]],
                       tp=tp)
                    MM(o, qsc[rows, 1, pr, :], q["tbst"][rows, t, pr, :], False, not is_main, [r_qsc, q["r_tbst"][t]],
                       [PR[2]], tp=tp)
                    if is_main:
                        MM(o, qsc[rows, 2, pr, :], Sin[rows, 0, pr, :], False, False, [r_qsc, r_sin], [PR[2]], tp=tp)
                        MM(o, qsc[rows, 3, pr, :], Sin[rows, 1, pr, :], False, True, [r_qsc, r_sin], [PR[2]], tp=tp)

            def R4():
                o3 = PB[2][:, 0:384].rearrange("p (h e) -> p h e", e=64)
                ACT(tB[:], PB[2][:, 0:384], AF.Square, [PR[2]], [r_tB])
                S.op("dve", lambda h_: h_.reduce_sum(out=st6[:, 1, :], in_=tB[:].rearrange("p (h e) -> p h e", e=64),
                                                     axis=AX.X), [r_tB], [r_st6])
                ACT(st6[:, 4, :], st6[:, 1, :], AF.Sqrt, [r_st6, r_eps], [r_st6], bias=epsc[:], scale=1.0 / 64.0)
                S.op("dve", lambda h_: h_.reciprocal(out=st6[:, 5, :], in_=st6[:, 4, :]), [r_st6], [r_st6])
                tB3 = tB[:].rearrange("p (h e) -> p h e", e=64)
                TT("dve", tB3, o3, st6[:, 5, :].unsqueeze(2).broadcast_to([128, 6, 64]), ALU.mult, [PR[2], r_st6], [r_tB])
                TT("pool", yret[:], tB[:], srg[:], ALU.mult, [r_tB, r_srg], [r_yret])

            def C():
                uT_, gc_ = q["uT"], q["gcT"]
                ru = [q["r_u"][t], q["r_u"][t + 1], q["r_u"][t + 2]]
                b0 = t * 128

                def wb(j):
                    return cp_t[:, l, :, j:j + 1].broadcast_to([128, 2, 128])

                TT("pool", cvt[:], uT_[:, :, b0:b0 + 128], wb(0), ALU.mult, ru + [r_misc], [r_cvt])
                TT("pool", cvu[:], uT_[:, :, b0 + 1:b0 + 129], wb(1), ALU.mult, ru + [r_misc], [r_cvu])
                TT("pool", cvt[:], cvt[:], cvu[:], ALU.add, [r_cvt, r_cvu], [r_cvt])
                TT("pool", cvu[:], uT_[:, :, b0 + 2:b0 + 130], wb(2), ALU.mult, ru + [r_misc], [r_cvu])
                TT("pool", cvt[:], cvt[:], cvu[:], ALU.add, [r_cvt, r_cvu], [r_cvt])
                TT("pool", cvt[:], cvt[:], wb(3), ALU.add, [r_cvt, r_misc], [r_cvt])
                TT("pool", yT[:, 0:2, :], cvt[:], gc_[:, :, t * 128:(t + 1) * 128], ALU.mult, [r_cvt, q["r_gcT"][t]], [r_yTc])

            def Y_ret():
                for c in range(3):
                    TR(PB[7][:, c * 128:(c + 1) * 128], yret[:, c * 128:(c + 1) * 128], identb[:], [r_yret, r_idb],
                       [PR[7]])
                CP("act", yT[:, 2:5, :], PB[7][:, 0:384].rearrange("p (c i) -> p c i", i=128), [PR[7]], [r_yTr])

            def Y_na():
                for c in range(3):
                    TR(PB[7][:, (3 + c) * 128:(4 + c) * 128], yna[:, c * 128:(c + 1) * 128], identb[:], [r_yna, r_idb],
                       [PR[7]])
                CP("act", yT[:, 5:8, :], PB[7][:, 384:768].rearrange("p (c i) -> p c i", i=128), [PR[7]], [r_yTn])

            def O(ks, first, last):
                for j in range(2):
                    for k in ks:
                        rk_ = r_yTc if k < 2 else (r_yTr if k < 5 else r_yTn)
                        MM(PB[5 + j][:, :], yT[:, k, :], Wb[:, k, 1536 + j * 512:1536 + (j + 1) * 512],
                           first and k == ks[0], last and k == ks[-1], [rk_] + WOK(k), [PR[5 + j]])

            def L():
                for j in range(2):
                    TT("dve", zb[:, j * 512:(j + 1) * 512], PB[5 + j][:, :], gate_bc[:, j * 512:(j + 1) * 512], ALU.mult,
                       [PR[5 + j], r_gate], [r_zb])
                STT(zb[:], xt[s][:], ALPHA, zb[:], ALU.mult, ALU.add, [r_xt[s], r_zb], [r_zb])
                S.op("dve", lambda h_: h_.reduce_sum(out=st1[:, 0:1], in_=zb[:], axis=AX.X), [r_zb], [r_st1a])
                MSET("pool", st1[:, 1:3], 0.0, [r_st1])
                ACT(PB[4][:, :], zb[:, 0:512], AF.Square, [r_zb], [PR[4], r_st1], accum=st1[:, 1:2])
                ACT(PB[4][:, :], zb[:, 512:1024], AF.Square, [r_zb], [PR[4], r_st1], accum=st1[:, 2:3])
                TS("pool", st1[:, 0:1], st1[:, 0:1], 1.0 / D, None, ALU.mult, None, [r_st1a], [r_st1a])
                TT("pool", st1[:, 1:2], st1[:, 1:2], st1[:, 2:3], ALU.add, [r_st1], [r_st1])
                TT("pool", st1[:, 2:3], st1[:, 0:1], st1[:, 0:1], ALU.mult, [r_st1a, r_st1], [r_st1])
                TS("pool", st1[:, 1:2], st1[:, 1:2], 1.0 / D, None, ALU.mult, None, [r_st1], [r_st1])
                TT("pool", st1[:, 3:4], st1[:, 1:2], st1[:, 2:3], ALU.subtract, [r_st1], [r_st1])
                ACT(st1[:, 4:5], st1[:, 3:4], AF.Sqrt, [r_st1, r_eps], [r_st1], bias=epsc[:], scale=1.0)
                S.op("dve", lambda h_: h_.reciprocal(out=st1[:, 5:6], in_=st1[:, 4:5]), [r_st1], [r_st1])
                STT(st1[:, 6:7], st1[:, 0:1], -1.0, st1[:, 5:6], ALU.mult, ALU.mult, [r_st1, r_st1a], [r_st1])
                ACT(zb[:], zb[:], AF.Identity, [r_zb, r_st1], [r_zb], bias=st1[:, 6:7], scale=st1[:, 5:6])
                TT("pool", zb[:], zb[:], gbc[:], ALU.mult, [r_zb, r_lnp], [r_zb])
                TT("pool", zb[:], zb[:], bbc[:], ALU.add, [r_zb, r_lnp], [r_zb])
                DMA("sp", dst[t * 128:(t + 1) * 128, :], zb[:], [r_zb], [dst_r[t]] if dst_r is not None else [], "st")

            has_b = nb > 4
            N_S(0, 0)
            R1()
            N_E(0, 0)
            R2()
            if has_b:
                N_S(0, 1)
                N_E(0, 1)
            N_PV(0)
            N_S(1, 0)
            C()
            if not late_ret:
                R3()
            N_E(1, 0)
            if has_b:
                N_S(1, 1)
                N_E(1, 1)
            N_PV(1)
            if not late_ret:
                R4()
            N_S(2, 0)
            N_E(2, 0)
            if has_b:
                N_S(2, 1)
            if not late_ret:
                Y_ret()
            if has_b:
                N_E(2, 1)
            if not late_ret:
                O([0, 1, 2, 3, 4], True, False)
            if hoist is not None:
                hoist()
            N_PV(2)
            if hooks is not None and "post_nbr" in hooks:
                hooks["post_nbr"]()
            if late_ret:
                if hooks is not None and "pre_ret" in hooks:
                    hooks["pre_ret"]()
                R3()
                R4()
                Y_ret()
                O([0, 1, 2, 3, 4], True, False)
            N_F()
            Y_na()
            O([5, 6, 7], False, True)
            if hooks is not None and "o" in hooks:
                hooks["o"]()
            L()

        def exchange_gathers(q, gb, r_gb):
            nk, nv, uT_ = q["nkT"], q["nva"], q["uT"]

            def gather(o, col0, n, which, wr, shape3=None):
                cands = (0, 1, 2) if which == 0 else (1, 2, 3)
                for j, k in enumerate(cands):
                    e = Eb[j % 2]
                    re = r_Eb[j % 2]
                    stg = e[:].rearrange("p a b -> p (a b)")[:, 0:n]
                    DMA("sp", stg, gb[k * 128:(k + 1) * 128, col0:col0 + n], [r_gb], [re], "ex")
                    sv = stg if shape3 is None else stg.rearrange("p (a b) -> p a b", b=shape3)
                    sc = sel_t[:, which * 4 + k:which * 4 + k + 1]
                    if j == 0:
                        TS("dve", o, sv, sc, None, ALU.mult, None, [re, r_misc], wr)
                    else:
                        STT(o, sv, sc, o, ALU.mult, ALU.add, [re, r_misc] + wr, wr)

            items = [
                lambda: gather(nk[:, :, 0:256], OFF_NKT_BOT, 768, 0, [q["r_nk"][0], q["r_nk"][1]], 256),
                lambda: gather(nk[:, :, (NT + 2) * 128:(NT + 4) * 128], OFF_NKT_TOP, 768, 1,
                               [q["r_nk"][NT + 2], q["r_nk"][NT + 3]], 256),
                lambda: gather(nv[:, 0:2, :, :].rearrange("p a h e -> p (a h e)"), OFF_NVA_BOT, 780, 0,
                               [q["r_nv"][0], q["r_nv"][1]]),
                lambda: gather(nv[:, NT + 2:NT + 4, :, :].rearrange("p a h e -> p (a h e)"), OFF_NVA_TOP, 780, 1,
                               [q["r_nv"][NT + 2], q["r_nv"][NT + 3]]),
                lambda: gather(uT_[:, :, 0], OFF_U_LAST, 2, 0, [q["r_u"][0]]),
                lambda: gather(uT_[:, :, NT * 128 + 1], OFF_U_FIRST, 2, 1, [q["r_u"][NT + 1]]),
            ]
            return items

        def exchange(l):
            q = MQ
            pb, gb = packB[l], gathB[l]
            r_pb, r_gb = R("pb"), R("gb")
            pf = pb[:, OFF_ST:OFF_ST + 768].bitcast(F32)
            DMA("sp", pf[:, 0:192].rearrange("p (a b) -> p a b", b=64), q["runf"][:], [q["r_runf"]], [r_pb], "ex")
            DMA("sp", pf[:, 192:384].rearrange("p (a b) -> p a b", b=64), totb[:], [r_totb], [r_pb], "ex")
            nk, nv, uT_ = q["nkT"], q["nva"], q["uT"]
            DMA("sp", pb[:, OFF_NKT_TOP:OFF_NKT_TOP + 768].rearrange("p (a b) -> p a b", b=256), nk[:, :, 2 * 128:4 * 128],
                [q["r_nk"][2], q["r_nk"][3]], [r_pb], "ex")
            DMA("sp", pb[:, OFF_NKT_BOT:OFF_NKT_BOT + 768].rearrange("p (a b) -> p a b", b=256),
                nk[:, :, (NT) * 128:(NT + 2) * 128], [q["r_nk"][NT], q["r_nk"][NT + 1]], [r_pb], "ex")
            DMA("sp", pb[:, OFF_NVA_TOP:OFF_NVA_TOP + 780], nv[:, 2:4, :, :].rearrange("p a h e -> p (a h e)"),
                [q["r_nv"][2], q["r_nv"][3]], [r_pb], "ex")
            DMA("sp", pb[:, OFF_NVA_BOT:OFF_NVA_BOT + 780], nv[:, NT:NT + 2, :, :].rearrange("p a h e -> p (a h e)"),
                [q["r_nv"][NT], q["r_nv"][NT + 1]], [r_pb], "ex")
            MSET("dve", ub[:, 4:8], 0.0, [r_ub])
            CP("dve", ub[:, 0:2], uT_[:, :, 1], [q["r_u"][1]], [r_ub])
            CP("dve", ub[:, 2:4], uT_[:, :, NT * 128], [q["r_u"][NT]], [r_ub])
            DMA("sp", pb[:, OFF_U_FIRST:OFF_U_FIRST + 8], ub[:], [r_ub], [r_pb], "ex")
            S.op("pool", lambda h: h.collective_compute("AllGather", ALU.bypass, replica_groups=RG, ins=[pb[:, :]],
                                                        outs=[gb[:, :]]), [r_pb], [r_gb], dma="cc", inc=1)
            return dict(gb=gb, r_gb=r_gb)

        def exchange_recv_halo(l, X):
            return exchange_gathers(MQ, X["gb"], X["r_gb"])

        def exchange_recv(l, X):
            gb, r_gb = X["gb"], X["r_gb"]
            stgs = [(tA[:], r_tA),
                    (Eb[0][:].rearrange("p a b -> p (a b)")[:, 0:768].bitcast(F32), r_Eb[0]),
                    (Eb[1][:].rearrange("p a b -> p (a b)")[:, 0:768].bitcast(F32), r_Eb[1])]
            for k in range(4):
                stg, rs = stgs[k % 3]
                DMA("sp", stg, gb[k * 128:(k + 1) * 128, OFF_ST:OFF_ST + 768].bitcast(F32), [r_gb], [rs], "ex")
                v4 = stg.rearrange("p (d a b) -> p d a b", d=2, a=3)
                cb_ = coef[:, :, :, k:k + 1].broadcast_to([128, 2, 3, 64])
                if k == 0:
                    TT("dve", sacc, v4, cb_, ALU.mult, [rs, r_coef], [r_sacc])
                else:
                    TT("dve", v4, v4, cb_, ALU.mult, [rs, r_coef], [rs])
                    TT("dve", sacc, sacc, v4, ALU.add, [rs, r_sacc], [r_sacc])
            t4 = tA[:].rearrange("p (d a b) -> p d a b", d=2, a=3)
            for d in range(2):
                s0 = CQ["runf"] if d == 0 else CQ["runb"]
                r_s0 = CQ["r_runf"] if d == 0 else CQ["r_runb"]
                TT("dve", t4[:, d, :, :], s0[:], coef[:, d, :, 4:5].broadcast_to([128, 3, 64]), ALU.mult,
                   [r_s0, r_coef, r_tA], [r_tA])
            TT("dve", sacc, sacc, t4, ALU.add, [r_tA, r_sacc], [r_sacc])
            CP("dve", Sin[:], sacc, [r_sacc], [r_sin])

        def reset_run(q):
            if q is MQ:
                MSET("pool", totb[:], 0.0, [r_totb])
            MSET("pool", q["runf"][:], 0.0, [q["r_runf"]])
            MSET("pool", q["runb"][:], 0.0, [q["r_runb"]])

        try:
            for l in range(2):
                if l == 0:
                    load_w1(0, 0)
                    load_w1(0, 1)
                ck(10 * l + 0)
                if l == 0:
                    mod_setup(0)
                    mod_setup(1)
                layer_setup(l)
                ck(10 * l + 1)
                csrc, csrc_r = (ctx_in, None) if l == 0 else (xc1, xc1_r)
                msrc, msrc_r = (x_in, None) if l == 0 else (x1, x1_r)
                mdst, mdst_r = (x1, x1_r) if l == 0 else (out, None)
                reset_run(CQ)
                reset_run(MQ)
                LX(msrc, msrc_r, 0, 0, l, rope_on=True)
                for t in range(NT):
                    hz = (lambda t=t: LX(msrc, msrc_r, t + 1, 0, l, rope_on=True)) if t + 1 < NT else \
                        (lambda: LX(csrc, csrc_r, 0, 1))
                    P1(MQ, t, msrc, msrc_r, 0, True, hoist=hz)
                ck(10 * l + 2)
                X = exchange(l)
                for t in range(NCX):
                    hz = (lambda t=t: LX(csrc, csrc_r, t + 1, 1)) if t + 1 < NCX else None
                    P1(CQ, t, csrc, csrc_r, 1, False, hoist=hz)
                ck(10 * l + 3)
                P1_finish(CQ)
                if l == 1:
                    P1_finish(MQ)
                load_w2(l)
                ck(10 * l + 4)
                ck(10 * l + 5)
                if l == 0:
                    build_gate(1)
                    ck(5.1)
                    LX(csrc, csrc_r, 0, 1)
                    for t in range(NCX):
                        hz = (lambda t=t: LX(csrc, csrc_r, t + 1, 1)) if t + 1 < NCX else None
                        P2(CQ, t, xc1, xc1_r, l, False, hoist=hz)
                if l == 0:
                    P1_finish(MQ)
                ck(10 * l + 6)
                build_gate(0)
                order = [2, 3, 4, 5, 6, 7, 8, 9, 10, 11, 12, 13, 0, 1, 14, 15]
                LX(msrc, msrc_r, order[0], 0, l, rope_on=True, bias_on=True)
                halo_items = exchange_recv_halo(l, X)
                for i, t in enumerate(order):
                    hz = (lambda tn=order[i + 1]: LX(msrc, msrc_r, tn, 0, l, rope_on=True, bias_on=True)) \
                        if i + 1 < NT else None
                    hk = None
                    if i == 0:
                        hk = {"pre_ret": (lambda: exchange_recv(l, X))}
                    if 1 <= i <= len(halo_items):
                        hk = {"post_nbr": halo_items[i - 1]}
                    if l == 0 and i == NT - 1:
                        hk = {"a2": (lambda: load_w1(1, 0)), "o": (lambda: load_w1(1, 1))}
                    P2(MQ, t, mdst, mdst_r, l, True, hoist=hz, hooks=hk, late_ret=(i == 0))
                    ck(10 * l + 7)
                ck(10 * l + 8)
        except _Stop:
            pass
        S.finish()
        S.emit_all()
        print("ops", S.nops, "sems", S.nsem)
    return nc


def _host_tables():
    P = np.arange(128)
    I = np.arange(128)
    cst = np.zeros((128, 5 * 128 + 2), np.float32)
    cst[:, 0:128] = np.eye(128)
    diff = I[None, :] - P[:, None]
    BIG = 1.0e6
    cst[:, 128:256] = np.where(diff >= 0, diff, BIG)
    cst[:, 256:384] = np.where(diff < 0, -diff, BIG)
    cst[:, 384:512] = (I + 1)[None, :]
    cst[:, 512:640] = (128 - I)[None, :]
    cst[:, 640] = 127 - P
    cst[:, 641] = P
    return cst


def _rope_tables(rank):
    nf = 16
    inv = (10000.0 ** (-np.arange(nf, dtype=np.float64) / nf))
    out = np.zeros((NT, 128, 128), np.float32)
    for t in range(NT):
        p = np.arange(128)
        row = (32 * rank + 2 * t + p // 64).astype(np.float64)
        col = (p % 64).astype(np.float64)
        ar = (row[:, None].astype(np.float32) * inv.astype(np.float32)[None, :]).astype(np.float64)
        ac = (col[:, None].astype(np.float32) * inv.astype(np.float32)[None, :]).astype(np.float64)
        cr, sr, cc, sc = np.cos(ar), np.sin(ar), np.cos(ac), np.sin(ac)
        out[t, :, 0:64] = np.concatenate([cr, cr, cc, cc], 1)
        out[t, :, 64:128] = np.concatenate([-sr, sr, -sc, sc], 1)
    return out


def _bias_tables(rpb, rank):
    out = np.full((NT, 128, 6, 6, 128), NEG, np.float32)
    j = np.arange(128)
    i = np.arange(128)
    rkl, ck = j // 64, j % 64
    rql, cq = i // 64, i % 64
    cstart = np.clip(cq - 8, 0, 48)
    colok = (ck[:, None] >= cstart[None, :]) & (ck[:, None] < cstart[None, :] + 16)
    dcol = ck[:, None] - cq[None, :] + 15
    for t in range(NT):
        dlo = -3 if t == NT - 1 else -2
        dhi = 3 if t == 0 else 2
        rq = 32 * rank + 2 * t + rql
        rstart = np.clip(rq - 4, 0, 120)
        for dt in range(dlo, dhi + 1):
            b = dt - dlo
            rk = 32 * rank + 2 * (t + dt) + rkl
            rowok = (rk[:, None] >= rstart[None, :]) & (rk[:, None] < rstart[None, :] + 8) & \
                    (rk[:, None] >= 0) & (rk[:, None] < 128)
            ok = rowok & colok
            drow = np.clip(rk[:, None] - rq[None, :] + 7, 0, 14)
            dc = np.clip(dcol, 0, 30)
            vals = rpb[:, drow, dc]
            blk = np.where(ok[None], vals, NEG)
            out[t, :, :, b, :] = blk.transpose(1, 0, 2)
    out = out.reshape(NT, 128, 3, 2, 6, 128).transpose(0, 1, 2, 4, 3, 5)
    return np.ascontiguousarray(out).reshape(NT, 128, 6 * 6 * 128).astype(ml_dtypes.bfloat16)


_NC_CACHE = {}


def kernel(x, c, ctx, c_ctx, w_mod, b_mod, w_in, conv_w, conv_b, ret_decay, na_rpb, w_out, ln_g, ln_b):
    f32 = np.float32
    x = np.asarray(x, f32)
    c = np.asarray(c, f32)
    ctx = np.asarray(ctx, f32)
    c_ctx = np.asarray(c_ctx, f32)
    w_mod = np.ascontiguousarray(np.asarray(w_mod, f32))
    b_mod = np.asarray(b_mod, f32)
    w_in = np.ascontiguousarray(np.asarray(w_in, f32))
    w_out = np.ascontiguousarray(np.asarray(w_out, f32))
    conv_w = np.asarray(conv_w, f32)
    conv_b = np.asarray(conv_b, f32)
    ret_decay = np.asarray(ret_decay, f32)
    na_rpb = np.asarray(na_rpb, f32)
    ln_g = np.asarray(ln_g, f32)
    ln_b = np.asarray(ln_b, f32)

    if "nc" not in _NC_CACHE:
        _NC_CACHE["nc"] = build_program()
    nc = _NC_CACHE["nc"]

    cst = _host_tables()
    bmodT = np.ascontiguousarray(b_mod.reshape(2, 24, 128).transpose(2, 0, 1))
    convp = np.zeros((128, 2, 2, 4), f32)
    for l in range(2):
        for ch in range(2):
            convp[:, l, ch, 0:3] = conv_w[l, :, ch * 128:(ch + 1) * 128].T
            convp[:, l, ch, 3] = conv_b[l, ch * 128:(ch + 1) * 128]
    dec = np.zeros((128, 36), f32)
    dec[:, 0:24] = ret_decay.reshape(1, 24)
    for l in range(2):
        for d in range(2):
            for pr in range(3):
                dec[0:64, 24 + l * 6 + d * 3 + pr] = ret_decay[l, d, 2 * pr]
                dec[64:128, 24 + l * 6 + d * 3 + pr] = ret_decay[l, d, 2 * pr + 1]
    lnp = np.zeros((2, 2, 128, D), f32)
    lnp[:, 0] = ln_g[:, None, :]
    lnp[:, 1] = ln_b[:, None, :]

    in_maps = []
    for core in range(8):
        b, r = core // 4, core % 4
        cvec = np.zeros((128, 8, 2), f32)
        cvec[:, :, 0] = c[b].reshape(8, 128).T
        cvec[:, :, 1] = c_ctx.reshape(8, 128).T
        rankc = np.zeros((128, 20), f32)
        for k in range(4):
            if k < r:
                rankc[:, 0 + k] = 2048.0 * (r - k - 1)
                rankc[:, 10 + k] = 1.0
            if k > r:
                rankc[:, 5 + k] = 2048.0 * (k - r - 1)
                rankc[:, 15 + k] = 1.0
        rankc[:, 4] = 2048.0 * r
        rankc[:, 14] = 1.0
        rankc[:, 9] = 2048.0 * (3 - r)
        rankc[:, 19] = 1.0
        edge = np.zeros((128, 2), f32)
        edge[:, 0] = 1.0 if r > 0 else 0.0
        edge[:, 1] = 1.0 if r < 3 else 0.0
        sel = np.zeros((128, 8), f32)
        if r > 0:
            sel[:, r - 1] = 1.0
        if r < 3:
            sel[:, 4 + r + 1] = 1.0
        idx = np.zeros((128, 2), np.int32)
        idx[:, 0] = (r - 1 if r > 0 else r) * 128 + np.arange(128)
        idx[:, 1] = (r + 1 if r < 3 else r) * 128 + np.arange(128)
        nab = np.stack([_bias_tables(na_rpb[l], r) for l in range(2)], 0)
        in_maps.append({
            "x": np.ascontiguousarray(x[b, r * 2048:(r + 1) * 2048, :]),
            "ctx": np.ascontiguousarray(ctx[b]),
            "cvec": cvec, "w_mod": w_mod, "bmodT": bmodT, "w_in": w_in, "w_out": w_out, "convp": convp,
            "dec": dec, "nabias": nab, "lnp": lnp, "rope": _rope_tables(r), "cst": cst, "rankc": rankc,
            "edge": edge, "idx": idx, "sel": sel,
        })
    res = run_bass_kernel_spmd(nc, in_maps, core_ids=list(range(8)))
    out = np.zeros((2, 8192, D), f32)
    for core in range(8):
        b, r = core // 4, core % 4
        out[b, r * 2048:(r + 1) * 2048, :] = np.asarray(res.results[core]["out"], f32)
    if DEBUG:
        kernel.debug = res.results
    return out
```

```python
import numpy as np
import ml_dtypes
from contextlib import ExitStack
import concourse.bass as bass
import concourse.mybir as mybir
from concourse.bass_utils import run_bass_kernel_spmd

F32 = mybir.dt.float32
BF16 = mybir.dt.bfloat16
I32 = mybir.dt.int32
ALU = mybir.AluOpType
AF = mybir.ActivationFunctionType
AX = mybir.AxisListType

DEBUG = False
NT = 16
NCX = 2
D = 1024
ALPHA = float((2 * 2) ** 0.25)
LN_EPS = 1e-5
NEG = -30000.0
PKB = 3104 + 768
OFF_ST = 3104
OFF_NKT_TOP, OFF_NKT_BOT, OFF_NVA_TOP, OFF_NVA_BOT, OFF_U_FIRST, OFF_U_LAST = 0, 768, 1536, 2316, 3096, 3098
C_RK, C_RV, C_NK, C_NV, C_RQ, C_RG, C_NQ, C_NG, C_CH, C_CB, C_CCG, C_CZ = (
    0, 384, 768, 1152, 1536, 1920, 2304, 2688, 3072, 3328, 3584, 3840)


class Reg:
    __slots__ = ("name", "w", "rs", "excl")

    def __init__(self, name="", excl=False):
        self.name = name
        self.w = None
        self.rs = {}
        self.excl = excl


class Sched:
    ENG = ("pe", "act", "dve", "pool", "sp")
    LIMIT = 2000
    NDMA = 6

    def __init__(self, nc, stack):
        self.nc = nc
        self.stack = stack
        self.prog = {e: [] for e in self.ENG}
        self.state = {}
        self.waited = {e: {} for e in self.ENG}
        self.nsem = 0
        self.nops = 0
        self.dcount = {}

    def _tick(self, key, inc):
        st = self.state.get(key)
        if st is None or st[1] + inc > self.LIMIT:
            s = self.stack.enter_context(self.nc.semaphore(f"s{self.nsem}_{key}"))
            self.nsem += 1
            st = [s, 0]
            self.state[key] = st
            self.allsems.append(st)
        st[1] += inc
        return st[0], st[1]

    allsems = None

    def op(self, eng, fn, reads=(), writes=(), dma=None, inc=None):
        if self.allsems is None:
            self.allsems = []
        deps = {}

        def add(tok):
            if tok is None:
                return
            e, s, v = tok
            if e == "pe" and eng == "pe" and dma is None:
                return
            k = id(s)
            if k not in deps or deps[k][1] < v:
                deps[k] = (s, v)

        for r in reads:
            add(r.w)
            if r.excl:
                for t in r.rs.values():
                    if t[0] != eng:
                        add(t)
        for w in writes:
            add(w.w)
            for t in w.rs.values():
                add(t)
        wd = self.waited[eng]
        waits = []
        for k, (s, v) in deps.items():
            if wd.get(k, 0) < v:
                wd[k] = v
                waits.append((s, v))
        isdma = dma is not None
        if inc is None:
            inc = 16 if isdma else 1
        if isdma:
            i = self.dcount.get(dma, 0)
            self.dcount[dma] = i + 1
            slot = i % self.NDMA
            sem, val = self._tick(f"dma_{dma}_{slot}", inc)
            if val > inc and wd.get(id(sem), 0) < val - inc:
                wd[id(sem)] = val - inc
                waits.append((sem, val - inc))
        else:
            sem, val = self._tick(eng, inc)
        tok = (None if isdma else eng, sem, val)

        def emit(h, waits=waits, fn=fn, sem=sem, inc=inc):
            for s, v in waits:
                h.wait_ge(s, v)
            fn(h).then_inc(sem, inc)

        self.prog[eng].append(emit)
        self.nops += 1
        for r in reads:
            k = id(sem)
            old = r.rs.get(k)
            if old is None or old[2] < val:
                r.rs[k] = tok
        for w in writes:
            w.w = tok
            w.rs = {}
        return tok

    def finish(self):
        fin = [(st[0], st[1]) for st in self.allsems]

        def emit(h, fin=fin):
            for s, v in fin:
                h.wait_ge(s, v)

        self.prog["sp"].append(emit)

    def emit_all(self):
        nc = self.nc
        prog = self.prog
        with nc.Block() as block:
            @block.tensor
            def _(h):
                for f in prog["pe"]:
                    f(h)

            @block.scalar
            def _(h):
                for f in prog["act"]:
                    f(h)

            @block.vector
            def _(h):
                for f in prog["dve"]:
                    f(h)

            @block.gpsimd
            def _(h):
                for f in prog["pool"]:
                    f(h)

            @block.sync
            def _(h):
                for f in prog["sp"]:
                    f(h)


def build_program():
    nc = bass.Bass("TRN2", target_bir_lowering=False)

    def din(name, shape, dt=F32):
        return nc.dram_tensor(name, list(shape), dt, kind="ExternalInput").ap()

    x_in = din("x", [NT * 128, D])
    ctx_in = din("ctx", [NCX * 128, D])
    cvec = din("cvec", [128, 8, 2])
    w_mod = din("w_mod", [2, D, 3 * D])
    bmodT = din("bmodT", [128, 2, 24])
    w_in = din("w_in", [2, D, 4096])
    w_out = din("w_out", [2, D, D])
    convp = din("convp", [128, 2, 2, 4])
    dec = din("dec", [128, 36])
    nabias = din("nabias", [2, NT, 128, 6 * 6 * 128], BF16)
    lnp = din("lnp", [2, 2, 128, D])
    rope = din("rope", [NT, 128, 128])
    cst = din("cst", [128, 5 * 128 + 2])
    rankc = din("rankc", [128, 20])
    edge = din("edge", [128, 2])
    selin = din("sel", [128, 8])
    idx = din("idx", [128, 2], I32)
    out = nc.dram_tensor("out", [NT * 128, D], F32, kind="ExternalOutput").ap()
    okind = "ExternalOutput" if DEBUG else "Internal"
    x1 = nc.dram_tensor("x1", [NT * 128, D], F32, kind=okind).ap()
    xc1 = nc.dram_tensor("xc1", [NCX * 128, D], F32, kind=okind).ap()
    packF = [nc.dram_tensor(f"packF{l}", [128, 384], F32).ap() for l in range(2)]
    gathF = [nc.dram_tensor(f"gathF{l}", [512, 384], F32).ap() for l in range(2)]
    packB = [nc.dram_tensor(f"packB{l}", [128, PKB], BF16).ap() for l in range(2)]
    gathB = [nc.dram_tensor(f"gathB{l}", [512, PKB], BF16).ap() for l in range(2)]
    RG = [[0, 1, 2, 3], [4, 5, 6, 7]]

    with ExitStack() as st:
        S = Sched(nc, st)

        def sb(name, shape, dt=F32):
            return st.enter_context(nc.sbuf_tensor(name, list(shape), dt))

        def R(n=""):
            return Reg(n)

        def Rs(n, k):
            return [Reg(f"{n}{i}") for i in range(k)]

        import os
        KSTOP = float(os.environ.get("KSTOP", "999"))

        class _Stop(Exception):
            pass

        def ck(n):
            if n >= KSTOP:
                raise _Stop()

        def MM(o, lhsT, rhs, start, stop, rd, wr, tp=None):
            if tp is None:
                S.op("pe", lambda h: h.matmul(o, lhsT, rhs, start=start, stop=stop), rd, wr)
            else:
                S.op("pe", lambda h: h.matmul(o, lhsT, rhs, start=start, stop=stop, tile_position=tp), rd, wr)

        def TR(o, i, ident, rd, wr):
            S.op("pe", lambda h: h.transpose(o, i, ident), rd, wr)

        def ACT(o, i, func, rd, wr, bias=None, scale=None, accum=None):
            kw = {}
            if bias is not None:
                kw["bias"] = bias
            if scale is not None:
                kw["scale"] = scale
            if accum is not None:
                kw["accum_out"] = accum
            S.op("act", lambda h: h.activation(out=o, in_=i, func=func, **kw), rd, wr)

        def TT(e, o, a, b, op, rd, wr):
            S.op(e, lambda h: h.tensor_tensor(out=o, in0=a, in1=b, op=op), rd, wr)

        def TS(e, o, a, s1, s2, op0, op1, rd, wr):
            if s2 is None:
                S.op(e, lambda h: h.tensor_scalar(out=o, in0=a, scalar1=s1, scalar2=None, op0=op0), rd, wr)
            else:
                S.op(e, lambda h: h.tensor_scalar(out=o, in0=a, scalar1=s1, scalar2=s2, op0=op0, op1=op1), rd, wr)

        def STT(o, a, s, b, op0, op1, rd, wr):
            S.op("dve", lambda h: h.scalar_tensor_tensor(out=o, in0=a, scalar=s, in1=b, op0=op0, op1=op1), rd, wr)

        def CP(e, o, i, rd, wr):
            if e == "act":
                S.op("act", lambda h: h.copy(out=o, in_=i), rd, wr)
            else:
                S.op(e, lambda h: h.tensor_copy(out=o, in_=i), rd, wr)

        def MSET(e, ap, v, wr):
            S.op(e, lambda h: h.memset(ap, v), [], wr)

        def DMA(e, o, i, rd, wr, stream):
            S.op(e, lambda h: h.dma_start(out=o, in_=i), rd, wr, dma=stream)

        PB = [st.enter_context(nc.psum_tensor(f"pb{i}", [128, 512], F32)) for i in range(7)]
        PB.append(st.enter_context(nc.psum_tensor("pb7", [128, 1024], BF16)))
        PR = [Reg(f"pb{i}", excl=True) for i in range(8)]

        cs = sb("cs", [128, 5 * 128 + 2])
        r_cs = R("cs")
        DMA("sp", cs[:], cst[:, :], [], [r_cs], "ld")
        identf = cs[:, 0:128]
        DFt = cs[:, 128:256]
        DBt = cs[:, 256:384]
        ip1 = cs[:, 384:512]
        rev = cs[:, 512:640]
        jrev = cs[:, 640:641]
        jpos = cs[:, 641:642]
        identb = sb("identb", [128, 128], BF16)
        r_idb = R("idb")
        CP("dve", identb[:], identf, [r_cs], [r_idb])
        onesf = sb("onesf", [128, 128])
        r_ones = R("ones")
        MSET("pool", onesf[:], 1.0, [r_ones])
        epsc = sb("epsc", [128, 1])
        r_eps = R("eps")
        MSET("pool", epsc[:], LN_EPS, [r_eps])
        rk_t = sb("rk_t", [128, 20])
        eg_t = sb("eg_t", [128, 2])
        ix_t = sb("ix_t", [128, 2], I32)
        cp_t = sb("cp_t", [128, 2, 2, 4])
        r_misc = R("misc")
        DMA("sp", rk_t[:], rankc[:, :], [], [r_misc], "ld")
        DMA("sp", eg_t[:], edge[:, :], [], [r_misc], "ld")
        sel_t = sb("sel_t", [128, 8])
        DMA("sp", sel_t[:], selin[:, :], [], [r_misc], "ld")
        DMA("sp", ix_t[:], idx[:, :], [], [r_misc], "ld")
        DMA("sp", cp_t[:], convp[:, :, :, :], [], [r_misc], "ld")
        dc = sb("dc", [128, 36])
        lg = sb("lg", [128, 36])
        r_lg = R("lg")
        DMA("sp", dc[:], dec[:, :], [], [r_lg], "ld")
        ACT(lg[:], dc[:], AF.Exp, [r_lg], [r_lg], scale=-1.0)
        TS("dve", lg[:], lg[:], 1.0, None, ALU.add, None, [r_lg], [r_lg])
        ACT(lg[:], lg[:], AF.Ln, [r_lg], [r_lg])
        TS("dve", lg[:], lg[:], -1.0, None, ALU.mult, None, [r_lg], [r_lg])

        def lg_bc(l, d, h):
            c = l * 12 + d * 6 + h
            return lg[:, c:c + 1]

        def lg_pp(l, d, pr):
            c = 24 + l * 6 + d * 3 + pr
            return lg[:, c:c + 1]

        T128 = sb("T128", [128, 2, 16])
        r_t128 = R("t128")
        TS("dve", T128[:, 0, :], ip1[:, 0:16], -1.0, 128.0, ALU.add, ALU.mult, [r_cs], [r_t128])
        TS("dve", T128[:, 1, :], ip1[:, 0:16], -16.0, -128.0, ALU.add, ALU.mult, [r_cs, r_t128], [r_t128])

        ca = sb("ca", [128, 8, 2])
        r_ca = R("ca")
        DMA("sp", ca[:], cvec[:, :, :], [], [r_ca], "ld")
        ACT(ca[:], ca[:], AF.Silu, [r_ca], [r_ca])
        cab = sb("cab", [128, 8, 2], BF16)
        CP("dve", cab[:], ca[:], [r_ca], [r_ca])

        Wb = sb("Wb", [128, 8, 2560], BF16)
        WRK = [[Reg(f"W{g}_{kp}") for kp in range(4)] for g in range(8)]

        def WOK(k):
            return [WRK[g][k // 2] for g in range(4, 8)]

        def mkseq(name, nt, nslot, halo):
            q = dict(name=name, nt=nt, nslot=nslot, halo=halo)
            q["kT"] = sb(name + "kT", [128, 3, nt * 128], BF16)
            q["vret"] = sb(name + "vret", [128, nt, 384], BF16)
            q["nkT"] = sb(name + "nkT", [128, 3, nslot * 128], BF16)
            q["nva"] = sb(name + "nva", [128, nslot, 6, 65], BF16)
            q["sfst"] = sb(name + "sfst", [128, nt, 3, 64], BF16)
            q["tbst"] = sb(name + "tbst", [128, nt, 3, 64], BF16)
            q["uT"] = sb(name + "uT", [128, 2, nt * 128 + 2], BF16)
            q["gcT"] = sb(name + "gcT", [128, 2, nt * 128], BF16)
            q["runf"] = sb(name + "runf", [128, 3, 64])
            q["runb"] = sb(name + "runb", [128, 3, 64])
            for k in ("kT", "vret", "sfst", "tbst", "gcT"):
                q["r_" + k] = Rs(name + k, nt)
            q["r_nk"] = Rs(name + "nk", nslot)
            q["r_nv"] = Rs(name + "nv", nslot)
            q["r_u"] = Rs(name + "u", nt + 2)
            q["r_runf"] = R(name + "runf")
            q["r_runb"] = R(name + "runb")
            return q

        MQ = mkseq("m", NT, NT + 4, 2)
        CQ = mkseq("c", NCX, NCX, 0)
        MSET("pool", MQ["nva"][:], 1.0, MQ["r_nv"])
        MSET("pool", CQ["nva"][:], 1.0, CQ["r_nv"])
        MSET("pool", CQ["uT"][:], 0.0, CQ["r_u"])
        MSET("pool", MQ["uT"][:], 0.0, MQ["r_u"])

        biasb = sb("biasb", [128, 3, 6, 2, 128], BF16)
        r_bias = R("bias")
        gbc = sb("gbc", [128, D])
        bbc = sb("bbc", [128, D])
        r_lnp = R("lnp")
        gate_bc = sb("gate_bc", [128, D])
        r_gate = R("gate")
        modTs = [sb(f"modT{i}", [128, 24, 2]) for i in range(2)]
        r_mods = Rs("mod", 2)
        cur = {"l": 0}
        Dcomb = sb("Dcomb", [128, 6, 128])
        r_dc = R("dcomb")
        WQ = sb("WQ", [128, 2, 3, 128])
        r_wq = R("wq")
        WK = sb("WK", [128, 2, 6])
        r_wk = R("wk")
        G128 = sb("G128", [128, 2, 3])
        r_g128 = R("g128")
        gpow = sb("gpow", [128, 3, 3, 16])
        r_gpow = R("gpow")
        coef = sb("coef", [128, 2, 3, 5])
        r_coef = R("coef")
        Sin = sb("Sin", [128, 2, 3, 64], BF16)
        r_sin = R("sin")

        xt = [sb(f"xt{i}", [128, D]) for i in range(2)]
        r_xt = Rs("xt", 2)
        hT = [sb(f"hT{i}", [128, 8, 128], BF16) for i in range(2)]
        r_hT = Rs("hT", 2)
        ropet = [sb(f"ropet{i}", [128, 128]) for i in range(2)]
        r_rope = Rs("rope", 2)
        tA = sb("tA", [128, 384])
        tB = sb("tB", [128, 384])
        r_tA, r_tB = R("tA"), R("tB")
        krot = sb("krot", [128, 384], BF16)
        r_krot = R("krot")
        cvt = sb("cvt", [128, 2, 128])
        r_cvt = R("cvt")
        cvu = sb("cvu", [128, 2, 128])
        r_cvu = R("cvu")
        qT = sb("qT", [128, 3, 128], BF16)
        r_qT = R("qT")
        qsc = sb("qsc", [128, 4, 3, 128], BF16)
        r_qsc = R("qsc")
        kw = qsc[:, 0:2, :, :].rearrange("p d a b -> p d (a b)")
        r_kw = r_qsc
        nqTm = sb("nqTm", [128, 3, 2, 128], BF16)
        r_nqT = R("nqT")
        MSET("pool", nqTm[:], 0.0, [r_nqT])
        srg = sb("srg", [128, 384], BF16)
        sng = sb("sng", [128, 384], BF16)
        r_srg, r_sng = R("srg"), R("sng")
        AT = sb("AT", [128, 6, 128], BF16)
        r_AT = R("AT")
        st6 = sb("st6", [128, 8, 6])
        r_st6 = R("st6")
        r_st6a = R("st6a")
        r_st6c = R("st6c")
        r_st6b = R("st6b")
        yret = sb("yret", [128, 384], BF16)
        yna = sb("yna", [128, 384], BF16)
        r_yret, r_yna = R("yret"), R("yna")
        Eb = [sb(f"Eb{i}", [128, 8, 128], BF16) for i in range(2)]
        r_Eb = Rs("Eb", 2)
        yT = sb("yT", [128, 8, 128], BF16)
        r_yTc, r_yTr, r_yTn = R("yTc"), R("yTr"), R("yTn")
        zb = sb("zb", [128, D])
        r_zb = R("zb")
        st1 = sb("st1", [128, 8])
        r_st1 = R("st1")
        r_st1a = R("st1a")
        tmp64 = tA[:, 0:192].rearrange("p (a b) -> p a b", b=64)
        r_tmp64 = r_tA
        sacc = tB[:].rearrange("p (d a b) -> p d a b", d=2, a=3)
        r_sacc = r_tB
        tmp2 = tB[:, 0:192].rearrange("p (a b) -> p a b", b=64)
        r_tmp2 = r_tB
        totb = sb("totb", [128, 3, 64])
        r_totb = R("totb")
        ub = sb("ub", [128, 8], BF16)
        ubg = sb("ubg", [128, 4], BF16)
        r_ub, r_ubg = R("ub"), R("ubg")

        x1_r = Rs("x1d", NT)
        xc1_r = Rs("xc1d", NCX)

        def mod_piece(l, fc, stg_pair, bst_pair, ce, acc):
            wsrc = w_mod[l].rearrange("(k p) c -> p k c", p=128)
            bt, rb = stg_pair
            bb, rbb = bst_pair
            b = bt[:].rearrange("p (k c) -> p k c", c=128)
            DMA("sp", b, wsrc[:, :, fc * 128:(fc + 1) * 128], [], [rb], "wm")
            CP(ce, bb[:], b, [rb], [rbb])
            accap, accr = acc
            for k in range(8):
                MM(accap[:, fc * 2:fc * 2 + 2], bb[:, k, :], cab[:, k, :], k == 0, k == 7, [rbb, r_ca], [accr])

        def mod_finish(l, acc):
            modT, r_mod = modTs[l], r_mods[l]
            accap, accr = acc
            bm = sb(f"bm{l}", [128, 24])
            r_bm = R("bm")
            DMA("sp", bm[:], bmodT[:, l, :], [], [r_bm], "ld")
            TT("dve", modT[:], accap[:, 0:48].rearrange("p (a b) -> p a b", b=2),
               bm[:].unsqueeze(2).broadcast_to([128, 24, 2]), ALU.add, [accr, r_bm], [r_mod])
            TS("dve", modT[:, 8:16, :], modT[:, 8:16, :], 1.0, None, ALU.add, None, [r_mod], [r_mod])

        def mod_setup(l):
            stg = [(xt[0], r_xt[0]), (xt[1], r_xt[1]), (zb, r_zb), (gate_bc, r_gate)]
            bst = [(hT[0], r_hT[0]), (hT[1], r_hT[1]), (Eb[0], r_Eb[0]), (Eb[1], r_Eb[1])]
            ceng = ("act", "dve", "act", "dve")
            acc = (PB[6][:, 0:48], PR[6])
            for fc in range(24):
                mod_piece(l, fc, stg[fc % 4], bst[fc % 4], ceng[fc % 4], acc)
            mod_finish(l, acc)

        acc1 = (PB[7][:].bitcast(F32)[:, 256:304], PR[7])

        def mod1_piece(fc):
            stg = [(zb, r_zb), (gate_bc, r_gate)]
            bst = [(Eb[0], r_Eb[0]), (Eb[1], r_Eb[1])]
            mod_piece(1, fc, stg[fc % 2], bst[fc % 2], ("act", "dve")[fc % 2], acc1)

        def layer_setup(l):
            cur["l"] = l
            DMA("sp", gbc[:], lnp[l, 0, :, :], [], [r_lnp], "ld")
            DMA("sp", bbc[:], lnp[l, 1, :, :], [], [r_lnp], "ld")
            for h in range(6):
                ACT(tA[:, 0:128], DFt, AF.Exp, [r_cs, r_lg], [r_tA], scale=lg_bc(l, 0, h))
                ACT(tB[:, 0:128], DBt, AF.Exp, [r_cs, r_lg], [r_tB], scale=lg_bc(l, 1, h))
                TT("dve", Dcomb[:, (h % 2) * 3 + h // 2, :], tA[:, 0:128], tB[:, 0:128], ALU.add, [r_tA, r_tB], [r_dc])
            TS("dve", Dcomb[:], Dcomb[:], 0.125, None, ALU.mult, None, [r_dc], [r_dc])
            for pr in range(3):
                ACT(WQ[:, 0, pr, :], ip1, AF.Exp, [r_cs, r_lg], [r_wq], scale=lg_pp(l, 0, pr))
                ACT(WQ[:, 1, pr, :], rev, AF.Exp, [r_cs, r_lg], [r_wq], scale=lg_pp(l, 1, pr))
                ACT(gpow[:, 0, pr, :], T128[:, 0, :], AF.Exp, [r_t128, r_lg], [r_gpow], scale=lg_pp(l, 0, pr))
                ACT(gpow[:, 1, pr, :], T128[:, 1, :], AF.Exp, [r_t128, r_lg], [r_gpow], scale=lg_pp(l, 1, pr))
                ACT(gpow[:, 2, pr, :], T128[:, 0, :], AF.Exp, [r_t128, r_lg], [r_gpow], scale=lg_pp(l, 1, pr))
                for d in range(2):
                    ACT(coef[:, d, pr, :], rk_t[:, d * 5:d * 5 + 5], AF.Exp, [r_misc, r_lg], [r_coef],
                        scale=lg_pp(l, d, pr))
                    TT("dve", coef[:, d, pr, :], coef[:, d, pr, :], rk_t[:, 10 + d * 5:15 + d * 5], ALU.mult,
                       [r_coef, r_misc], [r_coef])
            for d in range(2):
                ACT(G128[:, d, :], lg[:, 24 + l * 6 + d * 3:24 + l * 6 + d * 3 + 3], AF.Exp, [r_lg], [r_g128],
                    scale=128.0)
                TS("dve", WK[:, d, :], lg[:, l * 12 + d * 6:l * 12 + d * 6 + 6], jrev if d == 0 else jpos, None,
                   ALU.mult, None, [r_lg, r_cs], [r_wk])
            ACT(WK[:], WK[:], AF.Exp, [r_wk], [r_wk])
            TS("dve", WK[:], WK[:], 0.125, None, ALU.mult, None, [r_wk], [r_wk])

        def build_gate(m):
            for k in range(8):
                TS("dve", tB[:, 0:128], identf, modTs[cur["l"]][:, 16 + k, m:m + 1], None, ALU.mult, None,
                   [r_cs, r_mods[cur["l"]]], [r_tB])
                MM(PB[5][:, (k % 4) * 128:(k % 4 + 1) * 128], onesf[:], tB[:, 0:128], True, True, [r_ones, r_tB], [PR[5]])
                if k % 4 == 3:
                    CP("act", gate_bc[:, (k // 4) * 512:(k // 4 + 1) * 512], PB[5][:, :], [PR[5]], [r_gate])

        W1_GROUPS = ((C_RK, 384, 0, 0), (C_RV, 384, 384, 1), (C_NV, 384, 768, 2), (C_NK, 384, 1152, 3),
                     (C_CH, 256, 1536, 4), (C_CCG, 256, 1792, 5), (C_CB, 256, 2048, 6), (C_CZ, 256, 2304, 7))

        def load_w1(l, part):
            src = w_in[l].rearrange("(k p) c -> p k c", p=128)
            for (c0, n, o, g) in (W1_GROUPS[0:4] if part == 0 else W1_GROUPS[4:8]):
                DMA("pool", Wb[:, :, o:o + n], src[:, :, c0:c0 + n], [], WRK[g], "w")

        def load_w2(l):
            src = w_in[l].rearrange("(k p) c -> p k c", p=128)
            if l == 0:
                for (c0, n, o, g) in ((C_RQ, 384, 0, 0), (C_NQ, 384, 1152, 3), (C_RG, 384, 384, 1), (C_NG, 384, 768, 2)):
                    DMA("pool", Wb[:, :, o:o + n], src[:, :, c0:c0 + n], [], WRK[g], "w")
                srco0 = w_out[l].rearrange("(k p) c -> p k c", p=128)
                DMA("pool", Wb[:, :, 1536:2560], srco0[:, :, :], [], [r for g in range(4, 8) for r in WRK[g]], "w")
                return
            stg = [(xt[0], r_xt[0]), (xt[1], r_xt[1]), (zb, r_zb), (gate_bc, r_gate)]
            ceng = ("pool", "act", "dve", "pool")
            i = 0
            for (c0, n, o, g) in ((C_RQ, 384, 0, 0), (C_NQ, 384, 1152, 3), (C_RG, 384, 384, 1), (C_NG, 384, 768, 2)):
                for kp in range(4):
                    bt, rb = stg[i % 4]
                    v = bt[:, 0:768].rearrange("p (k c) -> p k c", c=384)
                    DMA("sp", v, src[:, 2 * kp:2 * kp + 2, c0:c0 + n], [], [rb], "w")
                    CP(ceng[i % 4], Wb[:, 2 * kp:2 * kp + 2, o:o + n], v, [rb], [WRK[g][kp]])
                    i += 1
            srco = w_out[l].rearrange("(k p) c -> p k c", p=128)
            for k in range(8):
                bt, rb = stg[i % 4]
                DMA("sp", bt[:], srco[:, k, :], [], [rb], "w")
                CP(ceng[i % 4], Wb[:, k, 1536:2560], bt[:], [rb], WOK(k))
                i += 1

        def LX(src, src_r, t, m, l=None, rope_on=False, bias_on=False):
            s = t % 2
            rd = [src_r[t]] if src_r is not None else []
            DMA("sp", xt[s][:], src[t * 128:(t + 1) * 128, :], rd, [r_xt[s]], "x")
            if rope_on:
                DMA("sp", ropet[s][:], rope[t, :, :], [], [r_rope[s]], "x")
            if bias_on:
                DMA("sp", biasb[:].rearrange("p a b h i -> p (a b h i)"), nabias[l, t, :, :], [], [r_bias], "x")
            for k in range(8):
                TR(PB[k // 4][:, (k % 4) * 128:(k % 4 + 1) * 128], xt[s][:, k * 128:(k + 1) * 128], identf,
                   [r_xt[s], r_cs], [PR[k // 4]])
            for k in range(8):
                ACT(hT[s][:, k, :], PB[k // 4][:, (k % 4) * 128:(k % 4 + 1) * 128], AF.Identity,
                    [PR[k // 4], r_mods[cur["l"]]], [r_hT[s]], bias=modTs[cur["l"]][:, k, m:m + 1],
                    scale=modTs[cur["l"]][:, 8 + k, m:m + 1])

        def do_rope(src_ps, src_r, s_rope, dst, dst_r):
            v = src_ps.rearrange("p (h d) -> p h d", d=64)
            cosb = ropet[s_rope][:, 0:64].unsqueeze(1).broadcast_to([128, 6, 64])
            TT("dve", tA[:].rearrange("p (h d) -> p h d", d=64), v, cosb, ALU.mult, [src_r, r_rope[s_rope]], [r_tA])
            v5 = src_ps.rearrange("p (h r a f) -> p h r a f", r=2, a=2, f=16)
            tB5 = tB[:].rearrange("p (h r a f) -> p h r a f", r=2, a=2, f=16)
            sn = ropet[s_rope][:, 64:128].rearrange("p (r a f) -> p r a f", r=2, a=2)
            for a in range(2):
                TT("dve", tB5[:, :, :, a, :], v5[:, :, :, 1 - a, :],
                   sn[:, :, a, :].unsqueeze(1).broadcast_to([128, 6, 2, 16]), ALU.mult,
                   [src_r, r_rope[s_rope]], [r_tB])
            TT("dve", dst, tA[:], tB[:], ALU.add, [r_tA, r_tB], [dst_r])

        def P1(q, t, src, src_r, m, use_rope, hoist=None):
            nt = q["nt"]
            slot = t + q["halo"]
            s = t % 2
            hs, rhs_ = hT[s], r_hT[s]
            for (bank, wo, g) in ((2, 0, 0), (3, 384, 1), (4, 768, 2)):
                for k in range(8):
                    MM(PB[bank][:, 0:384], hs[:, k, :], Wb[:, k, wo:wo + 384], k == 0, k == 7, [rhs_, WRK[g][k // 2]],
                       [PR[bank]])
            for c in range(3):
                for k in range(8):
                    MM(PB[5][:, c * 128:(c + 1) * 128], Wb[:, k, 1152 + c * 128:1152 + (c + 1) * 128], hs[:, k, :],
                       k == 0, k == 7, [rhs_, WRK[3][k // 2]], [PR[5]])
            def conv_half(bank, groups):
                for gi, (wo, g) in enumerate(groups):
                    for c in range(2):
                        blk = gi * 2 + c
                        for k in range(8):
                            MM(PB[bank][:, blk * 128:(blk + 1) * 128], Wb[:, k, wo + c * 128:wo + (c + 1) * 128],
                               hs[:, k, :], k == 0, k == 7, [rhs_, WRK[g][k // 2]], [PR[bank]])

            conv_half(6, ((1536, 4), (1792, 5)))
            v3 = PB[3][:, 0:384].rearrange("p (h e) -> p h e", e=64)
            S.op("dve", lambda h_: h_.reduce_sum(out=st6[:, 7, :], in_=v3, axis=AX.X), [PR[3]], [r_st6c])
            TS("dve", st6[:, 7, :], st6[:, 7, :], 1.0 / 64.0, None, ALU.mult, None, [r_st6c], [r_st6c])
            TT("dve", q["vret"][:, t, :].rearrange("p (h e) -> p h e", e=64), v3,
               st6[:, 7, :].unsqueeze(2).broadcast_to([128, 6, 64]), ALU.subtract, [PR[3], r_st6c], [q["r_vret"][t]])
            CP("act", q["nva"][:, slot, :, 0:64], PB[4][:, 0:384].rearrange("p (h e) -> p h e", e=64), [PR[4]],
               [q["r_nv"][slot]])
            conv_half(4, ((2048, 6), (2304, 7)))
            if hoist is not None:
                hoist()
            if use_rope:
                do_rope(PB[2][:, 0:384], PR[2], s, krot[:], r_krot)
            else:
                CP("dve", krot[:], PB[2][:, 0:384], [PR[2]], [r_krot])
            for d in range(2):
                TT("pool", kw[:, d, :].rearrange("p (h e) -> p h e", e=64), krot[:].rearrange("p (h e) -> p h e", e=64),
                   WK[:, d, :].unsqueeze(2).broadcast_to([128, 6, 64]), ALU.mult, [r_krot, r_wk], [r_kw])
            for c in range(3):
                TR(PB[7][:, c * 128:(c + 1) * 128], krot[:, c * 128:(c + 1) * 128], identb[:], [r_krot, r_idb], [PR[7]])
            CP("dve", q["kT"][:, :, t * 128:(t + 1) * 128], PB[7][:, 0:384].rearrange("p (c i) -> p c i", i=128),
               [PR[7]], [q["r_kT"][t]])
            CP("dve", q["nkT"][:, :, slot * 128:(slot + 1) * 128], PB[5][:, 0:384].rearrange("p (c i) -> p c i", i=128),
               [PR[5]], [q["r_nk"][slot]])
            CP("act", cvt[:], PB[6][:, 256:512].rearrange("p (c i) -> p c i", i=128), [PR[6]], [r_cvt])
            TT("dve", q["uT"][:, :, 1 + t * 128:1 + (t + 1) * 128], PB[6][:, 0:256].rearrange("p (c i) -> p c i", i=128),
               cvt[:], ALU.mult, [PR[6], r_cvt], [q["r_u"][t + 1]])
            ACT(cvu[:], PB[4][:, 256:512].rearrange("p (c i) -> p c i", i=128), AF.Silu, [PR[4]], [r_cvu])
            TT("dve", q["gcT"][:, :, t * 128:(t + 1) * 128], PB[4][:, 0:256].rearrange("p (c i) -> p c i", i=128),
               cvu[:], ALU.mult, [PR[4], r_cvu], [q["r_gcT"][t]])
            for d in range(2):
                for pr in range(3):
                    MM(PB[2 + d][:, pr * 128:(pr + 1) * 128], kw[:, d, pr * 128:(pr + 1) * 128],
                       q["vret"][:, t, pr * 128:(pr + 1) * 128], True, True, [r_kw, q["r_vret"][t]], [PR[2 + d]])

            def diag(bank, hp):
                rows = slice(hp * 64, hp * 64 + 64)
                return PB[bank][rows, 0:384].rearrange("p (a b) -> p a b", b=128)[:, :, hp * 64:(hp + 1) * 64]

            CP("pool", q["sfst"][:, t, :, :], q["runf"][:], [q["r_runf"]], [q["r_sfst"][t]])
            TT("pool", tmp64, q["runf"][:], G128[:, 0, :].unsqueeze(2).broadcast_to([128, 3, 64]), ALU.mult,
               [q["r_runf"], r_g128], [r_tmp64])
            for hp in range(2):
                rows = slice(hp * 64, hp * 64 + 64)
                TT("dve", q["runf"][rows, :, :], diag(2, hp), tmp64[rows, :, :], ALU.add, [PR[2], r_tmp64], [q["r_runf"]])
                CP("act", q["tbst"][rows, t, :, :], diag(3, hp), [PR[3]], [q["r_tbst"][t]])
            if use_rope:
                for hp in range(2):
                    rows = slice(hp * 64, hp * 64 + 64)
                    TT("dve", tmp2[rows, :, :], diag(3, hp), gpow[rows, 2, :, t:t + 1].broadcast_to([64, 3, 64]), ALU.mult,
                       [PR[3], r_gpow], [r_tmp2])
                TT("pool", totb[:], totb[:], tmp2, ALU.add, [r_totb, r_tmp2], [r_totb])

        def P1_finish(q):
            nt = q["nt"]
            for t in range(nt - 1, -1, -1):
                CP("dve", tmp64, q["tbst"][:, t, :, :], [q["r_tbst"][t]], [r_tmp64])
                CP("dve", q["tbst"][:, t, :, :], q["runb"][:], [q["r_runb"]], [q["r_tbst"][t]])
                for pr in range(3):
                    STT(q["runb"][:, pr, :], q["runb"][:, pr, :], G128[:, 1, pr:pr + 1], tmp64[:, pr, :], ALU.mult,
                        ALU.add, [q["r_runb"], r_g128, r_tmp64], [q["r_runb"]])

        def P2(q, t, dst, dst_r, l, is_main, hoist=None, hooks=None, late_ret=False):
            nt = q["nt"]
            s = t % 2
            hs, rhs_ = hT[s], r_hT[s]
            def tok_proj(bank, wo, g):
                for k in range(8):
                    MM(PB[bank][:, 0:384], hs[:, k, :], Wb[:, k, wo:wo + 384], k == 0, k == 7, [rhs_, WRK[g][k // 2]],
                       [PR[bank]])

            tok_proj(2, 0, 0)
            for c in range(3):
                for k in range(8):
                    MM(PB[5][:, c * 128:(c + 1) * 128], Wb[:, k, 1152 + c * 128:1152 + (c + 1) * 128], hs[:, k, :],
                       k == 0, k == 7, [rhs_, WRK[3][k // 2]], [PR[5]])
            tok_proj(3, 384, 1)
            tok_proj(4, 768, 2)
            if hooks is not None and "a2" in hooks:
                hooks["a2"]()
            for hp in range(2):
                rows = slice(hp * 64, hp * 64 + 64)
                ACT(nqTm[rows, :, hp, :], PB[5][rows, 0:384].rearrange("p (c i) -> p c i", i=128), AF.Identity, [PR[5]],
                    [r_nqT], scale=0.125)
            if is_main:
                do_rope(PB[2][:, 0:384], PR[2], s, krot[:], r_krot)
            else:
                CP("dve", krot[:], PB[2][:, 0:384], [PR[2]], [r_krot])
            ACT(srg[:], PB[3][:, 0:384], AF.Silu, [PR[3]], [r_srg])
            ACT(sng[:], PB[4][:, 0:384], AF.Silu, [PR[4]], [r_sng])

            if is_main:
                dlo = -3 if t == NT - 1 else -2
                dhi = 3 if t == 0 else 2
                blocks = [(q, t + dt + 2, dt - dlo) for dt in range(dlo, dhi + 1)] + [(CQ, c, None) for c in range(NCX)]
            else:
                blocks = [(CQ, c, None) for c in range(NCX)]
            nb = len(blocks)

            def unit_blocks(half):
                return list(enumerate(blocks))[0:4] if half == 0 else list(enumerate(blocks))[4:nb]

            def N_S(pr, half):
                banks = (5, 6) if half == 0 else (0, 1)
                for j, (bi, (kq, slot, bidx)) in enumerate(unit_blocks(half)):
                    bank = banks[j // 2]
                    col = (j % 2) * 256
                    last = bidx is None
                    MM(PB[bank][:, col:col + 256], kq["nkT"][:, pr, slot * 128:(slot + 1) * 128], nqTm[:, pr, :, :],
                       True, last, [kq["r_nk"][slot], r_nqT], [PR[bank]])
                    if not last:
                        MM(PB[bank][:, col:col + 256], identb[:], biasb[:, pr, bidx, :, :], False, True, [r_idb, r_bias],
                           [PR[bank]])

            def N_E(pr, half):
                banks = (5, 6) if half == 0 else (0, 1)
                ub_ = unit_blocks(half)
                e4 = Eb[half][:].rearrange("p (b h) i -> p b h i", h=2)
                for jb in range(2):
                    nblk = min(2, len(ub_) - jb * 2)
                    if nblk <= 0:
                        continue
                    ACT(e4[:, jb * 2:jb * 2 + nblk, :, :],
                        PB[banks[jb]][:, 0:nblk * 256].rearrange("p (b h i) -> p b h i", h=2, i=128), AF.Exp,
                        [PR[banks[jb]]], [r_Eb[half]])

            def N_PV(pr):
                for hp in range(2):
                    h = 2 * pr + hp
                    for bi, (kq, slot, bidx) in enumerate(blocks):
                        half, j = (0, bi) if bi < 4 else (1, bi - 4)
                        e4 = Eb[half][:].rearrange("p (b h) i -> p b h i", h=2)
                        MM(PB[3][:, h * 65:(h + 1) * 65], e4[:, j, hp, :], kq["nva"][:, slot, h, :], bi == 0, bi == nb - 1,
                           [r_Eb[half], kq["r_nv"][slot]], [PR[3]])

            def N_F():
                ona = PB[3][:, 0:390].rearrange("p (h e) -> p h e", e=65)
                S.op("dve", lambda h_: h_.reciprocal(out=st6[:, 6, :], in_=ona[:, :, 64]), [PR[3]], [r_st6b])
                tA3 = tA[:].rearrange("p (h e) -> p h e", e=64)
                TT("dve", tA3, ona[:, :, 0:64], st6[:, 6, :].unsqueeze(2).broadcast_to([128, 6, 64]), ALU.mult,
                   [PR[3], r_st6b], [r_tA])
                TT("pool", yna[:], tA[:], sng[:], ALU.mult, [r_tA, r_sng], [r_yna])

            def R1():
                for c in range(3):
                    TR(PB[7][:, c * 128:(c + 1) * 128], krot[:, c * 128:(c + 1) * 128], identb[:], [r_krot, r_idb],
                       [PR[7]])
                CP("dve", qT[:], PB[7][:, 0:384].rearrange("p (c i) -> p c i", i=128), [PR[7]], [r_qT])
                for d in range(2):
                    TT("pool", qsc[:, d, :, :], qT[:], WQ[:, d, :, :], ALU.mult, [r_qT, r_wq], [r_qsc])
                if is_main:
                    for d in range(2):
                        TT("pool", qsc[:, 2 + d, :, :], qsc[:, d, :, :], gpow[:, d, :, t:t + 1].broadcast_to([128, 3, 128]),
                           ALU.mult, [r_qsc, r_gpow], [r_qsc])

            def R2():
                for h in range(6):
                    pr, hp = h // 2, h % 2
                    rows = slice(hp * 64, hp * 64 + 64)
                    MM(PB[hp][:, pr * 128:(pr + 1) * 128], q["kT"][rows, pr, t * 128:(t + 1) * 128], qT[rows, pr, :],
                       True, True, [q["r_kT"][t], r_qT], [PR[hp]], tp=(hp * 64, 0))
                for hp in range(2):
                    TT("dve", AT[:, hp * 3:hp * 3 + 3, :], PB[hp][:, 0:384].rearrange("p (h i) -> p h i", i=128),
                       Dcomb[:, hp * 3:hp * 3 + 3, :], ALU.mult, [PR[hp], r_dc], [r_AT])

            def R3():
                for h in range(6):
                    pr, hp = h // 2, h % 2
                    rows = slice(hp * 64, hp * 64 + 64)
                    o = PB[2][:, h * 64:(h + 1) * 64]
                    tp = (hp * 64, 0)
                    MM(o, AT[:, hp * 3 + pr, :], q["vret"][:, t, h * 64:(h + 1) * 64], True, False,
                       [r_AT, q["r_vret"][t]], [PR[2]])
                    MM(o, qsc[rows, 0, pr, :], q["sfst"][rows, t, pr, :], False, False, [r_qsc, q["r_sfst"][t]], [PR[2]],
                       tp=tp)
                    MM(o, qsc[rows, 1, pr, :], q["tbst"][rows, t, pr, :], False, not is_main, [r_qsc, q["r_tbst"][t]],
                       [PR[2]], tp=tp)
                    if is_main:
                        MM(o, qsc[rows, 2, pr, :], Sin[rows, 0, pr, :], False, False, [r_qsc, r_sin], [PR[2]], tp=tp)
                        MM(o, qsc[rows, 3, pr, :], Sin[rows, 1, pr, :], False, True, [r_qsc, r_sin], [PR[2]], tp=tp)

            def R4():
                o3 = PB[2][:, 0:384].rearrange("p (h e) -> p h e", e=64)
                ACT(tB[:], PB[2][:, 0:384], AF.Square, [PR[2]], [r_tB])
                S.op("dve", lambda h_: h_.reduce_sum(out=st6[:, 1, :], in_=tB[:].rearrange("p (h e) -> p h e", e=64),
                                                     axis=AX.X), [r_tB], [r_st6])
                ACT(st6[:, 4, :], st6[:, 1, :], AF.Sqrt, [r_st6, r_eps], [r_st6], bias=epsc[:], scale=1.0 / 64.0)
                S.op("dve", lambda h_: h_.reciprocal(out=st6[:, 5, :], in_=st6[:, 4, :]), [r_st6], [r_st6])
                tB3 = tB[:].rearrange("p (h e) -> p h e", e=64)
                TT("dve", tB3, o3, st6[:, 5, :].unsqueeze(2).broadcast_to([128, 6, 64]), ALU.mult, [PR[2], r_st6], [r_tB])
                TT("pool", yret[:], tB[:], srg[:], ALU.mult, [r_tB, r_srg], [r_yret])

            def C():
                uT_, gc_ = q["uT"], q["gcT"]
                ru = [q["r_u"][t], q["r_u"][t + 1], q["r_u"][t + 2]]
                b0 = t * 128

                def wb(j):
                    return cp_t[:, l, :, j:j + 1].broadcast_to([128, 2, 128])

                TT("pool", cvt[:], uT_[:, :, b0:b0 + 128], wb(0), ALU.mult, ru + [r_misc], [r_cvt])
                TT("pool", cvu[:], uT_[:, :, b0 + 1:b0 + 129], wb(1), ALU.mult, ru + [r_misc], [r_cvu])
                TT("pool", cvt[:], cvt[:], cvu[:], ALU.add, [r_cvt, r_cvu], [r_cvt])
                TT("pool", cvu[:], uT_[:, :, b0 + 2:b0 + 130], wb(2), ALU.mult, ru + [r_misc], [r_cvu])
                TT("pool", cvt[:], cvt[:], cvu[:], ALU.add, [r_cvt, r_cvu], [r_cvt])
                TT("pool", cvt[:], cvt[:], wb(3), ALU.add, [r_cvt, r_misc], [r_cvt])
                TT("pool", yT[:, 0:2, :], cvt[:], gc_[:, :, t * 128:(t + 1) * 128], ALU.mult, [r_cvt, q["r_gcT"][t]], [r_yTc])

            def Y_ret():
                for c in range(3):
                    TR(PB[7][:, c * 128:(c + 1) * 128], yret[:, c * 128:(c + 1) * 128], identb[:], [r_yret, r_idb],
                       [PR[7]])
                CP("act", yT[:, 2:5, :], PB[7][:, 0:384].rearrange("p (c i) -> p c i", i=128), [PR[7]], [r_yTr])

            def Y_na():
                for c in range(3):
                    TR(PB[7][:, (3 + c) * 128:(4 + c) * 128], yna[:, c * 128:(c + 1) * 128], identb[:], [r_yna, r_idb],
                       [PR[7]])
                CP("act", yT[:, 5:8, :], PB[7][:, 384:768].rearrange("p (c i) -> p c i", i=128), [PR[7]], [r_yTn])

            def O(ks, first, last):
                for j in range(2):
                    for k in ks:
                        rk_ = r_yTc if k < 2 else (r_yTr if k < 5 else r_yTn)
                        MM(PB[5 + j][:, :], yT[:, k, :], Wb[:, k, 1536 + j * 512:1536 + (j + 1) * 512],
                           first and k == ks[0], last and k == ks[-1], [rk_] + WOK(k), [PR[5 + j]])

            def L():
                for j in range(2):
                    TT("dve", zb[:, j * 512:(j + 1) * 512], PB[5 + j][:, :], gate_bc[:, j * 512:(j + 1) * 512], ALU.mult,
                       [PR[5 + j], r_gate], [r_zb])
                STT(zb[:], xt[s][:], ALPHA, zb[:], ALU.mult, ALU.add, [r_xt[s], r_zb], [r_zb])
                S.op("dve", lambda h_: h_.reduce_sum(out=st1[:, 0:1], in_=zb[:], axis=AX.X), [r_zb], [r_st1a])
                MSET("pool", st1[:, 1:3], 0.0, [r_st1])
                ACT(PB[4][:, :], zb[:, 0:512], AF.Square, [r_zb], [PR[4], r_st1], accum=st1[:, 1:2])
                ACT(PB[4][:, :], zb[:, 512:1024], AF.Square, [r_zb], [PR[4], r_st1], accum=st1[:, 2:3])
                TS("pool", st1[:, 0:1], st1[:, 0:1], 1.0 / D, None, ALU.mult, None, [r_st1a], [r_st1a])
                TT("pool", st1[:, 1:2], st1[:, 1:2], st1[:, 2:3], ALU.add, [r_st1], [r_st1])
                TT("pool", st1[:, 2:3], st1[:, 0:1], st1[:, 0:1], ALU.mult, [r_st1a, r_st1], [r_st1])
                TS("pool", st1[:, 1:2], st1[:, 1:2], 1.0 / D, None, ALU.mult, None, [r_st1], [r_st1])
                TT("pool", st1[:, 3:4], st1[:, 1:2], st1[:, 2:3], ALU.subtract, [r_st1], [r_st1])
                ACT(st1[:, 4:5], st1[:, 3:4], AF.Sqrt, [r_st1, r_eps], [r_st1], bias=epsc[:], scale=1.0)
                S.op("dve", lambda h_: h_.reciprocal(out=st1[:, 5:6], in_=st1[:, 4:5]), [r_st1], [r_st1])
                STT(st1[:, 6:7], st1[:, 0:1], -1.0, st1[:, 5:6], ALU.mult, ALU.mult, [r_st1, r_st1a], [r_st1])
                ACT(zb[:], zb[:], AF.Identity, [r_zb, r_st1], [r_zb], bias=st1[:, 6:7], scale=st1[:, 5:6])
                TT("pool", zb[:], zb[:], gbc[:], ALU.mult, [r_zb, r_lnp], [r_zb])
                TT("pool", zb[:], zb[:], bbc[:], ALU.add, [r_zb, r_lnp], [r_zb])
                DMA("sp", dst[t * 128:(t + 1) * 128, :], zb[:], [r_zb], [dst_r[t]] if dst_r is not None else [], "st")

            has_b = nb > 4
            N_S(0, 0)
            R1()
            N_E(0, 0)
            R2()
            if has_b:
                N_S(0, 1)
                N_E(0, 1)
            N_PV(0)
            N_S(1, 0)
            C()
            if not late_ret:
                R3()
            N_E(1, 0)
            if has_b:
                N_S(1, 1)
                N_E(1, 1)
            N_PV(1)
            if not late_ret:
                R4()
            N_S(2, 0)
            N_E(2, 0)
            if has_b:
                N_S(2, 1)
            if not late_ret:
                Y_ret()
            if has_b:
                N_E(2, 1)
            if not late_ret:
                O([0, 1, 2, 3, 4], True, False)
            if hoist is not None:
                hoist()
            N_PV(2)
            if hooks is not None and "post_nbr" in hooks:
                hooks["post_nbr"]()
            if late_ret:
                if hooks is not None and "pre_ret" in hooks:
                    hooks["pre_ret"]()
                R3()
                R4()
                Y_ret()
                O([0, 1, 2, 3, 4], True, False)
            N_F()
            Y_na()
            O([5, 6, 7], False, True)
            if hooks is not None and "o" in hooks:
                hooks["o"]()
            L()

        def exchange_gathers(q, gb, r_gb):
            nk, nv, uT_ = q["nkT"], q["nva"], q["uT"]

            def gather(o, col0, n, which, wr, shape3=None):
                cands = (0, 1, 2) if which == 0 else (1, 2, 3)
                for j, k in enumerate(cands):
                    e = Eb[j % 2]
                    re = r_Eb[j % 2]
                    stg = e[:].rearrange("p a b -> p (a b)")[:, 0:n]
                    DMA("sp", stg, gb[k * 128:(k + 1) * 128, col0:col0 + n], [r_gb], [re], "ex")
                    sv = stg if shape3 is None else stg.rearrange("p (a b) -> p a b", b=shape3)
                    sc = sel_t[:, which * 4 + k:which * 4 + k + 1]
                    if j == 0:
                        TS("dve", o, sv, sc, None, ALU.mult, None, [re, r_misc], wr)
                    else:
                        STT(o, sv, sc, o, ALU.mult, ALU.add, [re, r_misc] + wr, wr)

            items = [
                lambda: gather(nk[:, :, 0:256], OFF_NKT_BOT, 768, 0, [q["r_nk"][0], q["r_nk"][1]], 256),
                lambda: gather(nk[:, :, (NT + 2) * 128:(NT + 4) * 128], OFF_NKT_TOP, 768, 1,
                               [q["r_nk"][NT + 2], q["r_nk"][NT + 3]], 256),
                lambda: gather(nv[:, 0:2, :, :].rearrange("p a h e -> p (a h e)"), OFF_NVA_BOT, 780, 0,
                               [q["r_nv"][0], q["r_nv"][1]]),
                lambda: gather(nv[:, NT + 2:NT + 4, :, :].rearrange("p a h e -> p (a h e)"), OFF_NVA_TOP, 780, 1,
                               [q["r_nv"][NT + 2], q["r_nv"][NT + 3]]),
                lambda: gather(uT_[:, :, 0], OFF_U_LAST, 2, 0, [q["r_u"][0]]),
                lambda: gather(uT_[:, :, NT * 128 + 1], OFF_U_FIRST, 2, 1, [q["r_u"][NT + 1]]),
            ]
            return items

        def exchange(l):
            q = MQ
            pb, gb = packB[l], gathB[l]
            r_pb, r_gb = R("pb"), R("gb")
            pf = pb[:, OFF_ST:OFF_ST + 768].bitcast(F32)
            DMA("sp", pf[:, 0:192].rearrange("p (a b) -> p a b", b=64), q["runf"][:], [q["r_runf"]], [r_pb], "ex")
            DMA("sp", pf[:, 192:384].rearrange("p (a b) -> p a b", b=64), totb[:], [r_totb], [r_pb], "ex")
            nk, nv, uT_ = q["nkT"], q["nva"], q["uT"]
            DMA("sp", pb[:, OFF_NKT_TOP:OFF_NKT_TOP + 768].rearrange("p (a b) -> p a b", b=256), nk[:, :, 2 * 128:4 * 128],
                [q["r_nk"][2], q["r_nk"][3]], [r_pb], "ex")
            DMA("sp", pb[:, OFF_NKT_BOT:OFF_NKT_BOT + 768].rearrange("p (a b) -> p a b", b=256),
                nk[:, :, (NT) * 128:(NT + 2) * 128], [q["r_nk"][NT], q["r_nk"][NT + 1]], [r_pb], "ex")
            DMA("sp", pb[:, OFF_NVA_TOP:OFF_NVA_TOP + 780], nv[:, 2:4, :, :].rearrange("p a h e -> p (a h e)"),
                [q["r_nv"][2], q["r_nv"][3]], [r_pb], "ex")
            DMA("sp", pb[:, OFF_NVA_BOT:OFF_NVA_BOT + 780], nv[:, NT:NT + 2, :, :].rearrange("p a h e -> p (a h e)"),
                [q["r_nv"][NT], q["r_nv"][NT + 1]], [r_pb], "ex")
            MSET("dve", ub[:, 4:8], 0.0, [r_ub])
            CP("dve", ub[:, 0:2], uT_[:, :, 1], [q["r_u"][1]], [r_ub])
            CP("dve", ub[:, 2:4], uT_[:, :, NT * 128], [q["r_u"][NT]], [r_ub])
            DMA("sp", pb[:, OFF_U_FIRST:OFF_U_FIRST + 8], ub[:], [r_ub], [r_pb], "ex")
            S.op("pool", lambda h: h.collective_compute("AllGather", ALU.bypass, replica_groups=RG, ins=[pb[:, :]],
                                                        outs=[gb[:, :]]), [r_pb], [r_gb], dma="cc", inc=1)
            return dict(gb=gb, r_gb=r_gb)

        def exchange_recv_halo(l, X):
            return exchange_gathers(MQ, X["gb"], X["r_gb"])

        def exchange_recv(l, X):
            gb, r_gb = X["gb"], X["r_gb"]
            stgs = [(tA[:], r_tA),
                    (Eb[0][:].rearrange("p a b -> p (a b)")[:, 0:768].bitcast(F32), r_Eb[0]),
                    (Eb[1][:].rearrange("p a b -> p (a b)")[:, 0:768].bitcast(F32), r_Eb[1])]
            for k in range(4):
                stg, rs = stgs[k % 3]
                DMA("sp", stg, gb[k * 128:(k + 1) * 128, OFF_ST:OFF_ST + 768].bitcast(F32), [r_gb], [rs], "ex")
                v4 = stg.rearrange("p (d a b) -> p d a b", d=2, a=3)
                cb_ = coef[:, :, :, k:k + 1].broadcast_to([128, 2, 3, 64])
                if k == 0:
                    TT("dve", sacc, v4, cb_, ALU.mult, [rs, r_coef], [r_sacc])
                else:
                    TT("dve", v4, v4, cb_, ALU.mult, [rs, r_coef], [rs])
                    TT("dve", sacc, sacc, v4, ALU.add, [rs, r_sacc], [r_sacc])
            t4 = tA[:].rearrange("p (d a b) -> p d a b", d=2, a=3)
            for d in range(2):
                s0 = CQ["runf"] if d == 0 else CQ["runb"]
                r_s0 = CQ["r_runf"] if d == 0 else CQ["r_runb"]
                TT("dve", t4[:, d, :, :], s0[:], coef[:, d, :, 4:5].broadcast_to([128, 3, 64]), ALU.mult,
                   [r_s0, r_coef, r_tA], [r_tA])
            TT("dve", sacc, sacc, t4, ALU.add, [r_tA, r_sacc], [r_sacc])
            CP("dve", Sin[:], sacc, [r_sacc], [r_sin])

        def reset_run(q):
            if q is MQ:
                MSET("pool", totb[:], 0.0, [r_totb])
            MSET("pool", q["runf"][:], 0.0, [q["r_runf"]])
            MSET("pool", q["runb"][:], 0.0, [q["r_runb"]])

        try:
            for l in range(2):
                if l == 0:
                    load_w1(0, 0)
                    load_w1(0, 1)
                ck(10 * l + 0)
                if l == 0:
                    mod_setup(0)
                layer_setup(l)
                ck(10 * l + 1)
                csrc, csrc_r = (ctx_in, None) if l == 0 else (xc1, xc1_r)
                msrc, msrc_r = (x_in, None) if l == 0 else (x1, x1_r)
                mdst, mdst_r = (x1, x1_r) if l == 0 else (out, None)
                reset_run(CQ)
                reset_run(MQ)
                LX(msrc, msrc_r, 0, 0, l, rope_on=True)
                for t in range(NT):
                    hz = (lambda t=t: LX(msrc, msrc_r, t + 1, 0, l, rope_on=True)) if t + 1 < NT else \
                        (lambda: LX(csrc, csrc_r, 0, 1))
                    P1(MQ, t, msrc, msrc_r, 0, True, hoist=hz)
                    if l == 0 and t < 12:
                        mod1_piece(2 * t)
                        mod1_piece(2 * t + 1)
                    if l == 0 and t == 12:
                        mod_finish(1, acc1)
                ck(10 * l + 2)
                X = exchange(l)
                for t in range(NCX):
                    hz = (lambda t=t: LX(csrc, csrc_r, t + 1, 1)) if t + 1 < NCX else None
                    P1(CQ, t, csrc, csrc_r, 1, False, hoist=hz)
                ck(10 * l + 3)
                P1_finish(CQ)
                if l == 1:
                    P1_finish(MQ)
                load_w2(l)
                ck(10 * l + 4)
                ck(10 * l + 5)
                if l == 0:
                    build_gate(1)
                    ck(5.1)
                    LX(csrc, csrc_r, 0, 1)
                    for t in range(NCX):
                        hz = (lambda t=t: LX(csrc, csrc_r, t + 1, 1)) if t + 1 < NCX else None
                        P2(CQ, t, xc1, xc1_r, l, False, hoist=hz)
                if l == 0:
                    P1_finish(MQ)
                ck(10 * l + 6)
                build_gate(0)
                order = [2, 3, 4, 5, 6, 7, 8, 9, 10, 11, 12, 13, 0, 1, 14, 15]
                LX(msrc, msrc_r, order[0], 0, l, rope_on=True, bias_on=True)
                halo_items = exchange_recv_halo(l, X)
                for i, t in enumerate(order):
                    hz = (lambda tn=order[i + 1]: LX(msrc, msrc_r, tn, 0, l, rope_on=True, bias_on=True)) \
                        if i + 1 < NT else None
                    hk = None
                    if i == 0:
                        hk = {"pre_ret": (lambda: exchange_recv(l, X))}
                    if 1 <= i <= len(halo_items):
                        hk = {"post_nbr": halo_items[i - 1]}
                    if l == 0 and i == NT - 1:
                        hk = {"a2": (lambda: load_w1(1, 0)), "o": (lambda: load_w1(1, 1))}
                    P2(MQ, t, mdst, mdst_r, l, True, hoist=hz, hooks=hk, late_ret=(i == 0))
                    ck(10 * l + 7)
                ck(10 * l + 8)
        except _Stop:
            pass
        S.finish()
        S.emit_all()
        print("ops", S.nops, "sems", S.nsem)
    return nc


def _host_tables():
    P = np.arange(128)
    I = np.arange(128)
    cst = np.zeros((128, 5 * 128 + 2), np.float32)
    cst[:, 0:128] = np.eye(128)
    diff = I[None, :] - P[:, None]
    BIG = 1.0e6
    cst[:, 128:256] = np.where(diff >= 0, diff, BIG)
    cst[:, 256:384] = np.where(diff < 0, -diff, BIG)
    cst[:, 384:512] = (I + 1)[None, :]
    cst[:, 512:640] = (128 - I)[None, :]
    cst[:, 640] = 127 - P
    cst[:, 641] = P
    return cst


def _rope_tables(rank):
    nf = 16
    inv = (10000.0 ** (-np.arange(nf, dtype=np.float64) / nf))
    out = np.zeros((NT, 128, 128), np.float32)
    for t in range(NT):
        p = np.arange(128)
        row = (32 * rank + 2 * t + p // 64).astype(np.float64)
        col = (p % 64).astype(np.float64)
        ar = (row[:, None].astype(np.float32) * inv.astype(np.float32)[None, :]).astype(np.float64)
        ac = (col[:, None].astype(np.float32) * inv.astype(np.float32)[None, :]).astype(np.float64)
        cr, sr, cc, sc = np.cos(ar), np.sin(ar), np.cos(ac), np.sin(ac)
        out[t, :, 0:64] = np.concatenate([cr, cr, cc, cc], 1)
        out[t, :, 64:128] = np.concatenate([-sr, sr, -sc, sc], 1)
    return out


def _bias_tables(rpb, rank):
    out = np.full((NT, 128, 6, 6, 128), NEG, np.float32)
    j = np.arange(128)
    i = np.arange(128)
    rkl, ck = j // 64, j % 64
    rql, cq = i // 64, i % 64
    cstart = np.clip(cq - 8, 0, 48)
    colok = (ck[:, None] >= cstart[None, :]) & (ck[:, None] < cstart[None, :] + 16)
    dcol = ck[:, None] - cq[None, :] + 15
    for t in range(NT):
        dlo = -3 if t == NT - 1 else -2
        dhi = 3 if t == 0 else 2
        rq = 32 * rank + 2 * t + rql
        rstart = np.clip(rq - 4, 0, 120)
        for dt in range(dlo, dhi + 1):
            b = dt - dlo
            rk = 32 * rank + 2 * (t + dt) + rkl
            rowok = (rk[:, None] >= rstart[None, :]) & (rk[:, None] < rstart[None, :] + 8) & \
                    (rk[:, None] >= 0) & (rk[:, None] < 128)
            ok = rowok & colok
            drow = np.clip(rk[:, None] - rq[None, :] + 7, 0, 14)
            dc = np.clip(dcol, 0, 30)
            vals = rpb[:, drow, dc]
            blk = np.where(ok[None], vals, NEG)
            out[t, :, :, b, :] = blk.transpose(1, 0, 2)
    out = out.reshape(NT, 128, 3, 2, 6, 128).transpose(0, 1, 2, 4, 3, 5)
    return np.ascontiguousarray(out).reshape(NT, 128, 6 * 6 * 128).astype(ml_dtypes.bfloat16)


_NC_CACHE = {}


def kernel(x, c, ctx, c_ctx, w_mod, b_mod, w_in, conv_w, conv_b, ret_decay, na_rpb, w_out, ln_g, ln_b):
    f32 = np.float32
    x = np.asarray(x, f32)
    c = np.asarray(c, f32)
    ctx = np.asarray(ctx, f32)
    c_ctx = np.asarray(c_ctx, f32)
    w_mod = np.ascontiguousarray(np.asarray(w_mod, f32))
    b_mod = np.asarray(b_mod, f32)
    w_in = np.ascontiguousarray(np.asarray(w_in, f32))
    w_out = np.ascontiguousarray(np.asarray(w_out, f32))
    conv_w = np.asarray(conv_w, f32)
    conv_b = np.asarray(conv_b, f32)
    ret_decay = np.asarray(ret_decay, f32)
    na_rpb = np.asarray(na_rpb, f32)
    ln_g = np.asarray(ln_g, f32)
    ln_b = np.asarray(ln_b, f32)

    if "nc" not in _NC_CACHE:
        _NC_CACHE["nc"] = build_program()
    nc = _NC_CACHE["nc"]

    cst = _host_tables()
    bmodT = np.ascontiguousarray(b_mod.reshape(2, 24, 128).transpose(2, 0, 1))
    convp = np.zeros((128, 2, 2, 4), f32)
    for l in range(2):
        for ch in range(2):
            convp[:, l, ch, 0:3] = conv_w[l, :, ch * 128:(ch + 1) * 128].T
            convp[:, l, ch, 3] = conv_b[l, ch * 128:(ch + 1) * 128]
    dec = np.zeros((128, 36), f32)
    dec[:, 0:24] = ret_decay.reshape(1, 24)
    for l in range(2):
        for d in range(2):
            for pr in range(3):
                dec[0:64, 24 + l * 6 + d * 3 + pr] = ret_decay[l, d, 2 * pr]
                dec[64:128, 24 + l * 6 + d * 3 + pr] = ret_decay[l, d, 2 * pr + 1]
    lnp = np.zeros((2, 2, 128, D), f32)
    lnp[:, 0] = ln_g[:, None, :]
    lnp[:, 1] = ln_b[:, None, :]

    in_maps = []
    for core in range(8):
        b, r = core // 4, core % 4
        cvec = np.zeros((128, 8, 2), f32)
        cvec[:, :, 0] = c[b].reshape(8, 128).T
        cvec[:, :, 1] = c_ctx.reshape(8, 128).T
        rankc = np.zeros((128, 20), f32)
        for k in range(4):
            if k < r:
                rankc[:, 0 + k] = 2048.0 * (r - k - 1)
                rankc[:, 10 + k] = 1.0
            if k > r:
                rankc[:, 5 + k] = 2048.0 * (k - r - 1)
                rankc[:, 15 + k] = 1.0
        rankc[:, 4] = 2048.0 * r
        rankc[:, 14] = 1.0
        rankc[:, 9] = 2048.0 * (3 - r)
        rankc[:, 19] = 1.0
        edge = np.zeros((128, 2), f32)
        edge[:, 0] = 1.0 if r > 0 else 0.0
        edge[:, 1] = 1.0 if r < 3 else 0.0
        sel = np.zeros((128, 8), f32)
        if r > 0:
            sel[:, r - 1] = 1.0
        if r < 3:
            sel[:, 4 + r + 1] = 1.0
        idx = np.zeros((128, 2), np.int32)
        idx[:, 0] = (r - 1 if r > 0 else r) * 128 + np.arange(128)
        idx[:, 1] = (r + 1 if r < 3 else r) * 128 + np.arange(128)
        nab = np.stack([_bias_tables(na_rpb[l], r) for l in range(2)], 0)
        in_maps.append({
            "x": np.ascontiguousarray(x[b, r * 2048:(r + 1) * 2048, :]),
            "ctx": np.ascontiguousarray(ctx[b]),
            "cvec": cvec, "w_mod": w_mod, "bmodT": bmodT, "w_in": w_in, "w_out": w_out, "convp": convp,
            "dec": dec, "nabias": nab, "lnp": lnp, "rope": _rope_tables(r), "cst": cst, "rankc": rankc,
            "edge": edge, "idx": idx, "sel": sel,
        })
    res = run_bass_kernel_spmd(nc, in_maps, core_ids=list(range(8)))
    out = np.zeros((2, 8192, D), f32)
    for core in range(8):
        b, r = core // 4, core % 4
        out[b, r * 2048:(r + 1) * 2048, :] = np.asarray(res.results[core]["out"], f32)
    if DEBUG:
        kernel.debug = res.results
    return out
```

```python
import numpy as np
import ml_dtypes
from contextlib import ExitStack
import concourse.bass as bass
import concourse.mybir as mybir
from concourse.bass_utils import run_bass_kernel_spmd

F32 = mybir.dt.float32
BF16 = mybir.dt.bfloat16
I32 = mybir.dt.int32
ALU = mybir.AluOpType
AF = mybir.ActivationFunctionType
AX = mybir.AxisListType

DEBUG = False
NT = 16
NCX = 2
D = 1024
ALPHA = float((2 * 2) ** 0.25)
LN_EPS = 1e-5
NEG = -30000.0
PKB = 3104 + 768
OFF_ST = 3104
OFF_NKT_TOP, OFF_NKT_BOT, OFF_NVA_TOP, OFF_NVA_BOT, OFF_U_FIRST, OFF_U_LAST = 0, 768, 1536, 2316, 3096, 3098
C_RK, C_RV, C_NK, C_NV, C_RQ, C_RG, C_NQ, C_NG, C_CH, C_CB, C_CCG, C_CZ = (
    0, 384, 768, 1152, 1536, 1920, 2304, 2688, 3072, 3328, 3584, 3840)


class Reg:
    __slots__ = ("name", "w", "rs", "excl")

    def __init__(self, name="", excl=False):
        self.name = name
        self.w = None
        self.rs = {}
        self.excl = excl


class Sched:
    ENG = ("pe", "act", "dve", "pool", "sp")
    LIMIT = 2000
    NDMA = 6

    def __init__(self, nc, stack):
        self.nc = nc
        self.stack = stack
        self.prog = {e: [] for e in self.ENG}
        self.state = {}
        self.waited = {e: {} for e in self.ENG}
        self.nsem = 0
        self.nops = 0
        self.dcount = {}

    def _tick(self, key, inc):
        st = self.state.get(key)
        if st is None or st[1] + inc > self.LIMIT:
            s = self.stack.enter_context(self.nc.semaphore(f"s{self.nsem}_{key}"))
            self.nsem += 1
            st = [s, 0]
            self.state[key] = st
            self.allsems.append(st)
        st[1] += inc
        return st[0], st[1]

    allsems = None

    def op(self, eng, fn, reads=(), writes=(), dma=None, inc=None):
        if self.allsems is None:
            self.allsems = []
        deps = {}

        def add(tok):
            if tok is None:
                return
            e, s, v = tok
            if e == "pe" and eng == "pe" and dma is None:
                return
            k = id(s)
            if k not in deps or deps[k][1] < v:
                deps[k] = (s, v)

        for r in reads:
            add(r.w)
            if r.excl:
                for t in r.rs.values():
                    if t[0] != eng:
                        add(t)
        for w in writes:
            add(w.w)
            for t in w.rs.values():
                add(t)
        wd = self.waited[eng]
        waits = []
        for k, (s, v) in deps.items():
            if wd.get(k, 0) < v:
                wd[k] = v
                waits.append((s, v))
        isdma = dma is not None
        if inc is None:
            inc = 16 if isdma else 1
        if isdma:
            i = self.dcount.get(dma, 0)
            self.dcount[dma] = i + 1
            slot = i % self.NDMA
            sem, val = self._tick(f"dma_{dma}_{slot}", inc)
            if val > inc and wd.get(id(sem), 0) < val - inc:
                wd[id(sem)] = val - inc
                waits.append((sem, val - inc))
        else:
            sem, val = self._tick(eng, inc)
        tok = (None if isdma else eng, sem, val)

        def emit(h, waits=waits, fn=fn, sem=sem, inc=inc):
            for s, v in waits:
                h.wait_ge(s, v)
            fn(h).then_inc(sem, inc)

        self.prog[eng].append(emit)
        self.nops += 1
        for r in reads:
            k = id(sem)
            old = r.rs.get(k)
            if old is None or old[2] < val:
                r.rs[k] = tok
        for w in writes:
            w.w = tok
            w.rs = {}
        return tok

    def finish(self):
        fin = [(st[0], st[1]) for st in self.allsems]

        def emit(h, fin=fin):
            for s, v in fin:
                h.wait_ge(s, v)

        self.prog["sp"].append(emit)

    def emit_all(self):
        nc = self.nc
        prog = self.prog
        with nc.Block() as block:
            @block.tensor
            def _(h):
                for f in prog["pe"]:
                    f(h)

            @block.scalar
            def _(h):
                for f in prog["act"]:
                    f(h)

            @block.vector
            def _(h):
                for f in prog["dve"]:
                    f(h)

            @block.gpsimd
            def _(h):
                for f in prog["pool"]:
                    f(h)

            @block.sync
            def _(h):
                for f in prog["sp"]:
                    f(h)


def build_program():
    nc = bass.Bass("TRN2", target_bir_lowering=False)

    def din(name, shape, dt=F32):
        return nc.dram_tensor(name, list(shape), dt, kind="ExternalInput").ap()

    x_in = din("x", [NT * 128, D])
    ctx_in = din("ctx", [NCX * 128, D])
    cvec = din("cvec", [128, 8, 2])
    w_mod = din("w_mod", [2, D, 3 * D])
    bmodT = din("bmodT", [128, 2, 24])
    w_in = din("w_in", [2, D, 4096])
    w_out = din("w_out", [2, D, D])
    convp = din("convp", [128, 2, 2, 4])
    dec = din("dec", [128, 36])
    nabias = din("nabias", [2, NT, 128, 6 * 6 * 128], BF16)
    lnp = din("lnp", [2, 2, 128, D])
    rope = din("rope", [NT, 128, 128])
    cst = din("cst", [128, 5 * 128 + 2])
    rankc = din("rankc", [128, 20])
    edge = din("edge", [128, 2])
    selin = din("sel", [128, 8])
    idx = din("idx", [128, 2], I32)
    out = nc.dram_tensor("out", [NT * 128, D], F32, kind="ExternalOutput").ap()
    okind = "ExternalOutput" if DEBUG else "Internal"
    x1 = nc.dram_tensor("x1", [NT * 128, D], F32, kind=okind).ap()
    xc1 = nc.dram_tensor("xc1", [NCX * 128, D], F32, kind=okind).ap()
    packF = [nc.dram_tensor(f"packF{l}", [128, 384], F32).ap() for l in range(2)]
    gathF = [nc.dram_tensor(f"gathF{l}", [512, 384], F32).ap() for l in range(2)]
    packB = [nc.dram_tensor(f"packB{l}", [128, PKB], BF16).ap() for l in range(2)]
    gathB = [nc.dram_tensor(f"gathB{l}", [512, PKB], BF16).ap() for l in range(2)]
    RG = [[0, 1, 2, 3], [4, 5, 6, 7]]

    with ExitStack() as st:
        S = Sched(nc, st)

        def sb(name, shape, dt=F32):
            return st.enter_context(nc.sbuf_tensor(name, list(shape), dt))

        def R(n=""):
            return Reg(n)

        def Rs(n, k):
            return [Reg(f"{n}{i}") for i in range(k)]

        import os
        KSTOP = float(os.environ.get("KSTOP", "999"))

        class _Stop(Exception):
            pass

        def ck(n):
            if n >= KSTOP:
                raise _Stop()

        def MM(o, lhsT, rhs, start, stop, rd, wr, tp=None):
            if tp is None:
                S.op("pe", lambda h: h.matmul(o, lhsT, rhs, start=start, stop=stop), rd, wr)
            else:
                S.op("pe", lambda h: h.matmul(o, lhsT, rhs, start=start, stop=stop, tile_position=tp), rd, wr)

        def TR(o, i, ident, rd, wr):
            S.op("pe", lambda h: h.transpose(o, i, ident), rd, wr)

        def ACT(o, i, func, rd, wr, bias=None, scale=None, accum=None):
            kw = {}
            if bias is not None:
                kw["bias"] = bias
            if scale is not None:
                kw["scale"] = scale
            if accum is not None:
                kw["accum_out"] = accum
            S.op("act", lambda h: h.activation(out=o, in_=i, func=func, **kw), rd, wr)

        def TT(e, o, a, b, op, rd, wr):
            S.op(e, lambda h: h.tensor_tensor(out=o, in0=a, in1=b, op=op), rd, wr)

        def TS(e, o, a, s1, s2, op0, op1, rd, wr):
            if s2 is None:
                S.op(e, lambda h: h.tensor_scalar(out=o, in0=a, scalar1=s1, scalar2=None, op0=op0), rd, wr)
            else:
                S.op(e, lambda h: h.tensor_scalar(out=o, in0=a, scalar1=s1, scalar2=s2, op0=op0, op1=op1), rd, wr)

        def STT(o, a, s, b, op0, op1, rd, wr):
            S.op("dve", lambda h: h.scalar_tensor_tensor(out=o, in0=a, scalar=s, in1=b, op0=op0, op1=op1), rd, wr)

        def CP(e, o, i, rd, wr):
            if e == "act":
                S.op("act", lambda h: h.copy(out=o, in_=i), rd, wr)
            else:
                S.op(e, lambda h: h.tensor_copy(out=o, in_=i), rd, wr)

        def MSET(e, ap, v, wr):
            S.op(e, lambda h: h.memset(ap, v), [], wr)

        def DMA(e, o, i, rd, wr, stream):
            S.op(e, lambda h: h.dma_start(out=o, in_=i), rd, wr, dma=stream)

        PB = [st.enter_context(nc.psum_tensor(f"pb{i}", [128, 512], F32)) for i in range(7)]
        PB.append(st.enter_context(nc.psum_tensor("pb7", [128, 1024], BF16)))
        PR = [Reg(f"pb{i}", excl=True) for i in range(8)]

        cs = sb("cs", [128, 5 * 128 + 2])
        r_cs = R("cs")
        DMA("sp", cs[:], cst[:, :], [], [r_cs], "ld")
        identf = cs[:, 0:128]
        DFt = cs[:, 128:256]
        DBt = cs[:, 256:384]
        ip1 = cs[:, 384:512]
        rev = cs[:, 512:640]
        jrev = cs[:, 640:641]
        jpos = cs[:, 641:642]
        identb = sb("identb", [128, 128], BF16)
        r_idb = R("idb")
        CP("dve", identb[:], identf, [r_cs], [r_idb])
        onesf = sb("onesf", [128, 128])
        r_ones = R("ones")
        MSET("pool", onesf[:], 1.0, [r_ones])
        epsc = sb("epsc", [128, 1])
        r_eps = R("eps")
        MSET("pool", epsc[:], LN_EPS, [r_eps])
        rk_t = sb("rk_t", [128, 20])
        eg_t = sb("eg_t", [128, 2])
        ix_t = sb("ix_t", [128, 2], I32)
        cp_t = sb("cp_t", [128, 2, 2, 4])
        r_misc = R("misc")
        DMA("sp", rk_t[:], rankc[:, :], [], [r_misc], "ld")
        DMA("sp", eg_t[:], edge[:, :], [], [r_misc], "ld")
        sel_t = sb("sel_t", [128, 8])
        DMA("sp", sel_t[:], selin[:, :], [], [r_misc], "ld")
        DMA("sp", ix_t[:], idx[:, :], [], [r_misc], "ld")
        DMA("sp", cp_t[:], convp[:, :, :, :], [], [r_misc], "ld")
        dc = sb("dc", [128, 36])
        lg = sb("lg", [128, 36])
        r_lg = R("lg")
        DMA("sp", dc[:], dec[:, :], [], [r_lg], "ld")
        ACT(lg[:], dc[:], AF.Exp, [r_lg], [r_lg], scale=-1.0)
        TS("dve", lg[:], lg[:], 1.0, None, ALU.add, None, [r_lg], [r_lg])
        ACT(lg[:], lg[:], AF.Ln, [r_lg], [r_lg])
        TS("dve", lg[:], lg[:], -1.0, None, ALU.mult, None, [r_lg], [r_lg])

        def lg_bc(l, d, h):
            c = l * 12 + d * 6 + h
            return lg[:, c:c + 1]

        def lg_pp(l, d, pr):
            c = 24 + l * 6 + d * 3 + pr
            return lg[:, c:c + 1]

        T128 = sb("T128", [128, 2, 16])
        r_t128 = R("t128")
        TS("dve", T128[:, 0, :], ip1[:, 0:16], -1.0, 128.0, ALU.add, ALU.mult, [r_cs], [r_t128])
        TS("dve", T128[:, 1, :], ip1[:, 0:16], -16.0, -128.0, ALU.add, ALU.mult, [r_cs, r_t128], [r_t128])

        ca = sb("ca", [128, 8, 2])
        r_ca = R("ca")
        DMA("sp", ca[:], cvec[:, :, :], [], [r_ca], "ld")
        ACT(ca[:], ca[:], AF.Silu, [r_ca], [r_ca])
        cab = sb("cab", [128, 8, 2], BF16)
        CP("dve", cab[:], ca[:], [r_ca], [r_ca])

        Wb = sb("Wb", [128, 8, 2560], BF16)
        WRK = [[Reg(f"W{g}_{kp}") for kp in range(4)] for g in range(8)]

        def WOK(k):
            return [WRK[g][k // 2] for g in range(4, 8)]

        def mkseq(name, nt, nslot, halo):
            q = dict(name=name, nt=nt, nslot=nslot, halo=halo)
            q["kT"] = sb(name + "kT", [128, 3, nt * 128], BF16)
            q["vret"] = sb(name + "vret", [128, nt, 384], BF16)
            q["nkT"] = sb(name + "nkT", [128, 3, nslot * 128], BF16)
            q["nva"] = sb(name + "nva", [128, nslot, 6, 65], BF16)
            q["sfst"] = sb(name + "sfst", [128, nt, 3, 64], BF16)
            q["tbst"] = sb(name + "tbst", [128, nt, 3, 64], BF16)
            q["uT"] = sb(name + "uT", [128, 2, nt * 128 + 2], BF16)
            q["gcT"] = sb(name + "gcT", [128, 2, nt * 128], BF16)
            q["runf"] = sb(name + "runf", [128, 3, 64])
            q["runb"] = sb(name + "runb", [128, 3, 64])
            for k in ("kT", "vret", "sfst", "tbst", "gcT"):
                q["r_" + k] = Rs(name + k, nt)
            q["r_nk"] = Rs(name + "nk", nslot)
            q["r_nv"] = Rs(name + "nv", nslot)
            q["r_u"] = Rs(name + "u", nt + 2)
            q["r_runf"] = R(name + "runf")
            q["r_runb"] = R(name + "runb")
            return q

        MQ = mkseq("m", NT, NT + 4, 2)
        CQ = mkseq("c", NCX, NCX, 0)
        MSET("pool", MQ["nva"][:], 1.0, MQ["r_nv"])
        MSET("pool", CQ["nva"][:], 1.0, CQ["r_nv"])
        MSET("pool", CQ["uT"][:], 0.0, CQ["r_u"])
        MSET("pool", MQ["uT"][:], 0.0, MQ["r_u"])

        biasb = sb("biasb", [128, 3, 6, 2, 128], BF16)
        r_bias = R("bias")
        gbc = sb("gbc", [128, D])
        bbc = sb("bbc", [128, D])
        r_lnp = R("lnp")
        gate_bc = sb("gate_bc", [128, D])
        r_gate = R("gate")
        modTs = [sb(f"modT{i}", [128, 24, 2]) for i in range(2)]
        r_mods = Rs("mod", 2)
        cur = {"l": 0}
        Dcomb = sb("Dcomb", [128, 6, 128])
        r_dc = R("dcomb")
        WQ = sb("WQ", [128, 2, 3, 128])
        r_wq = R("wq")
        WK = sb("WK", [128, 2, 6])
        r_wk = R("wk")
        G128 = sb("G128", [128, 2, 3])
        r_g128 = R("g128")
        gpow = sb("gpow", [128, 3, 3, 16])
        r_gpow = R("gpow")
        coef = sb("coef", [128, 2, 3, 5])
        r_coef = R("coef")
        Sin = sb("Sin", [128, 2, 3, 64], BF16)
        r_sin = R("sin")

        xt = [sb(f"xt{i}", [128, D]) for i in range(2)]
        r_xt = Rs("xt", 2)
        hT = [sb(f"hT{i}", [128, 8, 128], BF16) for i in range(2)]
        r_hT = Rs("hT", 2)
        ropet = [sb(f"ropet{i}", [128, 128]) for i in range(2)]
        r_rope = Rs("rope", 2)
        tA = sb("tA", [128, 384])
        tB = sb("tB", [128, 384])
        r_tA, r_tB = R("tA"), R("tB")
        krot = sb("krot", [128, 384], BF16)
        r_krot = R("krot")
        cvt = sb("cvt", [128, 2, 128])
        r_cvt = R("cvt")
        cvu = sb("cvu", [128, 2, 128])
        r_cvu = R("cvu")
        qT = sb("qT", [128, 3, 128], BF16)
        r_qT = R("qT")
        qsc = sb("qsc", [128, 4, 3, 128], BF16)
        r_qsc = R("qsc")
        kw = qsc[:, 0:2, :, :].rearrange("p d a b -> p d (a b)")
        r_kw = r_qsc
        nqTm = sb("nqTm", [128, 3, 2, 128], BF16)
        r_nqT = R("nqT")
        MSET("pool", nqTm[:], 0.0, [r_nqT])
        srg = sb("srg", [128, 384], BF16)
        sng = sb("sng", [128, 384], BF16)
        r_srg, r_sng = R("srg"), R("sng")
        AT = sb("AT", [128, 6, 128], BF16)
        r_AT = R("AT")
        st6 = sb("st6", [128, 8, 6])
        r_st6 = R("st6")
        r_st6a = R("st6a")
        r_st6c = R("st6c")
        r_st6b = R("st6b")
        yret = sb("yret", [128, 384], BF16)
        yna = sb("yna", [128, 384], BF16)
        r_yret, r_yna = R("yret"), R("yna")
        Eb = [sb(f"Eb{i}", [128, 8, 128], BF16) for i in range(2)]
        r_Eb = Rs("Eb", 2)
        yT = sb("yT", [128, 8, 128], BF16)
        r_yTc, r_yTr, r_yTn = R("yTc"), R("yTr"), R("yTn")
        zb = sb("zb", [128, D])
        r_zb = R("zb")
        st1 = sb("st1", [128, 8])
        r_st1 = R("st1")
        r_st1a = R("st1a")
        tmp64 = tA[:, 0:192].rearrange("p (a b) -> p a b", b=64)
        r_tmp64 = r_tA
        sacc = tB[:].rearrange("p (d a b) -> p d a b", d=2, a=3)
        r_sacc = r_tB
        tmp2 = tB[:, 0:192].rearrange("p (a b) -> p a b", b=64)
        r_tmp2 = r_tB
        totb = sb("totb", [128, 3, 64])
        r_totb = R("totb")
        ub = sb("ub", [128, 8], BF16)
        ubg = sb("ubg", [128, 4], BF16)
        r_ub, r_ubg = R("ub"), R("ubg")

        x1_r = Rs("x1d", NT)
        xc1_r = Rs("xc1d", NCX)

        def mod_piece(l, fc, stg_pair, bst_pair, ce, acc, dq="sp"):
            wsrc = w_mod[l].rearrange("(k p) c -> p k c", p=128)
            bt, rb = stg_pair
            bb, rbb = bst_pair
            b = bt[:].rearrange("p (k c) -> p k c", c=128)
            DMA(dq, b, wsrc[:, :, fc * 128:(fc + 1) * 128], [], [rb], "wm" + dq)
            CP(ce, bb[:], b, [rb], [rbb])
            accap, accr = acc
            for k in range(8):
                MM(accap[:, fc * 2:fc * 2 + 2], bb[:, k, :], cab[:, k, :], k == 0, k == 7, [rbb, r_ca], [accr])

        def mod_finish(l, acc):
            modT, r_mod = modTs[l], r_mods[l]
            accap, accr = acc
            bm = sb(f"bm{l}", [128, 24])
            r_bm = R("bm")
            DMA("sp", bm[:], bmodT[:, l, :], [], [r_bm], "ld")
            TT("dve", modT[:], accap[:, 0:48].rearrange("p (a b) -> p a b", b=2),
               bm[:].unsqueeze(2).broadcast_to([128, 24, 2]), ALU.add, [accr, r_bm], [r_mod])
            TS("dve", modT[:, 8:16, :], modT[:, 8:16, :], 1.0, None, ALU.add, None, [r_mod], [r_mod])

        def mod_setup(l):
            stg = [(xt[0], r_xt[0]), (xt[1], r_xt[1]), (zb, r_zb), (gate_bc, r_gate)]
            bst = [(hT[0], r_hT[0]), (hT[1], r_hT[1]), (Eb[0], r_Eb[0]), (Eb[1], r_Eb[1])]
            ceng = ("dve", "pool", "dve", "pool")
            acc = (PB[6][:, 0:48], PR[6])
            for fc in range(24):
                mod_piece(l, fc, stg[fc % 4], bst[fc % 4], ceng[fc % 4], acc, dq=("sp", "act")[fc % 2])
            mod_finish(l, acc)

        acc1 = (PB[7][:].bitcast(F32)[:, 256:304], PR[7])

        def mod1_piece(fc):
            stg = [(zb, r_zb), (gate_bc, r_gate)]
            bst = [(Eb[0], r_Eb[0]), (Eb[1], r_Eb[1])]
            mod_piece(1, fc, stg[fc % 2], bst[fc % 2], ("act", "dve")[fc % 2], acc1)

        def layer_setup(l):
            cur["l"] = l
            DMA("sp", gbc[:], lnp[l, 0, :, :], [], [r_lnp], "ld")
            DMA("sp", bbc[:], lnp[l, 1, :, :], [], [r_lnp], "ld")
            for h in range(6):
                ACT(tA[:, 0:128], DFt, AF.Exp, [r_cs, r_lg], [r_tA], scale=lg_bc(l, 0, h))
                ACT(tB[:, 0:128], DBt, AF.Exp, [r_cs, r_lg], [r_tB], scale=lg_bc(l, 1, h))
                TT("dve", Dcomb[:, (h % 2) * 3 + h // 2, :], tA[:, 0:128], tB[:, 0:128], ALU.add, [r_tA, r_tB], [r_dc])
            TS("dve", Dcomb[:], Dcomb[:], 0.125, None, ALU.mult, None, [r_dc], [r_dc])
            for pr in range(3):
                ACT(WQ[:, 0, pr, :], ip1, AF.Exp, [r_cs, r_lg], [r_wq], scale=lg_pp(l, 0, pr))
                ACT(WQ[:, 1, pr, :], rev, AF.Exp, [r_cs, r_lg], [r_wq], scale=lg_pp(l, 1, pr))
                ACT(gpow[:, 0, pr, :], T128[:, 0, :], AF.Exp, [r_t128, r_lg], [r_gpow], scale=lg_pp(l, 0, pr))
                ACT(gpow[:, 1, pr, :], T128[:, 1, :], AF.Exp, [r_t128, r_lg], [r_gpow], scale=lg_pp(l, 1, pr))
                ACT(gpow[:, 2, pr, :], T128[:, 0, :], AF.Exp, [r_t128, r_lg], [r_gpow], scale=lg_pp(l, 1, pr))
                for d in range(2):
                    ACT(coef[:, d, pr, :], rk_t[:, d * 5:d * 5 + 5], AF.Exp, [r_misc, r_lg], [r_coef],
                        scale=lg_pp(l, d, pr))
                    TT("dve", coef[:, d, pr, :], coef[:, d, pr, :], rk_t[:, 10 + d * 5:15 + d * 5], ALU.mult,
                       [r_coef, r_misc], [r_coef])
            for d in range(2):
                ACT(G128[:, d, :], lg[:, 24 + l * 6 + d * 3:24 + l * 6 + d * 3 + 3], AF.Exp, [r_lg], [r_g128],
                    scale=128.0)
                TS("dve", WK[:, d, :], lg[:, l * 12 + d * 6:l * 12 + d * 6 + 6], jrev if d == 0 else jpos, None,
                   ALU.mult, None, [r_lg, r_cs], [r_wk])
            ACT(WK[:], WK[:], AF.Exp, [r_wk], [r_wk])
            TS("dve", WK[:], WK[:], 0.125, None, ALU.mult, None, [r_wk], [r_wk])

        def build_gate(m):
            for k in range(8):
                TS("dve", tB[:, 0:128], identf, modTs[cur["l"]][:, 16 + k, m:m + 1], None, ALU.mult, None,
                   [r_cs, r_mods[cur["l"]]], [r_tB])
                MM(PB[5][:, (k % 4) * 128:(k % 4 + 1) * 128], onesf[:], tB[:, 0:128], True, True, [r_ones, r_tB], [PR[5]])
                if k % 4 == 3:
                    CP("act", gate_bc[:, (k // 4) * 512:(k // 4 + 1) * 512], PB[5][:, :], [PR[5]], [r_gate])

        W1_GROUPS = ((C_RK, 384, 0, 0), (C_RV, 384, 384, 1), (C_NV, 384, 768, 2), (C_NK, 384, 1152, 3),
                     (C_CH, 256, 1536, 4), (C_CCG, 256, 1792, 5), (C_CB, 256, 2048, 6), (C_CZ, 256, 2304, 7))

        def load_w1(l, part):
            src = w_in[l].rearrange("(k p) c -> p k c", p=128)
            for (c0, n, o, g) in (W1_GROUPS[0:4] if part == 0 else W1_GROUPS[4:8]):
                DMA("pool", Wb[:, :, o:o + n], src[:, :, c0:c0 + n], [], WRK[g], "w")

        def load_w2(l):
            src = w_in[l].rearrange("(k p) c -> p k c", p=128)
            if l == 0:
                for (c0, n, o, g) in ((C_RQ, 384, 0, 0), (C_NQ, 384, 1152, 3), (C_RG, 384, 384, 1), (C_NG, 384, 768, 2)):
                    DMA("pool", Wb[:, :, o:o + n], src[:, :, c0:c0 + n], [], WRK[g], "w")
                srco0 = w_out[l].rearrange("(k p) c -> p k c", p=128)
                DMA("pool", Wb[:, :, 1536:2560], srco0[:, :, :], [], [r for g in range(4, 8) for r in WRK[g]], "w")
                return
            stg = [(xt[0], r_xt[0]), (xt[1], r_xt[1]), (zb, r_zb), (gate_bc, r_gate)]
            ceng = ("pool", "act", "dve", "pool")
            i = 0
            for (c0, n, o, g) in ((C_RQ, 384, 0, 0), (C_NQ, 384, 1152, 3), (C_RG, 384, 384, 1), (C_NG, 384, 768, 2)):
                for kp in range(4):
                    bt, rb = stg[i % 4]
                    v = bt[:, 0:768].rearrange("p (k c) -> p k c", c=384)
                    DMA("sp", v, src[:, 2 * kp:2 * kp + 2, c0:c0 + n], [], [rb], "w")
                    CP(ceng[i % 4], Wb[:, 2 * kp:2 * kp + 2, o:o + n], v, [rb], [WRK[g][kp]])
                    i += 1
            srco = w_out[l].rearrange("(k p) c -> p k c", p=128)
            for k in range(8):
                bt, rb = stg[i % 4]
                DMA("sp", bt[:], srco[:, k, :], [], [rb], "w")
                CP(ceng[i % 4], Wb[:, k, 1536:2560], bt[:], [rb], WOK(k))
                i += 1

        def LX(src, src_r, t, m, l=None, rope_on=False, bias_on=False):
            s = t % 2
            rd = [src_r[t]] if src_r is not None else []
            DMA("sp", xt[s][:], src[t * 128:(t + 1) * 128, :], rd, [r_xt[s]], "x")
            if rope_on:
                DMA("sp", ropet[s][:], rope[t, :, :], [], [r_rope[s]], "x")
            if bias_on:
                DMA("sp", biasb[:].rearrange("p a b h i -> p (a b h i)"), nabias[l, t, :, :], [], [r_bias], "x")
            for k in range(8):
                TR(PB[k // 4][:, (k % 4) * 128:(k % 4 + 1) * 128], xt[s][:, k * 128:(k + 1) * 128], identf,
                   [r_xt[s], r_cs], [PR[k // 4]])
            for k in range(8):
                ACT(hT[s][:, k, :], PB[k // 4][:, (k % 4) * 128:(k % 4 + 1) * 128], AF.Identity,
                    [PR[k // 4], r_mods[cur["l"]]], [r_hT[s]], bias=modTs[cur["l"]][:, k, m:m + 1],
                    scale=modTs[cur["l"]][:, 8 + k, m:m + 1])

        def do_rope(src_ps, src_r, s_rope, dst, dst_r):
            v = src_ps.rearrange("p (h d) -> p h d", d=64)
            cosb = ropet[s_rope][:, 0:64].unsqueeze(1).broadcast_to([128, 6, 64])
            TT("dve", tA[:].rearrange("p (h d) -> p h d", d=64), v, cosb, ALU.mult, [src_r, r_rope[s_rope]], [r_tA])
            v5 = src_ps.rearrange("p (h r a f) -> p h r a f", r=2, a=2, f=16)
            tB5 = tB[:].rearrange("p (h r a f) -> p h r a f", r=2, a=2, f=16)
            sn = ropet[s_rope][:, 64:128].rearrange("p (r a f) -> p r a f", r=2, a=2)
            for a in range(2):
                TT("dve", tB5[:, :, :, a, :], v5[:, :, :, 1 - a, :],
                   sn[:, :, a, :].unsqueeze(1).broadcast_to([128, 6, 2, 16]), ALU.mult,
                   [src_r, r_rope[s_rope]], [r_tB])
            TT("dve", dst, tA[:], tB[:], ALU.add, [r_tA, r_tB], [dst_r])

        def P1(q, t, src, src_r, m, use_rope, hoist=None):
            nt = q["nt"]
            slot = t + q["halo"]
            s = t % 2
            hs, rhs_ = hT[s], r_hT[s]
            for (bank, wo, g) in ((2, 0, 0), (3, 384, 1), (4, 768, 2)):
                for k in range(8):
                    MM(PB[bank][:, 0:384], hs[:, k, :], Wb[:, k, wo:wo + 384], k == 0, k == 7, [rhs_, WRK[g][k // 2]],
                       [PR[bank]])
            for c in range(3):
                for k in range(8):
                    MM(PB[5][:, c * 128:(c + 1) * 128], Wb[:, k, 1152 + c * 128:1152 + (c + 1) * 128], hs[:, k, :],
                       k == 0, k == 7, [rhs_, WRK[3][k // 2]], [PR[5]])
            def conv_half(bank, groups):
                for gi, (wo, g) in enumerate(groups):
                    for c in range(2):
                        blk = gi * 2 + c
                        for k in range(8):
                            MM(PB[bank][:, blk * 128:(blk + 1) * 128], Wb[:, k, wo + c * 128:wo + (c + 1) * 128],
                               hs[:, k, :], k == 0, k == 7, [rhs_, WRK[g][k // 2]], [PR[bank]])

            conv_half(6, ((1536, 4), (1792, 5)))
            v3 = PB[3][:, 0:384].rearrange("p (h e) -> p h e", e=64)
            S.op("dve", lambda h_: h_.reduce_sum(out=st6[:, 7, :], in_=v3, axis=AX.X), [PR[3]], [r_st6c])
            TS("dve", st6[:, 7, :], st6[:, 7, :], 1.0 / 64.0, None, ALU.mult, None, [r_st6c], [r_st6c])
            TT("dve", q["vret"][:, t, :].rearrange("p (h e) -> p h e", e=64), v3,
               st6[:, 7, :].unsqueeze(2).broadcast_to([128, 6, 64]), ALU.subtract, [PR[3], r_st6c], [q["r_vret"][t]])
            CP("act", q["nva"][:, slot, :, 0:64], PB[4][:, 0:384].rearrange("p (h e) -> p h e", e=64), [PR[4]],
               [q["r_nv"][slot]])
            conv_half(4, ((2048, 6), (2304, 7)))
            if hoist is not None:
                hoist()
            if use_rope:
                do_rope(PB[2][:, 0:384], PR[2], s, krot[:], r_krot)
            else:
                CP("dve", krot[:], PB[2][:, 0:384], [PR[2]], [r_krot])
            for d in range(2):
                TT("pool", kw[:, d, :].rearrange("p (h e) -> p h e", e=64), krot[:].rearrange("p (h e) -> p h e", e=64),
                   WK[:, d, :].unsqueeze(2).broadcast_to([128, 6, 64]), ALU.mult, [r_krot, r_wk], [r_kw])
            for c in range(3):
                TR(PB[7][:, c * 128:(c + 1) * 128], krot[:, c * 128:(c + 1) * 128], identb[:], [r_krot, r_idb], [PR[7]])
            CP("dve", q["kT"][:, :, t * 128:(t + 1) * 128], PB[7][:, 0:384].rearrange("p (c i) -> p c i", i=128),
               [PR[7]], [q["r_kT"][t]])
            CP("dve", q["nkT"][:, :, slot * 128:(slot + 1) * 128], PB[5][:, 0:384].rearrange("p (c i) -> p c i", i=128),
               [PR[5]], [q["r_nk"][slot]])
            CP("act", cvt[:], PB[6][:, 256:512].rearrange("p (c i) -> p c i", i=128), [PR[6]], [r_cvt])
            TT("dve", q["uT"][:, :, 1 + t * 128:1 + (t + 1) * 128], PB[6][:, 0:256].rearrange("p (c i) -> p c i", i=128),
               cvt[:], ALU.mult, [PR[6], r_cvt], [q["r_u"][t + 1]])
            ACT(cvu[:], PB[4][:, 256:512].rearrange("p (c i) -> p c i", i=128), AF.Silu, [PR[4]], [r_cvu])
            TT("dve", q["gcT"][:, :, t * 128:(t + 1) * 128], PB[4][:, 0:256].rearrange("p (c i) -> p c i", i=128),
               cvu[:], ALU.mult, [PR[4], r_cvu], [q["r_gcT"][t]])
            for d in range(2):
                for pr in range(3):
                    MM(PB[2 + d][:, pr * 128:(pr + 1) * 128], kw[:, d, pr * 128:(pr + 1) * 128],
                       q["vret"][:, t, pr * 128:(pr + 1) * 128], True, True, [r_kw, q["r_vret"][t]], [PR[2 + d]])

            def diag(bank, hp):
                rows = slice(hp * 64, hp * 64 + 64)
                return PB[bank][rows, 0:384].rearrange("p (a b) -> p a b", b=128)[:, :, hp * 64:(hp + 1) * 64]

            CP("pool", q["sfst"][:, t, :, :], q["runf"][:], [q["r_runf"]], [q["r_sfst"][t]])
            TT("pool", tmp64, q["runf"][:], G128[:, 0, :].unsqueeze(2).broadcast_to([128, 3, 64]), ALU.mult,
               [q["r_runf"], r_g128], [r_tmp64])
            for hp in range(2):
                rows = slice(hp * 64, hp * 64 + 64)
                TT("dve", q["runf"][rows, :, :], diag(2, hp), tmp64[rows, :, :], ALU.add, [PR[2], r_tmp64], [q["r_runf"]])
                CP("act", q["tbst"][rows, t, :, :], diag(3, hp), [PR[3]], [q["r_tbst"][t]])
            if use_rope:
                for hp in range(2):
                    rows = slice(hp * 64, hp * 64 + 64)
                    TT("dve", tmp2[rows, :, :], diag(3, hp), gpow[rows, 2, :, t:t + 1].broadcast_to([64, 3, 64]), ALU.mult,
                       [PR[3], r_gpow], [r_tmp2])
                TT("pool", totb[:], totb[:], tmp2, ALU.add, [r_totb, r_tmp2], [r_totb])

        def P1_finish(q):
            nt = q["nt"]
            for t in range(nt - 1, -1, -1):
                CP("dve", tmp64, q["tbst"][:, t, :, :], [q["r_tbst"][t]], [r_tmp64])
                CP("dve", q["tbst"][:, t, :, :], q["runb"][:], [q["r_runb"]], [q["r_tbst"][t]])
                for pr in range(3):
                    STT(q["runb"][:, pr, :], q["runb"][:, pr, :], G128[:, 1, pr:pr + 1], tmp64[:, pr, :], ALU.mult,
                        ALU.add, [q["r_runb"], r_g128, r_tmp64], [q["r_runb"]])

        def P2(q, t, dst, dst_r, l, is_main, hoist=None, hooks=None, late_ret=False):
            nt = q["nt"]
            s = t % 2
            hs, rhs_ = hT[s], r_hT[s]
            def tok_proj(bank, wo, g):
                for k in range(8):
                    MM(PB[bank][:, 0:384], hs[:, k, :], Wb[:, k, wo:wo + 384], k == 0, k == 7, [rhs_, WRK[g][k // 2]],
                       [PR[bank]])

            tok_proj(2, 0, 0)
            for c in range(3):
                for k in range(8):
                    MM(PB[5][:, c * 128:(c + 1) * 128], Wb[:, k, 1152 + c * 128:1152 + (c + 1) * 128], hs[:, k, :],
                       k == 0, k == 7, [rhs_, WRK[3][k // 2]], [PR[5]])
            tok_proj(3, 384, 1)
            tok_proj(4, 768, 2)
            if hooks is not None and "a2" in hooks:
                hooks["a2"]()
            for hp in range(2):
                rows = slice(hp * 64, hp * 64 + 64)
                ACT(nqTm[rows, :, hp, :], PB[5][rows, 0:384].rearrange("p (c i) -> p c i", i=128), AF.Identity, [PR[5]],
                    [r_nqT], scale=0.125)
            if is_main:
                do_rope(PB[2][:, 0:384], PR[2], s, krot[:], r_krot)
            else:
                CP("dve", krot[:], PB[2][:, 0:384], [PR[2]], [r_krot])
            ACT(srg[:], PB[3][:, 0:384], AF.Silu, [PR[3]], [r_srg])
            ACT(sng[:], PB[4][:, 0:384], AF.Silu, [PR[4]], [r_sng])

            if is_main:
                dlo = -3 if t == NT - 1 else -2
                dhi = 3 if t == 0 else 2
                blocks = [(q, t + dt + 2, dt - dlo) for dt in range(dlo, dhi + 1)] + [(CQ, c, None) for c in range(NCX)]
            else:
                blocks = [(CQ, c, None) for c in range(NCX)]
            nb = len(blocks)

            def unit_blocks(half):
                return list(enumerate(blocks))[0:4] if half == 0 else list(enumerate(blocks))[4:nb]

            def N_S(pr, half):
                banks = (5, 6) if half == 0 else (0, 1)
                for j, (bi, (kq, slot, bidx)) in enumerate(unit_blocks(half)):
                    bank = banks[j // 2]
                    col = (j % 2) * 256
                    last = bidx is None
                    MM(PB[bank][:, col:col + 256], kq["nkT"][:, pr, slot * 128:(slot + 1) * 128], nqTm[:, pr, :, :],
                       True, last, [kq["r_nk"][slot], r_nqT], [PR[bank]])
                    if not last:
                        MM(PB[bank][:, col:col + 256], identb[:], biasb[:, pr, bidx, :, :], False, True, [r_idb, r_bias],
                           [PR[bank]])

            def N_E(pr, half):
                banks = (5, 6) if half == 0 else (0, 1)
                ub_ = unit_blocks(half)
                e4 = Eb[half][:].rearrange("p (b h) i -> p b h i", h=2)
                for jb in range(2):
                    nblk = min(2, len(ub_) - jb * 2)
                    if nblk <= 0:
                        continue
                    ACT(e4[:, jb * 2:jb * 2 + nblk, :, :],
                        PB[banks[jb]][:, 0:nblk * 256].rearrange("p (b h i) -> p b h i", h=2, i=128), AF.Exp,
                        [PR[banks[jb]]], [r_Eb[half]])

            def N_PV(pr):
                for hp in range(2):
                    h = 2 * pr + hp
                    for bi, (kq, slot, bidx) in enumerate(blocks):
                        half, j = (0, bi) if bi < 4 else (1, bi - 4)
                        e4 = Eb[half][:].rearrange("p (b h) i -> p b h i", h=2)
                        MM(PB[3][:, h * 65:(h + 1) * 65], e4[:, j, hp, :], kq["nva"][:, slot, h, :], bi == 0, bi == nb - 1,
                           [r_Eb[half], kq["r_nv"][slot]], [PR[3]])

            def N_F():
                ona = PB[3][:, 0:390].rearrange("p (h e) -> p h e", e=65)
                S.op("dve", lambda h_: h_.reciprocal(out=st6[:, 6, :], in_=ona[:, :, 64]), [PR[3]], [r_st6b])
                tA3 = tA[:].rearrange("p (h e) -> p h e", e=64)
                TT("dve", tA3, ona[:, :, 0:64], st6[:, 6, :].unsqueeze(2).broadcast_to([128, 6, 64]), ALU.mult,
                   [PR[3], r_st6b], [r_tA])
                TT("pool", yna[:], tA[:], sng[:], ALU.mult, [r_tA, r_sng], [r_yna])

            def R1():
                for c in range(3):
                    TR(PB[7][:, c * 128:(c + 1) * 128], krot[:, c * 128:(c + 1) * 128], identb[:], [r_krot, r_idb],
                       [PR[7]])
                CP("dve", qT[:], PB[7][:, 0:384].rearrange("p (c i) -> p c i", i=128), [PR[7]], [r_qT])
                for d in range(2):
                    TT("pool", qsc[:, d, :, :], qT[:], WQ[:, d, :, :], ALU.mult, [r_qT, r_wq], [r_qsc])
                if is_main:
                    for d in range(2):
                        TT("pool", qsc[:, 2 + d, :, :], qsc[:, d, :, :], gpow[:, d, :, t:t + 1].broadcast_to([128, 3, 128]),
                           ALU.mult, [r_qsc, r_gpow], [r_qsc])

            def R2():
                for h in range(6):
                    pr, hp = h // 2, h % 2
                    rows = slice(hp * 64, hp * 64 + 64)
                    MM(PB[hp][:, pr * 128:(pr + 1) * 128], q["kT"][rows, pr, t * 128:(t + 1) * 128], qT[rows, pr, :],
                       True, True, [q["r_kT"][t], r_qT], [PR[hp]], tp=(hp * 64, 0))
                for hp in range(2):
                    TT("dve", AT[:, hp * 3:hp * 3 + 3, :], PB[hp][:, 0:384].rearrange("p (h i) -> p h i", i=128),
                       Dcomb[:, hp * 3:hp * 3 + 3, :], ALU.mult, [PR[hp], r_dc], [r_AT])

            def R3():
                for h in range(6):
                    pr, hp = h // 2, h % 2
                    rows = slice(hp * 64, hp * 64 + 64)
                    o = PB[2][:, h * 64:(h + 1) * 64]
                    tp = (hp * 64, 0)
                    MM(o, AT[:, hp * 3 + pr, :], q["vret"][:, t, h * 64:(h + 1) * 64], True, False,
                       [r_AT, q["r_vret"][t]], [PR[2]])
                    MM(o, qsc[rows, 0, pr, :], q["sfst"][rows, t, pr, :], False, False, [r_qsc, q["r_sfst"][t]], [PR[2]],
                       tp=tp)
                    MM(o, qsc[rows, 1, pr, :], q["tbst"][rows, t, pr, :], False, not is_main, [r_qsc, q["r_tbst"][t]],
                       [PR[2]], tp=tp)
                    if is_main:
                        MM(o, qsc[rows, 2, pr, :], Sin[rows, 0, pr, :], False, False, [r_qsc, r_sin], [PR[2]], tp=tp)
                        MM(o, qsc[rows, 3, pr, :], Sin[rows, 1, pr, :], False, True, [r_qsc, r_sin], [PR[2]], tp=tp)

            def R4():
                o3 = PB[2][:, 0:384].rearrange("p (h e) -> p h e", e=64)
                ACT(tB[:], PB[2][:, 0:384], AF.Square, [PR[2]], [r_tB])
                S.op("dve", lambda h_: h_.reduce_sum(out=st6[:, 1, :], in_=tB[:].rearrange("p (h e) -> p h e", e=64),
                                                     axis=AX.X), [r_tB], [r_st6])
                ACT(st6[:, 4, :], st6[:, 1, :], AF.Sqrt, [r_st6, r_eps], [r_st6], bias=epsc[:], scale=1.0 / 64.0)
                S.op("dve", lambda h_: h_.reciprocal(out=st6[:, 5, :], in_=st6[:, 4, :]), [r_st6], [r_st6])
                tB3 = tB[:].rearrange("p (h e) -> p h e", e=64)
                TT("dve", tB3, o3, st6[:, 5, :].unsqueeze(2).broadcast_to([128, 6, 64]), ALU.mult, [PR[2], r_st6], [r_tB])
                TT("pool", yret[:], tB[:], srg[:], ALU.mult, [r_tB, r_srg], [r_yret])

            def C():
                uT_, gc_ = q["uT"], q["gcT"]
                ru = [q["r_u"][t], q["r_u"][t + 1], q["r_u"][t + 2]]
                b0 = t * 128

                def wb(j):
                    return cp_t[:, l, :, j:j + 1].broadcast_to([128, 2, 128])

                TT("pool", cvt[:], uT_[:, :, b0:b0 + 128], wb(0), ALU.mult, ru + [r_misc], [r_cvt])
                TT("pool", cvu[:], uT_[:, :, b0 + 1:b0 + 129], wb(1), ALU.mult, ru + [r_misc], [r_cvu])
                TT("pool", cvt[:], cvt[:], cvu[:], ALU.add, [r_cvt, r_cvu], [r_cvt])
                TT("pool", cvu[:], uT_[:, :, b0 + 2:b0 + 130], wb(2), ALU.mult, ru + [r_misc], [r_cvu])
                TT("pool", cvt[:], cvt[:], cvu[:], ALU.add, [r_cvt, r_cvu], [r_cvt])
                TT("pool", cvt[:], cvt[:], wb(3), ALU.add, [r_cvt, r_misc], [r_cvt])
                TT("pool", yT[:, 0:2, :], cvt[:], gc_[:, :, t * 128:(t + 1) * 128], ALU.mult, [r_cvt, q["r_gcT"][t]], [r_yTc])

            def Y_ret():
                for c in range(3):
                    TR(PB[7][:, c * 128:(c + 1) * 128], yret[:, c * 128:(c + 1) * 128], identb[:], [r_yret, r_idb],
                       [PR[7]])
                CP("act", yT[:, 2:5, :], PB[7][:, 0:384].rearrange("p (c i) -> p c i", i=128), [PR[7]], [r_yTr])

            def Y_na():
                for c in range(3):
                    TR(PB[7][:, (3 + c) * 128:(4 + c) * 128], yna[:, c * 128:(c + 1) * 128], identb[:], [r_yna, r_idb],
                       [PR[7]])
                CP("act", yT[:, 5:8, :], PB[7][:, 384:768].rearrange("p (c i) -> p c i", i=128), [PR[7]], [r_yTn])

            def O(ks, first, last):
                for j in range(2):
                    for k in ks:
                        rk_ = r_yTc if k < 2 else (r_yTr if k < 5 else r_yTn)
                        MM(PB[5 + j][:, :], yT[:, k, :], Wb[:, k, 1536 + j * 512:1536 + (j + 1) * 512],
                           first and k == ks[0], last and k == ks[-1], [rk_] + WOK(k), [PR[5 + j]])

            def L():
                for j in range(2):
                    TT("dve", zb[:, j * 512:(j + 1) * 512], PB[5 + j][:, :], gate_bc[:, j * 512:(j + 1) * 512], ALU.mult,
                       [PR[5 + j], r_gate], [r_zb])
                STT(zb[:], xt[s][:], ALPHA, zb[:], ALU.mult, ALU.add, [r_xt[s], r_zb], [r_zb])
                S.op("dve", lambda h_: h_.reduce_sum(out=st1[:, 0:1], in_=zb[:], axis=AX.X), [r_zb], [r_st1a])
                MSET("pool", st1[:, 1:3], 0.0, [r_st1])
                ACT(PB[4][:, :], zb[:, 0:512], AF.Square, [r_zb], [PR[4], r_st1], accum=st1[:, 1:2])
                ACT(PB[4][:, :], zb[:, 512:1024], AF.Square, [r_zb], [PR[4], r_st1], accum=st1[:, 2:3])
                TS("pool", st1[:, 0:1], st1[:, 0:1], 1.0 / D, None, ALU.mult, None, [r_st1a], [r_st1a])
                TT("pool", st1[:, 1:2], st1[:, 1:2], st1[:, 2:3], ALU.add, [r_st1], [r_st1])
                TT("pool", st1[:, 2:3], st1[:, 0:1], st1[:, 0:1], ALU.mult, [r_st1a, r_st1], [r_st1])
                TS("pool", st1[:, 1:2], st1[:, 1:2], 1.0 / D, None, ALU.mult, None, [r_st1], [r_st1])
                TT("pool", st1[:, 3:4], st1[:, 1:2], st1[:, 2:3], ALU.subtract, [r_st1], [r_st1])
                ACT(st1[:, 4:5], st1[:, 3:4], AF.Sqrt, [r_st1, r_eps], [r_st1], bias=epsc[:], scale=1.0)
                S.op("dve", lambda h_: h_.reciprocal(out=st1[:, 5:6], in_=st1[:, 4:5]), [r_st1], [r_st1])
                STT(st1[:, 6:7], st1[:, 0:1], -1.0, st1[:, 5:6], ALU.mult, ALU.mult, [r_st1, r_st1a], [r_st1])
                ACT(zb[:], zb[:], AF.Identity, [r_zb, r_st1], [r_zb], bias=st1[:, 6:7], scale=st1[:, 5:6])
                TT("pool", zb[:], zb[:], gbc[:], ALU.mult, [r_zb, r_lnp], [r_zb])
                TT("pool", zb[:], zb[:], bbc[:], ALU.add, [r_zb, r_lnp], [r_zb])
                DMA("sp", dst[t * 128:(t + 1) * 128, :], zb[:], [r_zb], [dst_r[t]] if dst_r is not None else [], "st")

            has_b = nb > 4
            N_S(0, 0)
            R1()
            N_E(0, 0)
            R2()
            if has_b:
                N_S(0, 1)
                N_E(0, 1)
            N_PV(0)
            N_S(1, 0)
            C()
            if not late_ret:
                R3()
            N_E(1, 0)
            if has_b:
                N_S(1, 1)
                N_E(1, 1)
            N_PV(1)
            if not late_ret:
                R4()
            N_S(2, 0)
            N_E(2, 0)
            if has_b:
                N_S(2, 1)
            if not late_ret:
                Y_ret()
            if has_b:
                N_E(2, 1)
            if not late_ret:
                O([0, 1, 2, 3, 4], True, False)
            if hoist is not None:
                hoist()
            N_PV(2)
            if hooks is not None and "post_nbr" in hooks:
                hooks["post_nbr"]()
            if late_ret:
                if hooks is not None and "pre_ret" in hooks:
                    hooks["pre_ret"]()
                R3()
                R4()
                Y_ret()
                O([0, 1, 2, 3, 4], True, False)
            N_F()
            Y_na()
            O([5, 6, 7], False, True)
            if hooks is not None and "o" in hooks:
                hooks["o"]()
            L()

        def exchange_gathers(q, gb, r_gb):
            nk, nv, uT_ = q["nkT"], q["nva"], q["uT"]

            def gather(o, col0, n, which, wr, shape3=None):
                cands = (0, 1, 2) if which == 0 else (1, 2, 3)
                for j, k in enumerate(cands):
                    e = Eb[j % 2]
                    re = r_Eb[j % 2]
                    stg = e[:].rearrange("p a b -> p (a b)")[:, 0:n]
                    DMA("sp", stg, gb[k * 128:(k + 1) * 128, col0:col0 + n], [r_gb], [re], "ex")
                    sv = stg if shape3 is None else stg.rearrange("p (a b) -> p a b", b=shape3)
                    sc = sel_t[:, which * 4 + k:which * 4 + k + 1]
                    if j == 0:
                        TS("dve", o, sv, sc, None, ALU.mult, None, [re, r_misc], wr)
                    else:
                        STT(o, sv, sc, o, ALU.mult, ALU.add, [re, r_misc] + wr, wr)

            items = [
                lambda: gather(nk[:, :, 0:256], OFF_NKT_BOT, 768, 0, [q["r_nk"][0], q["r_nk"][1]], 256),
                lambda: gather(nk[:, :, (NT + 2) * 128:(NT + 4) * 128], OFF_NKT_TOP, 768, 1,
                               [q["r_nk"][NT + 2], q["r_nk"][NT + 3]], 256),
                lambda: gather(nv[:, 0:2, :, :].rearrange("p a h e -> p (a h e)"), OFF_NVA_BOT, 780, 0,
                               [q["r_nv"][0], q["r_nv"][1]]),
                lambda: gather(nv[:, NT + 2:NT + 4, :, :].rearrange("p a h e -> p (a h e)"), OFF_NVA_TOP, 780, 1,
                               [q["r_nv"][NT + 2], q["r_nv"][NT + 3]]),
                lambda: gather(uT_[:, :, 0], OFF_U_LAST, 2, 0, [q["r_u"][0]]),
                lambda: gather(uT_[:, :, NT * 128 + 1], OFF_U_FIRST, 2, 1, [q["r_u"][NT + 1]]),
            ]
            return items

        def exchange(l):
            q = MQ
            pb, gb = packB[l], gathB[l]
            r_pb, r_gb = R("pb"), R("gb")
            pf = pb[:, OFF_ST:OFF_ST + 768].bitcast(F32)
            DMA("sp", pf[:, 0:192].rearrange("p (a b) -> p a b", b=64), q["runf"][:], [q["r_runf"]], [r_pb], "ex")
            DMA("sp", pf[:, 192:384].rearrange("p (a b) -> p a b", b=64), totb[:], [r_totb], [r_pb], "ex")
            nk, nv, uT_ = q["nkT"], q["nva"], q["uT"]
            DMA("sp", pb[:, OFF_NKT_TOP:OFF_NKT_TOP + 768].rearrange("p (a b) -> p a b", b=256), nk[:, :, 2 * 128:4 * 128],
                [q["r_nk"][2], q["r_nk"][3]], [r_pb], "ex")
            DMA("sp", pb[:, OFF_NKT_BOT:OFF_NKT_BOT + 768].rearrange("p (a b) -> p a b", b=256),
                nk[:, :, (NT) * 128:(NT + 2) * 128], [q["r_nk"][NT], q["r_nk"][NT + 1]], [r_pb], "ex")
            DMA("sp", pb[:, OFF_NVA_TOP:OFF_NVA_TOP + 780], nv[:, 2:4, :, :].rearrange("p a h e -> p (a h e)"),
                [q["r_nv"][2], q["r_nv"][3]], [r_pb], "ex")
            DMA("sp", pb[:, OFF_NVA_BOT:OFF_NVA_BOT + 780], nv[:, NT:NT + 2, :, :].rearrange("p a h e -> p (a h e)"),
                [q["r_nv"][NT], q["r_nv"][NT + 1]], [r_pb], "ex")
            MSET("dve", ub[:, 4:8], 0.0, [r_ub])
            CP("dve", ub[:, 0:2], uT_[:, :, 1], [q["r_u"][1]], [r_ub])
            CP("dve", ub[:, 2:4], uT_[:, :, NT * 128], [q["r_u"][NT]], [r_ub])
            DMA("sp", pb[:, OFF_U_FIRST:OFF_U_FIRST + 8], ub[:], [r_ub], [r_pb], "ex")
            S.op("pool", lambda h: h.collective_compute("AllGather", ALU.bypass, replica_groups=RG, ins=[pb[:, :]],
                                                        outs=[gb[:, :]]), [r_pb], [r_gb], dma="cc", inc=1)
            return dict(gb=gb, r_gb=r_gb)

        def exchange_recv_halo(l, X):
            return exchange_gathers(MQ, X["gb"], X["r_gb"])

        def exchange_recv(l, X):
            gb, r_gb = X["gb"], X["r_gb"]
            stgs = [(tA[:], r_tA),
                    (Eb[0][:].rearrange("p a b -> p (a b)")[:, 0:768].bitcast(F32), r_Eb[0]),
                    (Eb[1][:].rearrange("p a b -> p (a b)")[:, 0:768].bitcast(F32), r_Eb[1])]
            for k in range(4):
                stg, rs = stgs[k % 3]
                DMA("sp", stg, gb[k * 128:(k + 1) * 128, OFF_ST:OFF_ST + 768].bitcast(F32), [r_gb], [rs], "ex")
                v4 = stg.rearrange("p (d a b) -> p d a b", d=2, a=3)
                cb_ = coef[:, :, :, k:k + 1].broadcast_to([128, 2, 3, 64])
                if k == 0:
                    TT("dve", sacc, v4, cb_, ALU.mult, [rs, r_coef], [r_sacc])
                else:
                    TT("dve", v4, v4, cb_, ALU.mult, [rs, r_coef], [rs])
                    TT("dve", sacc, sacc, v4, ALU.add, [rs, r_sacc], [r_sacc])
            t4 = tA[:].rearrange("p (d a b) -> p d a b", d=2, a=3)
            for d in range(2):
                s0 = CQ["runf"] if d == 0 else CQ["runb"]
                r_s0 = CQ["r_runf"] if d == 0 else CQ["r_runb"]
                TT("dve", t4[:, d, :, :], s0[:], coef[:, d, :, 4:5].broadcast_to([128, 3, 64]), ALU.mult,
                   [r_s0, r_coef, r_tA], [r_tA])
            TT("dve", sacc, sacc, t4, ALU.add, [r_tA, r_sacc], [r_sacc])
            CP("dve", Sin[:], sacc, [r_sacc], [r_sin])

        def reset_run(q):
            if q is MQ:
                MSET("pool", totb[:], 0.0, [r_totb])
            MSET("pool", q["runf"][:], 0.0, [q["r_runf"]])
            MSET("pool", q["runb"][:], 0.0, [q["r_runb"]])

        try:
            for l in range(2):
                if l == 0:
                    load_w1(0, 0)
                    load_w1(0, 1)
                ck(10 * l + 0)
                if l == 0:
                    mod_setup(0)
                layer_setup(l)
                ck(10 * l + 1)
                csrc, csrc_r = (ctx_in, None) if l == 0 else (xc1, xc1_r)
                msrc, msrc_r = (x_in, None) if l == 0 else (x1, x1_r)
                mdst, mdst_r = (x1, x1_r) if l == 0 else (out, None)
                reset_run(CQ)
                reset_run(MQ)
                LX(msrc, msrc_r, 0, 0, l, rope_on=True)
                for t in range(NT):
                    hz = (lambda t=t: LX(msrc, msrc_r, t + 1, 0, l, rope_on=True)) if t + 1 < NT else \
                        (lambda: LX(csrc, csrc_r, 0, 1))
                    P1(MQ, t, msrc, msrc_r, 0, True, hoist=hz)
                    if l == 0 and t < 12:
                        mod1_piece(2 * t)
                        mod1_piece(2 * t + 1)
                    if l == 0 and t == 12:
                        mod_finish(1, acc1)
                ck(10 * l + 2)
                X = exchange(l)
                for t in range(NCX):
                    hz = (lambda t=t: LX(csrc, csrc_r, t + 1, 1)) if t + 1 < NCX else None
                    P1(CQ, t, csrc, csrc_r, 1, False, hoist=hz)
                ck(10 * l + 3)
                P1_finish(CQ)
                if l == 1:
                    P1_finish(MQ)
                load_w2(l)
                ck(10 * l + 4)
                ck(10 * l + 5)
                if l == 0:
                    build_gate(1)
                    ck(5.1)
                    LX(csrc, csrc_r, 0, 1)
                    for t in range(NCX):
                        hz = (lambda t=t: LX(csrc, csrc_r, t + 1, 1)) if t + 1 < NCX else None
                        P2(CQ, t, xc1, xc1_r, l, False, hoist=hz)
                if l == 0:
                    P1_finish(MQ)
                ck(10 * l + 6)
                build_gate(0)
                order = [2, 3, 4, 5, 6, 7, 8, 9, 10, 11, 12, 13, 0, 1, 14, 15]
                LX(msrc, msrc_r, order[0], 0, l, rope_on=True, bias_on=True)
                halo_items = exchange_recv_halo(l, X)
                for i, t in enumerate(order):
                    hz = (lambda tn=order[i + 1]: LX(msrc, msrc_r, tn, 0, l, rope_on=True, bias_on=True)) \
                        if i + 1 < NT else None
                    hk = None
                    if i == 0:
                        hk = {"pre_ret": (lambda: exchange_recv(l, X))}
                    if 1 <= i <= len(halo_items):
                        hk = {"post_nbr": halo_items[i - 1]}
                    if l == 0 and i == NT - 1:
                        hk = {"a2": (lambda: load_w1(1, 0)), "o": (lambda: load_w1(1, 1))}
                    P2(MQ, t, mdst, mdst_r, l, True, hoist=hz, hooks=hk, late_ret=(i == 0))
                    ck(10 * l + 7)
                ck(10 * l + 8)
        except _Stop:
            pass
        S.finish()
        S.emit_all()
        print("ops", S.nops, "sems", S.nsem)
    return nc


def _host_tables():
    P = np.arange(128)
    I = np.arange(128)
    cst = np.zeros((128, 5 * 128 + 2), np.float32)
    cst[:, 0:128] = np.eye(128)
    diff = I[None, :] - P[:, None]
    BIG = 1.0e6
    cst[:, 128:256] = np.where(diff >= 0, diff, BIG)
    cst[:, 256:384] = np.where(diff < 0, -diff, BIG)
    cst[:, 384:512] = (I + 1)[None, :]
    cst[:, 512:640] = (128 - I)[None, :]
    cst[:, 640] = 127 - P
    cst[:, 641] = P
    return cst


def _rope_tables(rank):
    nf = 16
    inv = (10000.0 ** (-np.arange(nf, dtype=np.float64) / nf))
    out = np.zeros((NT, 128, 128), np.float32)
    for t in range(NT):
        p = np.arange(128)
        row = (32 * rank + 2 * t + p // 64).astype(np.float64)
        col = (p % 64).astype(np.float64)
        ar = (row[:, None].astype(np.float32) * inv.astype(np.float32)[None, :]).astype(np.float64)
        ac = (col[:, None].astype(np.float32) * inv.astype(np.float32)[None, :]).astype(np.float64)
        cr, sr, cc, sc = np.cos(ar), np.sin(ar), np.cos(ac), np.sin(ac)
        out[t, :, 0:64] = np.concatenate([cr, cr, cc, cc], 1)
        out[t, :, 64:128] = np.concatenate([-sr, sr, -sc, sc], 1)
    return out


def _bias_tables(rpb, rank):
    out = np.full((NT, 128, 6, 6, 128), NEG, np.float32)
    j = np.arange(128)
    i = np.arange(128)
    rkl, ck = j // 64, j % 64
    rql, cq = i // 64, i % 64
    cstart = np.clip(cq - 8, 0, 48)
    colok = (ck[:, None] >= cstart[None, :]) & (ck[:, None] < cstart[None, :] + 16)
    dcol = ck[:, None] - cq[None, :] + 15
    for t in range(NT):
        dlo = -3 if t == NT - 1 else -2
        dhi = 3 if t == 0 else 2
        rq = 32 * rank + 2 * t + rql
        rstart = np.clip(rq - 4, 0, 120)
        for dt in range(dlo, dhi + 1):
            b = dt - dlo
            rk = 32 * rank + 2 * (t + dt) + rkl
            rowok = (rk[:, None] >= rstart[None, :]) & (rk[:, None] < rstart[None, :] + 8) & \
                    (rk[:, None] >= 0) & (rk[:, None] < 128)
            ok = rowok & colok
            drow = np.clip(rk[:, None] - rq[None, :] + 7, 0, 14)
            dc = np.clip(dcol, 0, 30)
            vals = rpb[:, drow, dc]
            blk = np.where(ok[None], vals, NEG)
            out[t, :, :, b, :] = blk.transpose(1, 0, 2)
    out = out.reshape(NT, 128, 3, 2, 6, 128).transpose(0, 1, 2, 4, 3, 5)
    return np.ascontiguousarray(out).reshape(NT, 128, 6 * 6 * 128).astype(ml_dtypes.bfloat16)


_NC_CACHE = {}


def kernel(x, c, ctx, c_ctx, w_mod, b_mod, w_in, conv_w, conv_b, ret_decay, na_rpb, w_out, ln_g, ln_b):
    f32 = np.float32
    x = np.asarray(x, f32)
    c = np.asarray(c, f32)
    ctx = np.asarray(ctx, f32)
    c_ctx = np.asarray(c_ctx, f32)
    w_mod = np.ascontiguousarray(np.asarray(w_mod, f32))
    b_mod = np.asarray(b_mod, f32)
    w_in = np.ascontiguousarray(np.asarray(w_in, f32))
    w_out = np.ascontiguousarray(np.asarray(w_out, f32))
    conv_w = np.asarray(conv_w, f32)
    conv_b = np.asarray(conv_b, f32)
    ret_decay = np.asarray(ret_decay, f32)
    na_rpb = np.asarray(na_rpb, f32)
    ln_g = np.asarray(ln_g, f32)
    ln_b = np.asarray(ln_b, f32)

    if "nc" not in _NC_CACHE:
        _NC_CACHE["nc"] = build_program()
    nc = _NC_CACHE["nc"]

    cst = _host_tables()
    bmodT = np.ascontiguousarray(b_mod.reshape(2, 24, 128).transpose(2, 0, 1))
    convp = np.zeros((128, 2, 2, 4), f32)
    for l in range(2):
        for ch in range(2):
            convp[:, l, ch, 0:3] = conv_w[l, :, ch * 128:(ch + 1) * 128].T
            convp[:, l, ch, 3] = conv_b[l, ch * 128:(ch + 1) * 128]
    dec = np.zeros((128, 36), f32)
    dec[:, 0:24] = ret_decay.reshape(1, 24)
    for l in range(2):
        for d in range(2):
            for pr in range(3):
                dec[0:64, 24 + l * 6 + d * 3 + pr] = ret_decay[l, d, 2 * pr]
                dec[64:128, 24 + l * 6 + d * 3 + pr] = ret_decay[l, d, 2 * pr + 1]
    lnp = np.zeros((2, 2, 128, D), f32)
    lnp[:, 0] = ln_g[:, None, :]
    lnp[:, 1] = ln_b[:, None, :]

    in_maps = []
    for core in range(8):
        b, r = core // 4, core % 4
        cvec = np.zeros((128, 8, 2), f32)
        cvec[:, :, 0] = c[b].reshape(8, 128).T
        cvec[:, :, 1] = c_ctx.reshape(8, 128).T
        rankc = np.zeros((128, 20), f32)
        for k in range(4):
            if k < r:
                rankc[:, 0 + k] = 2048.0 * (r - k - 1)
                rankc[:, 10 + k] = 1.0
            if k > r:
                rankc[:, 5 + k] = 2048.0 * (k - r - 1)
                rankc[:, 15 + k] = 1.0
        rankc[:, 4] = 2048.0 * r
        rankc[:, 14] = 1.0
        rankc[:, 9] = 2048.0 * (3 - r)
        rankc[:, 19] = 1.0
        edge = np.zeros((128, 2), f32)
        edge[:, 0] = 1.0 if r > 0 else 0.0
        edge[:, 1] = 1.0 if r < 3 else 0.0
        sel = np.zeros((128, 8), f32)
        if r > 0:
            sel[:, r - 1] = 1.0
        if r < 3:
            sel[:, 4 + r + 1] = 1.0
        idx = np.zeros((128, 2), np.int32)
        idx[:, 0] = (r - 1 if r > 0 else r) * 128 + np.arange(128)
        idx[:, 1] = (r + 1 if r < 3 else r) * 128 + np.arange(128)
        nab = np.stack([_bias_tables(na_rpb[l], r) for l in range(2)], 0)
        in_maps.append({
            "x": np.ascontiguousarray(x[b, r * 2048:(r + 1) * 2048, :]),
            "ctx": np.ascontiguousarray(ctx[b]),
            "cvec": cvec, "w_mod": w_mod, "bmodT": bmodT, "w_in": w_in, "w_out": w_out, "convp": convp,
            "dec": dec, "nabias": nab, "lnp": lnp, "rope": _rope_tables(r), "cst": cst, "rankc": rankc,
            "edge": edge, "idx": idx, "sel": sel,
        })
    res = run_bass_kernel_spmd(nc, in_maps, core_ids=list(range(8)))
    out = np.zeros((2, 8192, D), f32)
    for core in range(8):
        b, r = core // 4, core % 4
        out[b, r * 2048:(r + 1) * 2048, :] = np.asarray(res.results[core]["out"], f32)
    if DEBUG:
        kernel.debug = res.results
    return out
```
